# Optimizing a Trainium2 kernel written in Bass

```python
import jax, jax.numpy as jnp
from jax import lax
import numpy as np

D_MODEL = 2048
BATCH = 1
SEQ = 8192
DEPTH = 2

MEM_LEN = 256
N_MIXERS = 2
D_FF = 5632
CHUNK = 128
GMLP_WIDTH = 2048
GMLP_GROUPS = 8
GMLP_GROUP_DIM = GMLP_WIDTH // GMLP_GROUPS
CONV_WIDTH = 3
XATTN_HEADS = 4
XATTN_HEAD_DIM = D_MODEL // XATTN_HEADS
RMS_EPS = 1e-6
LN_EPS = 1e-5

kernel_name = "hybrid_gmlp_shortconv_macaron_memxattn"


def rmsnorm(x, g):
    xf = x.astype(jnp.float32)
    y = xf * lax.rsqrt(jnp.mean(xf * xf, axis=-1, keepdims=True) + RMS_EPS)
    return (y * g.astype(jnp.float32)).astype(x.dtype)


def layernorm(x, g, b):
    xf = x.astype(jnp.float32)
    mu = jnp.mean(xf, axis=-1, keepdims=True)
    xc = xf - mu
    var = jnp.mean(xc * xc, axis=-1, keepdims=True)
    y = xc * lax.rsqrt(var + LN_EPS) * g.astype(jnp.float32) + b.astype(jnp.float32)
    return y.astype(x.dtype)


def swiglu(h, w13, w2):
    gate, up = jnp.split(h @ w13, 2, axis=-1)
    return (jax.nn.silu(gate) * up) @ w2


def gmlp_mixer(h, w_in, ln_g, ln_b, w_s, b_s, w_out):
    bsz, seq, _ = h.shape
    z = jax.nn.gelu(h @ w_in, approximate=False)
    u, v = jnp.split(z, 2, axis=-1)
    v = layernorm(v, ln_g, ln_b)
    vc = v.reshape(bsz, seq // CHUNK, CHUNK, GMLP_GROUPS, GMLP_GROUP_DIM)
    causal = jnp.tril(jnp.ones((CHUNK, CHUNK), dtype=bool))
    w = jnp.where(causal[None], w_s, jnp.zeros_like(w_s)).astype(vc.dtype)
    f = jnp.einsum('gts,bcsge->bctge', w, vc) + b_s.T[:, :, None].astype(vc.dtype)
    return (u * f.reshape(bsz, seq, GMLP_WIDTH)) @ w_out


def short_conv_mixer(h, w_in, conv_w, w_out):
    d = h.shape[-1]
    gate_b, gate_c, val = jnp.split(h @ w_in, 3, axis=-1)
    z = gate_c * val
    kern = conv_w[:, None, :].astype(z.dtype)
    conv = lax.conv_general_dilated(
        z, kern, window_strides=(1,), padding=[(CONV_WIDTH - 1, 0)],
        dimension_numbers=('NWC', 'WIO', 'NWC'), feature_group_count=d)
    return (gate_b * conv) @ w_out


def mem_cross_attn(h, mem_n, wq, wkv, wo):
    bsz, seq, d = h.shape
    m = mem_n.shape[1]
    q = (h @ wq).reshape(bsz, seq, XATTN_HEADS, XATTN_HEAD_DIM)
    k, v = jnp.split(mem_n @ wkv, 2, axis=-1)
    k = k.reshape(bsz, m, XATTN_HEADS, XATTN_HEAD_DIM)
    v = v.reshape(bsz, m, XATTN_HEADS, XATTN_HEAD_DIM)
    s = jnp.einsum('bshd,bmhd->bhsm', q, k).astype(jnp.float32) * (XATTN_HEAD_DIM ** -0.5)
    p = jax.nn.softmax(s, axis=-1).astype(v.dtype)
    o = jnp.einsum('bhsm,bmhd->bshd', p, v).reshape(bsz, seq, d)
    return o @ wo


def setup_inputs(seed: int = 0) -> dict:
    key = jax.random.key(seed)
    ks = iter(jax.random.split(key, 32))
    n_a = (DEPTH + 1) // 2
    n_b = DEPTH // 2
    D, F, E = D_MODEL, D_FF, GMLP_WIDTH

    def w(shape, fan_in):
        return jax.random.normal(next(ks), shape, jnp.float32) * (fan_in ** -0.5)

    def gain(shape):
        return 1.0 + 0.02 * jax.random.normal(next(ks), shape, jnp.float32)

    def bias(shape):
        return 0.02 * jax.random.normal(next(ks), shape, jnp.float32)

    return {
        "x": jax.random.normal(next(ks), (BATCH, SEQ, D), jnp.float32),
        "mem": jax.random.normal(next(ks), (BATCH, MEM_LEN, D), jnp.float32),
        "ffn1_norm": gain((DEPTH, D)),
        "ffn1_w13": w((DEPTH, D, 2 * F), D),
        "ffn1_w2": w((DEPTH, F, D), F),
        "mix_norm": gain((DEPTH, D)),
        "gmlp_w_in": w((n_a, D, 2 * E), D),
        "gmlp_ln_g": gain((n_a, E)),
        "gmlp_ln_b": bias((n_a, E)),
        "gmlp_w_s": w((n_a, GMLP_GROUPS, CHUNK, CHUNK), CHUNK),
        "gmlp_b_s": gain((n_a, GMLP_GROUPS, CHUNK)),
        "gmlp_w_out": w((n_a, E, D), E),
        "conv_w_in": w((n_b, D, 3 * D), D),
        "conv_w": w((n_b, CONV_WIDTH, D), CONV_WIDTH),
        "conv_w_out": w((n_b, D, D), D),
        "xattn_norm": gain((DEPTH, D)),
        "mem_norm": gain((DEPTH, D)),
        "xattn_wq": w((DEPTH, D, D), D),
        "xattn_wkv": w((DEPTH, D, 2 * D), D),
        "xattn_wo": w((DEPTH, D, D), D),
        "ffn2_norm": gain((DEPTH, D)),
        "ffn2_w13": w((DEPTH, D, 2 * F), D),
        "ffn2_w2": w((DEPTH, F, D), F),
        "final_norm": gain((D,)),
    }


def reference(x, mem, ffn1_norm, ffn1_w13, ffn1_w2, mix_norm,
              gmlp_w_in, gmlp_ln_g, gmlp_ln_b, gmlp_w_s, gmlp_b_s, gmlp_w_out,
              conv_w_in, conv_w, conv_w_out,
              xattn_norm, mem_norm, xattn_wq, xattn_wkv, xattn_wo,
              ffn2_norm, ffn2_w13, ffn2_w2, final_norm):
    for i in range(DEPTH):
        x = x + 0.5 * swiglu(rmsnorm(x, ffn1_norm[i]), ffn1_w13[i], ffn1_w2[i])
        h = rmsnorm(x, mix_norm[i])
        j = i // N_MIXERS
        if i % N_MIXERS == 0:
            x = x + gmlp_mixer(h, gmlp_w_in[j], gmlp_ln_g[j], gmlp_ln_b[j],
                               gmlp_w_s[j], gmlp_b_s[j], gmlp_w_out[j])
        else:
            x = x + short_conv_mixer(h, conv_w_in[j], conv_w[j], conv_w_out[j])
        x = x + mem_cross_attn(rmsnorm(x, xattn_norm[i]), rmsnorm(mem, mem_norm[i]),
                               xattn_wq[i], xattn_wkv[i], xattn_wo[i])
        x = x + 0.5 * swiglu(rmsnorm(x, ffn2_norm[i]), ffn2_w13[i], ffn2_w2[i])
    return rmsnorm(x, final_norm)
```

```python
import contextlib
import numpy as np
import concourse.bass as bass
import concourse.mybir as mybir
from concourse.bass_utils import run_bass_kernel_spmd

F32 = mybir.dt.float32
BF16 = mybir.dt.bfloat16
AF = mybir.ActivationFunctionType
ALU = mybir.AluOpType
AX = mybir.AxisListType

D = 2048
KC = 16
HCH = 44
MEM = 256
NCORES = 8
TOWN = 1024
HALO = 128
NT = TOWN + HALO
SLOT = 8192
NSLOT = 3
GROUPS = [(0, 12), (12, 12), (24, 12), (36, 8)]
BL3 = [(0, 384), (384, 384), (768, 384)]
BL2 = [(128, 512), (640, 512)]
CB = [(126, 2), (128, 512), (640, 512)]
NSTAGES_ALL = 9


class Tok:
    __slots__ = ("sem", "val")

    def __init__(self, sem, val):
        self.sem, self.val = sem, val


class Eng:
    def __init__(self, name, sem):
        self.name, self.sem, self.cnt, self.ops, self.seen = name, sem, 0, [], {}

    def wait(self, toks, skip_own=False):
        for t in toks:
            if t is None:
                continue
            if skip_own and t.sem is self.sem:
                continue
            k = id(t.sem)
            if self.seen.get(k, 0) >= t.val:
                continue
            self.seen[k] = t.val
            self.ops.append(("wait", t.sem, t.val))

    def emit(self, fn, signal=False):
        if signal:
            self.cnt += 1
            self.ops.append(("sig", fn))
            return Tok(self.sem, self.cnt)
        self.ops.append(("op", fn))
        return None

    def replay(self, e):
        for op in self.ops:
            if op[0] == "wait":
                e.wait_ge(op[1], op[2])
            elif op[0] == "op":
                op[1](e)
            elif op[0] == "sig":
                op[1](e).then_inc(self.sem, 1)
            elif op[0] == "dma":
                kw = op[4]
                e.dma_start(out=op[1], in_=op[2], **kw).then_inc(op[3], 16)


class Tracker:
    def __init__(self):
        self.w, self.r = {}, {}

    def deps_read(self, keys):
        return [self.w.get(k) for k in keys]

    def deps_write(self, keys):
        out = []
        for k in keys:
            out.append(self.w.get(k))
            out.extend(self.r.get(k, {}).values())
        return out

    def did_read(self, keys, tok):
        for k in keys:
            d = self.r.setdefault(k, {})
            cur = d.get(id(tok.sem))
            if cur is None or cur.val < tok.val:
                d[id(tok.sem)] = tok

    def did_write(self, keys, tok):
        for k in keys:
            self.w[k] = tok
            self.r[k] = {}


def build(nstages=NSTAGES_ALL, dbg=False, dbgopts=None):
    dbgopts = dbgopts or {}
    nc = bass.Bass("TRN2", target_bir_lowering=False)
    es = contextlib.ExitStack()

    def din(name, shape):
        return nc.dram_tensor(name, list(shape), F32, kind="ExternalInput").ap()

    xT_d = din("xT", [D, NT]).rearrange("(c p) t -> p c t", p=128)
    hmask_d = din("hmask", [128, 2])
    memT_d = din("memT", [D, MEM]).rearrange("(c p) t -> p c t", p=128)
    gains_d = din("gains", [128, 11 * 16])
    w13_d = {(l, f): din(f"w13_{l}_{f}", [22, 128, SLOT]) for l in range(2) for f in range(2)}
    w2_d = {(l, f): din(f"w2_{l}_{f}", [16, 128, 12 * 512]) for l in range(2) for f in range(2)}
    gv_d = din("gmlp_v", [4, 128, SLOT])
    gu_d = din("gmlp_u", [4, 128, SLOT])
    go_d = din("gmlp_o", [4, 128, SLOT])
    lng_d = din("ln_g", [1, D])
    lnb_d = din("ln_b", [1, D])
    wsT_d = din("wsT", [128, 8 * 128])
    trilT_d = din("trilT", [128, 128])
    bs_d = din("b_s", [1, 8 * 128])
    cin_d = din("conv_in", [16, 128, 16 * 384])
    cw_d = din("conv_w", [128, 16 * 3])
    co_d = din("conv_o", [4, 128, SLOT])
    wq_d = [din(f"wq_{l}", [4, 128, SLOT]) for l in range(2)]
    wk_d = [din(f"wk_{l}", [4, 128, SLOT]) for l in range(2)]
    wv_d = [din(f"wv_{l}", [4, 128, SLOT]) for l in range(2)]
    wo_d = [din(f"wo_{l}", [4, 128, SLOT]) for l in range(2)]
    if dbg:
        out_d = nc.dram_tensor("dbg", [D, NT], F32, kind="ExternalOutput").ap().rearrange("(c p) t -> p c t", p=128)
    else:
        out_d = nc.dram_tensor("outT", [D, TOWN], F32, kind="ExternalOutput").ap().rearrange("(c p) t -> p c t", p=128)

    def sb(name, shape, dt):
        return es.enter_context(nc.sbuf_tensor(name, list(shape), dt))

    xT = sb("xT_sb", [128, KC, NT], F32)
    P = sb("P_sb", [128, 36864], BF16)
    ring = sb("ring_sb", [128, NSLOT, SLOT], BF16)
    sq = sb("sq_sb", [128, 4, 512], BF16)
    std = sb("std_sb", [128, 2, 512], F32)
    actT = sb("act_sb", [128, 1028], F32)
    ones = sb("ones_sb", [128, 128], BF16)
    e0 = sb("e0_sb", [128, 128], BF16)
    gains = sb("gains_sb", [128, 11, 16], F32)
    eps6 = sb("eps6_sb", [128, 1], F32)
    eps5 = sb("eps5_sb", [128, 1], F32)
    cw = sb("cw_sb", [128, 16, 3], F32)
    hmask = sb("hmask_sb", [128, 2], F32)
    lnst = sb("lnst_sb", [128, 64], F32)

    ps = [es.enter_context(nc.psum_tensor(f"ps{i}", [128, 512], F32)) for i in range(8)]

    def pv(off, dt, shape):
        n = int(np.prod(shape))
        if dt is BF16:
            assert off % 2 == 0
            ap = P[:, off // 2: off // 2 + n]
        else:
            assert off % 4 == 0
            ap = P[:, off // 2: off // 2 + 2 * n].bitcast(F32)
        assert off + n * (2 if dt is BF16 else 4) <= 36864 * 2
        if len(shape) == 2:
            ap = ap.rearrange("p (a b) -> p a b", a=shape[0])
        elif len(shape) == 3:
            ap = ap.rearrange("p (a b c) -> p a b c", a=shape[0], b=shape[1])
        return ap

    def sem(name):
        return es.enter_context(nc.semaphore(name))

    PE = Eng("pe", sem("s_pe"))
    ACT = Eng("act", sem("s_act"))
    DVE = Eng("dve", sem("s_dve"))
    POOL = Eng("pool", sem("s_pool"))
    SP = Eng("sp", sem("s_sp"))
    slot_sems = [sem(f"s_slot{i}") for i in range(NSLOT)]
    slot_cnt = [0] * NSLOT
    ld_sem = sem("s_ld")
    ld_cnt = [0]
    st_sem = sem("s_st")
    st_cnt = [0]
    TR = Tracker()

    def job(eng, fns, reads=(), writes=()):
        reads, writes = list(reads), list(writes)
        eng.wait(TR.deps_read(reads) + TR.deps_write(writes), skip_own=(eng is PE))
        for fn in fns[:-1]:
            eng.emit(fn)
        tok = eng.emit(fns[-1], signal=True)
        TR.did_read(reads, tok)
        TR.did_write(writes, tok)
        return tok

    def dma(q, out_ap, in_ap, semh, cnt, idx, reads=(), writes=(), **kw):
        reads, writes = list(reads), list(writes)
        q.wait(TR.deps_read(reads) + TR.deps_write(writes))
        cnt[idx] += 16
        q.ops.append(("dma", out_ap, in_ap, semh, kw))
        tok = Tok(semh, cnt[idx])
        TR.did_read(reads, tok)
        TR.did_write(writes, tok)
        return tok

    def barrier():
        toks = [Tok(e.sem, e.cnt) for e in (PE, ACT, DVE) if e.cnt > 0]
        for e in (PE, ACT, DVE, SP):
            e.wait(toks)

    def dump(name, ap, dt):
        barrier()
        shp = list(ap.shape)
        dd = nc.dram_tensor(name, shp, dt, kind="ExternalOutput").ap()
        SP.wait([Tok(e.sem, e.cnt) for e in (PE, ACT, DVE) if e.cnt > 0])
        st_cnt[0] += 16
        SP.ops.append(("dma", dd, ap, st_sem, {}))

    slot_next = [0]

    def load_piece(src_ap, nel):
        s = slot_next[0]
        slot_next[0] = (s + 1) % NSLOT
        nb = nel // 2048
        dma(POOL, ring[:, s, :nel].rearrange("p (a b) -> p a b", a=nb),
            src_ap.rearrange("p (a b) -> p a b", a=nb),
            slot_sems[s], slot_cnt, s, writes=[("w", s)])
        return s

    bank_next = [0]

    def next_banks(n):
        out = []
        for _ in range(n):
            out.append(bank_next[0])
            bank_next[0] = (bank_next[0] + 1) % 6
        return out

    stat_next = [0]
    sq_next = [0]
    std_next = [0]

    def xkeys(c, t0, n):
        return [("x", c, ch) for ch in range(t0 // 128, (t0 + n - 1) // 128 + 1)]

    def mm(o, l, r, st, sp):
        return lambda e: e.matmul(o, l, r, start=st, stop=sp)

    def pe_task(mms, reads, banks):
        return job(PE, [mm(*m) for m in mms], reads=reads, writes=[("ps", b) for b in banks])

    def act_fn(out, in_, func, **kw):
        return lambda e: e.activation(out=out, in_=in_, func=func, **kw)

    def tt(out, in0, in1, op):
        return lambda e: e.tensor_tensor(out=out, in0=in0, in1=in1, op=op)

    def stt(out, in0, scalar, in1, op0, op1):
        return lambda e: e.scalar_tensor_tensor(out=out, in0=in0, scalar=scalar, in1=in1, op0=op0, op1=op1)

    def ts(out, in0, s1, s2, op0, op1=None):
        if op1 is None:
            return lambda e: e.tensor_scalar(out=out, in0=in0, scalar1=s1, scalar2=None, op0=op0)
        return lambda e: e.tensor_scalar(out=out, in0=in0, scalar1=s1, scalar2=s2, op0=op0, op1=op1)

    dma(SP, xT[:, :, :], xT_d, ld_sem, ld_cnt, 0, writes=[("x", c, ch) for c in range(KC) for ch in range(9)])
    dma(SP, gains[:, :, :].rearrange("p a b -> p (a b)"), gains_d, ld_sem, ld_cnt, 0, writes=[("gains",)])
    dma(SP, cw[:, :, :].rearrange("p a b -> p (a b)"), cw_d, ld_sem, ld_cnt, 0, writes=[("cw",)])
    tok_ld = dma(SP, hmask[:, :], hmask_d, ld_sem, ld_cnt, 0, writes=[("hmask",)])
    for k in list(TR.w.keys()):
        TR.w[k] = tok_ld
    job(DVE, [lambda e: e.memset(ones[:, :], 1.0)], writes=[("ones",)])
    job(DVE, [lambda e: e.memset(e0[:, :], 0.0)], writes=[("e0",)])
    job(DVE, [lambda e: e.memset(e0[0:1, :], 1.0)], writes=[("e0",)])
    job(DVE, [lambda e: e.memset(eps6[:, :], 1e-6)], writes=[("eps",)])
    job(DVE, [lambda e: e.memset(eps5[:, :], 1e-5)], writes=[("eps",)])

    def norm_block(gi, t0, n, hdst, hkey, f32out=False):
        sbk = 6 + stat_next[0]
        stat_next[0] ^= 1
        for c in range(KC):
            s = sq_next[0]
            sq_next[0] = (s + 1) % 4
            job(ACT, [act_fn(sq[:, s, :n], xT[:, c, t0:t0 + n], AF.Square)],
                reads=xkeys(c, t0, n), writes=[("sq", s)])
            job(PE, [mm(ps[sbk][:, :n], ones[:, :], sq[:, s, :n], c == 0, c == KC - 1)],
                reads=[("sq", s), ("ones",)], writes=[("ps", sbk)])
        sl = std_next[0]
        std_next[0] ^= 1
        job(ACT, [act_fn(std[:, sl, :n], ps[sbk][:, :n], AF.Sqrt, bias=eps6[:, 0:1], scale=1.0 / D)],
            reads=[("ps", sbk), ("eps",)], writes=[("std", sl)])
        job(DVE, [lambda e: e.reciprocal(out=std[:, sl, :n], in_=std[:, sl, :n])],
            reads=[("std", sl)], writes=[("std", sl)])
        fns = [stt(hdst[:, c, :], xT[:, c, t0:t0 + n], gains[:, gi, c:c + 1], std[:, sl, :n], ALU.mult, ALU.mult)
               for c in range(KC)]
        rk = [("std", sl), ("gains",)]
        for c in range(KC):
            rk += xkeys(c, t0, n)
        job(DVE, fns, reads=rk, writes=[hkey])

    act_next = [0]

    def act_slot(n):
        a = act_next[0]
        act_next[0] ^= 1
        return a, actT[:, a * 512: a * 512 + n]

    def resid_evac(d, t0, n, bank, scale):
        if scale == 1.0:
            fn = tt(xT[:, d, t0:t0 + n], ps[bank][:, :n], xT[:, d, t0:t0 + n], ALU.add)
        else:
            fn = stt(xT[:, d, t0:t0 + n], ps[bank][:, :n], scale, xT[:, d, t0:t0 + n], ALU.mult, ALU.add)
        job(DVE, [fn], reads=[("ps", bank)] + xkeys(d, t0, n), writes=xkeys(d, t0, n))

    def proj_out(w_d, blocks, tb, G, gkeys):
        for q in range(4):
            s = load_piece(w_d[q], SLOT)
            W = ring[:, s, :].rearrange("p (k c) -> p k c", k=KC)
            for dd in range(4):
                d = q * 4 + dd
                bk = next_banks(len(blocks))
                mms = []
                for kc in range(KC):
                    for bi, (t0, n) in enumerate(blocks):
                        mms.append((ps[bk[bi]][:, :n], W[:, kc, dd * 128:(dd + 1) * 128],
                                    G[:, kc, t0 - tb:t0 - tb + n], kc == 0, kc == KC - 1))
                pe_task(mms, [("w", s)] + list(gkeys), bk)
                for bi, (t0, n) in enumerate(blocks):
                    resid_evac(d, t0, n, bk[bi], 1.0)

    def ffn(l, f, gi, blocks, tb):
        ntl = sum(n for _, n in blocks)
        barrier()
        H = pv(0, BF16, [KC, ntl])
        HID = pv(KC * ntl * 2, BF16, [12, ntl])
        nb = len(blocks)
        for bi, (t0, n) in enumerate(blocks):
            norm_block(gi, t0, n, H[:, :, t0 - tb:t0 - tb + n], ("h", bi))
        hk = [("h", bi) for bi in range(nb)]
        for g, (j0, ng) in enumerate(GROUPS):
            for pk in range(j0 // 2, (j0 + ng) // 2):
                s = load_piece(w13_d[(l, f)][pk], SLOT)
                W = ring[:, s, :].rearrange("p (k c) -> p k c", k=KC)
                for jj in range(2):
                    jl = 2 * pk + jj - j0
                    gb = next_banks(nb)
                    ub = next_banks(nb)
                    for banks_, col0 in ((gb, jj * 256), (ub, jj * 256 + 128)):
                        mms = []
                        for kc in range(KC):
                            for bi, (t0, n) in enumerate(blocks):
                                mms.append((ps[banks_[bi]][:, :n], W[:, kc, col0:col0 + 128],
                                            H[:, kc, t0 - tb:t0 - tb + n], kc == 0, kc == KC - 1))
                        pe_task(mms, [("w", s)] + hk, banks_)
                    for bi, (t0, n) in enumerate(blocks):
                        a, aT = act_slot(n)
                        job(ACT, [act_fn(aT, ps[gb[bi]][:, :n], AF.Silu)], reads=[("ps", gb[bi])], writes=[("act", a)])
                        job(DVE, [tt(HID[:, jl, t0 - tb:t0 - tb + n], ps[ub[bi]][:, :n], aT, ALU.mult)],
                            reads=[("ps", ub[bi]), ("act", a)], writes=[("hid", jl, bi)])
            for q in range(4):
                s = load_piece(w2_d[(l, f)][g * 4 + q][:, :ng * 512], ng * 512)
                W2 = ring[:, s, :ng * 512].rearrange("p (k c) -> p k c", k=ng)
                for dd in range(4):
                    d = q * 4 + dd
                    bk = next_banks(nb)
                    mms = []
                    for hc in range(ng):
                        for bi, (t0, n) in enumerate(blocks):
                            mms.append((ps[bk[bi]][:, :n], W2[:, hc, dd * 128:(dd + 1) * 128],
                                        HID[:, hc, t0 - tb:t0 - tb + n], hc == 0, hc == ng - 1))
                    pe_task(mms, [("w", s)] + [("hid", hc, bi) for hc in range(ng) for bi in range(nb)], bk)
                    for bi, (t0, n) in enumerate(blocks):
                        resid_evac(d, t0, n, bk[bi], 0.5)

    def gmlp(gi):
        barrier()
        Hb = pv(0, BF16, [KC, 384])
        Vtm = pv(12288, BF16, [3, D])
        Gb = pv(24576, BF16, [KC, 384])
        LNG = pv(36864, F32, [D])
        LNB = pv(45056, F32, [D])
        WsT = pv(53248, BF16, [8, 128])
        BSH = pv(55296, BF16, [8, 384])
        BSL = pv(61440, BF16, [8, 384])
        LNS = pv(67584, F32, [2, 512])
        WsF = pv(0, F32, [8, 128])
        TrF = pv(4096, F32, [128])
        BsF = pv(8192, F32, [8, 128])
        t1 = dma(SP, LNG, lng_d.partition_broadcast(128), ld_sem, ld_cnt, 0, writes=[("lng",)])
        t2 = dma(SP, LNB, lnb_d.partition_broadcast(128), ld_sem, ld_cnt, 0, writes=[("lnb",)])
        t3 = dma(SP, WsF.rearrange("p a b -> p (a b)"), wsT_d, ld_sem, ld_cnt, 0, writes=[("wsf",)])
        t4 = dma(SP, TrF, trilT_d, ld_sem, ld_cnt, 0, writes=[("trf",)])
        t5 = dma(SP, BsF[0:1].rearrange("p a b -> p (a b)"), bs_d, ld_sem, ld_cnt, 0, writes=[("bsf",)])
        for k in (("lng",), ("lnb",), ("wsf",), ("trf",), ("bsf",)):
            TR.w[k] = t5
        job(DVE, [tt(WsT[:, g, :], WsF[:, g, :], TrF, ALU.mult) for g in range(8)],
            reads=[("wsf",), ("trf",)], writes=[("wst",)])
        job(DVE, [lambda e: e.memset(BSH.rearrange("p a b -> p (a b)"), 0.0),
                  lambda e: e.memset(BSL.rearrange("p a b -> p (a b)"), 0.0)], writes=[("bsh",), ("bsl",)])
        job(DVE, [(lambda e, g=g, r=r: e.tensor_copy(out=BSH[0:1, g, r * 128:(r + 1) * 128], in_=BsF[0:1, g, :]))
                  for g in range(8) for r in range(3)], reads=[("bsf",)], writes=[("bsh",)])
        job(DVE, [tt(BSL[0:1, g, r * 128:(r + 1) * 128], BsF[0:1, g, :], BSH[0:1, g, r * 128:(r + 1) * 128], ALU.subtract)
                  for g in range(8) for r in range(3)], reads=[("bsf",), ("bsh",)], writes=[("bsl",)])
        barrier()
        for b, (t0, n) in enumerate(BL3):
            norm_block(gi, t0, n, Hb, ("h", 0))
            for fb in range(4):
                s = load_piece(gv_d[fb], SLOT)
                W = ring[:, s, :].rearrange("p (k c) -> p k c", k=KC)
                for ch in range(3):
                    bk = next_banks(1)
                    mms = [(ps[bk[0]][:, :], Hb[:, kc, ch * 128:(ch + 1) * 128], W[:, kc, :], kc == 0, kc == KC - 1)
                           for kc in range(KC)]
                    pe_task(mms, [("w", s), ("h", 0)], bk)
                    col = ch * 4 + fb
                    job(ACT, [act_fn(Vtm[:, ch, fb * 512:(fb + 1) * 512], ps[bk[0]][:, :], AF.Gelu,
                                     accum_out=lnst[:, col:col + 1])],
                        reads=[("ps", bk[0])], writes=[("v", ch, fb), ("s1", col)])
                    s_ = sq_next[0]
                    sq_next[0] = (s_ + 1) % 4
                    job(ACT, [act_fn(sq[:, s_, :], Vtm[:, ch, fb * 512:(fb + 1) * 512], AF.Square,
                                     accum_out=lnst[:, 12 + col:13 + col])],
                        reads=[("v", ch, fb)], writes=[("sq", s_), ("s2", col)])
            S1 = lnst[:, 0:12].rearrange("p (a b) -> p a b", a=3)
            S2 = lnst[:, 12:24].rearrange("p (a b) -> p a b", a=3)
            m1, m2, msq, var, sd, rstd, nmr = (lnst[:, 24 + 3 * i:27 + 3 * i] for i in range(7))
            allv = [("v", ch, fb) for ch in range(3) for fb in range(4)]
            job(DVE, [lambda e: e.tensor_reduce(out=m1, in_=S1, axis=AX.X, op=ALU.add),
                      lambda e: e.tensor_reduce(out=m2, in_=S2, axis=AX.X, op=ALU.add)],
                reads=[("s1", c_) for c_ in range(12)] + [("s2", c_) for c_ in range(12)], writes=[("m12",)])
            job(DVE, [ts(m1, m1, 1.0 / D, None, ALU.mult), ts(m2, m2, 1.0 / D, None, ALU.mult)],
                reads=[("m12",)], writes=[("m12",)])
            job(DVE, [tt(msq, m1, m1, ALU.mult)], reads=[("m12",)], writes=[("msq",)])
            job(DVE, [tt(var, m2, msq, ALU.subtract)], reads=[("m12",), ("msq",)], writes=[("var",)])
            job(ACT, [act_fn(sd, var, AF.Sqrt, bias=eps5[:, 0:1], scale=1.0)], reads=[("var",), ("eps",)], writes=[("sd",)])
            job(DVE, [lambda e: e.reciprocal(out=rstd, in_=sd)], reads=[("sd",)], writes=[("rstd",)])
            job(DVE, [stt(nmr, m1, -1.0, rstd, ALU.mult, ALU.mult)], reads=[("m12",), ("rstd",)], writes=[("nmr",)])
            for ch in range(3):
                for fb in range(4):
                    sl = (ch * 4 + fb) % 2
                    vs = Vtm[:, ch, fb * 512:(fb + 1) * 512]
                    job(DVE, [ts(LNS[:, sl, :], vs, rstd[:, ch:ch + 1], nmr[:, ch:ch + 1], ALU.mult, ALU.add)],
                        reads=[("v", ch, fb), ("rstd",), ("nmr",)], writes=[("lns", sl)])
                    job(DVE, [tt(LNS[:, sl, :], LNS[:, sl, :], LNG[:, fb * 512:(fb + 1) * 512], ALU.mult)],
                        reads=[("lns", sl), ("lng",)], writes=[("lns", sl)])
                    job(DVE, [tt(vs, LNS[:, sl, :], LNB[:, fb * 512:(fb + 1) * 512], ALU.add)],
                        reads=[("lns", sl), ("lnb",)], writes=[("v", ch, fb)])
            if dbgopts.get("gmlp") == "ln":
                dump("d_vtm", Vtm, BF16)
                dump("d_lnst", lnst[:, :], F32)
                dump("d_hb", Hb, BF16)
                return
            for pk in range(4):
                s = load_piece(gu_d[pk], SLOT)
                W = ring[:, s, :].rearrange("p (k c) -> p k c", k=KC)
                for ff in range(4):
                    fc = pk * 4 + ff
                    g = fc // 2
                    ub = next_banks(1)
                    mms = [(ps[ub[0]][:, :n], W[:, kc, ff * 128:(ff + 1) * 128], Hb[:, kc, :], kc == 0, kc == KC - 1)
                           for kc in range(KC)]
                    pe_task(mms, [("w", s), ("h", 0)], ub)
                    fbk = next_banks(1)
                    mms = [(ps[fbk[0]][:, :n], e0[:, :], BSH[:, g, :], True, False),
                           (ps[fbk[0]][:, :n], e0[:, :], BSL[:, g, :], False, False)]
                    for ch in range(3):
                        mms.append((ps[fbk[0]][:, ch * 128:(ch + 1) * 128], Vtm[:, ch, fc * 128:(fc + 1) * 128],
                                    WsT[:, g, :], False, ch == 2))
                    pe_task(mms, [("e0",), ("bsh",), ("bsl",), ("wst",)] + [("v", ch, fc // 4) for ch in range(3)], fbk)
                    a, aT = act_slot(n)
                    job(ACT, [act_fn(aT, ps[ub[0]][:, :n], AF.Gelu)], reads=[("ps", ub[0])], writes=[("act", a)])
                    job(DVE, [tt(Gb[:, fc, :], ps[fbk[0]][:, :n], aT, ALU.mult)],
                        reads=[("ps", fbk[0]), ("act", a)], writes=[("g", fc)])
            if dbgopts.get("gmlp") == "gate":
                dump("d_vtm", Vtm, BF16)
                dump("d_gb", Gb, BF16)
                dump("d_bsh", BSH, BF16)
                dump("d_bsl", BSL, BF16)
                dump("d_wst", WsT, BF16)
                return
            for q in range(4):
                s = load_piece(go_d[q], SLOT)
                W = ring[:, s, :].rearrange("p (k c) -> p k c", k=KC)
                for dd in range(4):
                    d = q * 4 + dd
                    bk = next_banks(1)
                    mms = [(ps[bk[0]][:, :n], W[:, kc, dd * 128:(dd + 1) * 128], Gb[:, kc, :], kc == 0, kc == KC - 1)
                           for kc in range(KC)]
                    pe_task(mms, [("w", s)] + [("g", fc) for fc in range(KC)], bk)
                    resid_evac(d, t0, n, bk[0], 1.0)

    def xattn(l, gi, gmi, blocks):
        barrier()
        nmax = max(n for _, n in blocks)
        Hb = pv(0, BF16, [KC, 512])
        Qb = pv(16384, BF16, [KC, 512])
        Ob = pv(32768, BF16, [KC, 512])
        KT = pv(49152, BF16, [KC, MEM])
        VT = pv(57344, BF16, [2, D])
        PT = pv(65536, BF16, [2, 2, 512])
        RS = pv(69632, F32, [512])
        MT = pv(0, F32, [KC, MEM])
        MN = pv(16384, BF16, [KC, MEM])
        tm = dma(SP, MT, memT_d, ld_sem, ld_cnt, 0, writes=[("mt",)])
        sbk = 6 + stat_next[0]
        stat_next[0] ^= 1
        for c in range(KC):
            s = sq_next[0]
            sq_next[0] = (s + 1) % 4
            job(ACT, [act_fn(sq[:, s, :MEM], MT[:, c, :], AF.Square)], reads=[("mt",)], writes=[("sq", s)])
            job(PE, [mm(ps[sbk][:, :MEM], ones[:, :], sq[:, s, :MEM], c == 0, c == KC - 1)],
                reads=[("sq", s), ("ones",)], writes=[("ps", sbk)])
        sl = std_next[0]
        std_next[0] ^= 1
        job(ACT, [act_fn(std[:, sl, :MEM], ps[sbk][:, :MEM], AF.Sqrt, bias=eps6[:, 0:1], scale=1.0 / D)],
            reads=[("ps", sbk), ("eps",)], writes=[("std", sl)])
        job(DVE, [lambda e: e.reciprocal(out=std[:, sl, :MEM], in_=std[:, sl, :MEM])], reads=[("std", sl)], writes=[("std", sl)])
        job(DVE, [stt(MN[:, c, :], MT[:, c, :], gains[:, gmi, c:c + 1], std[:, sl, :MEM], ALU.mult, ALU.mult)
                  for c in range(KC)], reads=[("std", sl), ("mt",), ("gains",)], writes=[("mn",)])
        for pk in range(4):
            s = load_piece(wk_d[l][pk], SLOT)
            W = ring[:, s, :].rearrange("p (k c) -> p k c", k=KC)
            for ff in range(4):
                fc = pk * 4 + ff
                bk = next_banks(1)
                mms = [(ps[bk[0]][:, :MEM], W[:, kc, ff * 128:(ff + 1) * 128], MN[:, kc, :], kc == 0, kc == KC - 1)
                       for kc in range(KC)]
                pe_task(mms, [("w", s), ("mn",)], bk)
                job(ACT, [act_fn(KT[:, fc, :], ps[bk[0]][:, :MEM], AF.Copy)], reads=[("ps", bk[0])], writes=[("kt", fc)])
        for fb in range(4):
            s = load_piece(wv_d[l][fb], SLOT)
            W = ring[:, s, :].rearrange("p (k c) -> p k c", k=KC)
            for mc in range(2):
                bk = next_banks(1)
                mms = [(ps[bk[0]][:, :], MN[:, kc, mc * 128:(mc + 1) * 128], W[:, kc, :], kc == 0, kc == KC - 1)
                       for kc in range(KC)]
                pe_task(mms, [("w", s), ("mn",)], bk)
                job(DVE, [lambda e, o=VT[:, mc, fb * 512:(fb + 1) * 512], i=ps[bk[0]][:, :]: e.tensor_copy(out=o, in_=i)],
                    reads=[("ps", bk[0])], writes=[("vt", mc, fb)])
        barrier()
        qscale = 512.0 ** -0.5
        pslot = [0]
        for b, (t0, n) in enumerate(blocks):
            norm_block(gi, t0, n, Hb[:, :, :n], ("h", 0))
            for pk in range(4):
                s = load_piece(wq_d[l][pk], SLOT)
                W = ring[:, s, :].rearrange("p (k c) -> p k c", k=KC)
                for ff in range(4):
                    fc = pk * 4 + ff
                    bk = next_banks(1)
                    mms = [(ps[bk[0]][:, :n], W[:, kc, ff * 128:(ff + 1) * 128], Hb[:, kc, :n], kc == 0, kc == KC - 1)
                           for kc in range(KC)]
                    pe_task(mms, [("w", s), ("h", 0)], bk)
                    job(ACT, [act_fn(Qb[:, fc, :n], ps[bk[0]][:, :n], AF.Copy, scale=qscale)],
                        reads=[("ps", bk[0])], writes=[("q", fc)])
            for hd in range(4):
                sp_ = pslot[0]
                pslot[0] ^= 1
                for mc in range(2):
                    bk = next_banks(1)
                    mms = [(ps[bk[0]][:, :n], KT[:, hd * 4 + dc, mc * 128:(mc + 1) * 128], Qb[:, hd * 4 + dc, :n],
                            dc == 0, dc == 3) for dc in range(4)]
                    pe_task(mms, [("kt", hd * 4 + dc) for dc in range(4)] + [("q", hd * 4 + dc) for dc in range(4)], bk)
                    job(ACT, [act_fn(PT[:, sp_, mc, :n], ps[bk[0]][:, :n], AF.Exp)],
                        reads=[("ps", bk[0])], writes=[("pt", sp_, mc)])
                bk = next_banks(1)
                mms = [(ps[bk[0]][:, :n], ones[:, :], PT[:, sp_, mc, :n], mc == 0, mc == 1) for mc in range(2)]
                pe_task(mms, [("ones",), ("pt", sp_, 0), ("pt", sp_, 1)], bk)
                job(DVE, [lambda e, o=RS[:, :n], i=ps[bk[0]][:, :n]: e.reciprocal(out=o, in_=i)],
                    reads=[("ps", bk[0])], writes=[("rs",)])
                for dc in range(4):
                    fc = hd * 4 + dc
                    bk = next_banks(1)
                    mms = [(ps[bk[0]][:, :n], VT[:, mc, fc * 128:(fc + 1) * 128], PT[:, sp_, mc, :n], mc == 0, mc == 1)
                           for mc in range(2)]
                    pe_task(mms, [("vt", mc, fc // 4) for mc in range(2)] + [("pt", sp_, 0), ("pt", sp_, 1)], bk)
                    job(DVE, [tt(Ob[:, fc, :n], ps[bk[0]][:, :n], RS[:, :n], ALU.mult)],
                        reads=[("ps", bk[0]), ("rs",)], writes=[("o", fc)])
            proj_out(wo_d[l], [(t0, n)], t0, Ob, [("o", fc) for fc in range(KC)])
        return

    def conv(gi):
        barrier()
        tb = 126
        H = pv(0, BF16, [KC, 1028])
        G = pv(32896, BF16, [KC, 1024])
        ACC = pv(65664, F32, [1024])
        Z = actT
        for bi, (t0, n) in enumerate(CB):
            norm_block(gi, t0, n, H[:, :, t0 - tb:t0 - tb + n], ("h", bi))
        hk = [("h", bi) for bi in range(3)]
        for fc in range(KC):
            s = load_piece(cin_d[fc], KC * 384)
            W = ring[:, s, :KC * 384].rearrange("p (k c) -> p k c", k=KC)
            cb_ = next_banks(3)
            vb_ = next_banks(3)
            for banks_, col0 in ((cb_, 128), (vb_, 256)):
                mms = []
                for kc in range(KC):
                    for bi, (t0, n) in enumerate(CB):
                        mms.append((ps[banks_[bi]][:, :n], W[:, kc, col0:col0 + 128], H[:, kc, t0 - tb:t0 - tb + n],
                                    kc == 0, kc == KC - 1))
                pe_task(mms, [("w", s)] + hk, banks_)
            for bi, (t0, n) in enumerate(CB):
                job(ACT, [act_fn(Z[:, t0 - tb:t0 - tb + n], ps[cb_[bi]][:, :n], AF.Copy)],
                    reads=[("ps", cb_[bi])], writes=[("z", bi)])
                job(DVE, [tt(Z[:, t0 - tb:t0 - tb + n], ps[vb_[bi]][:, :n], Z[:, t0 - tb:t0 - tb + n], ALU.mult)],
                    reads=[("ps", vb_[bi]), ("z", bi)], writes=[("z", bi)])
            job(DVE, [ts(Z[:, 0:2], Z[:, 0:2], hmask[:, 0:1], None, ALU.mult)], reads=[("z", 0), ("hmask",)], writes=[("z", 0)])
            zk = [("z", bi) for bi in range(3)]
            job(ACT, [act_fn(ACC, Z[:, 2:1026], AF.Copy, scale=cw[:, fc, 2:3])], reads=zk + [("cw",)], writes=[("acc",)])
            job(DVE, [stt(ACC, Z[:, 1:1025], cw[:, fc, 1:2], ACC, ALU.mult, ALU.add)], reads=zk + [("acc",), ("cw",)], writes=[("acc",)])
            job(DVE, [stt(ACC, Z[:, 0:1024], cw[:, fc, 0:1], ACC, ALU.mult, ALU.add)], reads=zk + [("acc",), ("cw",)], writes=[("acc",)])
            bb_ = next_banks(2)
            mms = []
            for kc in range(KC):
                for bi, (t0, n) in enumerate(BL2):
                    mms.append((ps[bb_[bi]][:, :n], W[:, kc, 0:128], H[:, kc, t0 - tb:t0 - tb + n], kc == 0, kc == KC - 1))
            pe_task(mms, [("w", s)] + hk, bb_)
            for bi, (t0, n) in enumerate(BL2):
                job(DVE, [tt(G[:, fc, t0 - 128:t0 - 128 + n], ps[bb_[bi]][:, :n], ACC[:, t0 - 128:t0 - 128 + n], ALU.mult)],
                    reads=[("ps", bb_[bi]), ("acc",)], writes=[("g", fc, bi)])
        for q in range(4):
            s = load_piece(co_d[q], SLOT)
            W = ring[:, s, :].rearrange("p (k c) -> p k c", k=KC)
            for dd in range(4):
                d = q * 4 + dd
                bk = next_banks(2)
                mms = []
                for kc in range(KC):
                    for bi, (t0, n) in enumerate(BL2):
                        mms.append((ps[bk[bi]][:, :n], W[:, kc, dd * 128:(dd + 1) * 128], G[:, kc, t0 - 128:t0 - 128 + n],
                                    kc == 0, kc == KC - 1))
                pe_task(mms, [("w", s)] + [("g", fc, bi) for fc in range(KC) for bi in range(2)], bk)
                for bi, (t0, n) in enumerate(BL2):
                    resid_evac(d, t0, n, bk[bi], 1.0)

    def final():
        barrier()
        Y = pv(0, F32, [2, KC, 512])
        for bi, (t0, n) in enumerate(BL2):
            norm_block(10, t0, n, Y[:, bi], ("y", bi))
            dma(SP, out_d[:, :, t0 - 128:t0 - 128 + n], Y[:, bi], st_sem, st_cnt, 0, reads=[("y", bi)])

    stages = [
        lambda: ffn(0, 0, 0, BL3, 0),
        lambda: gmlp(1),
        lambda: xattn(0, 2, 3, BL3),
        lambda: ffn(0, 1, 4, BL3, 0),
        lambda: ffn(1, 0, 5, BL3, 0),
        lambda: conv(6),
        lambda: xattn(1, 7, 8, BL2),
        lambda: ffn(1, 1, 9, BL2, 128),
        lambda: final(),
    ]
    for st in stages[:nstages]:
        st()
    if dbg:
        barrier()
        allx = [("x", c, ch) for c in range(KC) for ch in range(9)]
        dma(SP, out_d, xT[:, :, :], st_sem, st_cnt, 0, reads=allx)
    SP.ops.append(("wait", st_sem, st_cnt[0]))

    with nc.Block() as block:
        @block.tensor
        def _(e):
            PE.replay(e)

        @block.scalar
        def _(e):
            ACT.replay(e)

        @block.vector
        def _(e):
            DVE.replay(e)

        @block.gpsimd
        def _(e):
            POOL.replay(e)

        @block.sync
        def _(e):
            SP.replay(e)
    es.close()
    return nc


def _pieces(W, ncols):
    K, C = W.shape
    npc = C // ncols
    return np.ascontiguousarray(W.reshape(K // 128, 128, npc, ncols).transpose(2, 1, 0, 3)).reshape(npc, 128, (K // 128) * ncols)


def _prep(inputs):
    f = lambda a: np.asarray(a, dtype=np.float32)
    x = f(inputs["x"])[0]
    mem = f(inputs["mem"])[0]
    shared = {}
    shared["memT"] = np.ascontiguousarray(mem.T)
    gl = []
    for l in range(2):
        for nm in ("ffn1_norm", "mix_norm", "xattn_norm", "mem_norm", "ffn2_norm"):
            gl.append(f(inputs[nm])[l])
    gl.append(f(inputs["final_norm"]))
    g = np.stack(gl, 0)
    shared["gains"] = np.ascontiguousarray(g.reshape(11, 16, 128).transpose(2, 0, 1)).reshape(128, 11 * 16)
    for l in range(2):
        for fi, nm in enumerate(("ffn1", "ffn2")):
            w13 = f(inputs[nm + "_w13"])[l]
            gate = w13[:, :5632].reshape(16, 128, 22, 2, 1, 128)
            up = w13[:, 5632:].reshape(16, 128, 22, 2, 1, 128)
            cat = np.concatenate([gate, up], axis=4)
            shared[f"w13_{l}_{fi}"] = np.ascontiguousarray(cat.transpose(2, 1, 0, 3, 4, 5)).reshape(22, 128, SLOT)
            w2 = f(inputs[nm + "_w2"])[l]
            arr = np.zeros((16, 128, 12 * 512), np.float32)
            for gi_, (j0, ng) in enumerate(GROUPS):
                blk = w2[j0 * 128:(j0 + ng) * 128].reshape(ng, 128, 4, 512)
                arr[gi_ * 4:gi_ * 4 + 4, :, :ng * 512] = blk.transpose(2, 1, 0, 3).reshape(4, 128, ng * 512)
            shared[f"w2_{l}_{fi}"] = arr
        shared[f"wq_{l}"] = _pieces(f(inputs["xattn_wq"])[l], 512)
        wkv = f(inputs["xattn_wkv"])[l]
        shared[f"wk_{l}"] = _pieces(wkv[:, :D], 512)
        shared[f"wv_{l}"] = _pieces(wkv[:, D:], 512)
        shared[f"wo_{l}"] = _pieces(f(inputs["xattn_wo"])[l], 512)
    win = f(inputs["gmlp_w_in"])[0]
    shared["gmlp_u"] = _pieces(win[:, :D], 512)
    shared["gmlp_v"] = _pieces(win[:, D:], 512)
    shared["gmlp_o"] = _pieces(f(inputs["gmlp_w_out"])[0], 512)
    shared["ln_g"] = f(inputs["gmlp_ln_g"])[0].reshape(1, D)
    shared["ln_b"] = f(inputs["gmlp_ln_b"])[0].reshape(1, D)
    ws = f(inputs["gmlp_w_s"])[0]
    shared["wsT"] = np.ascontiguousarray(ws.transpose(2, 0, 1)).reshape(128, 8 * 128)
    shared["trilT"] = np.ascontiguousarray(np.tril(np.ones((128, 128), np.float32)).T)
    shared["b_s"] = f(inputs["gmlp_b_s"])[0].reshape(1, 8 * 128)
    cin = f(inputs["conv_w_in"])[0]
    c3 = cin.reshape(16, 128, 3, 16, 128)
    shared["conv_in"] = np.ascontiguousarray(c3.transpose(3, 1, 0, 2, 4)).reshape(16, 128, 16 * 384)
    cwv = f(inputs["conv_w"])[0]
    shared["conv_w"] = np.ascontiguousarray(cwv.reshape(3, 16, 128).transpose(2, 1, 0)).reshape(128, 48)
    shared["conv_o"] = _pieces(f(inputs["conv_w_out"])[0], 512)
    in_maps = []
    for i in range(NCORES):
        xs = np.zeros((NT, D), np.float32)
        if i > 0:
            xs[:] = x[i * TOWN - HALO:(i + 1) * TOWN]
        else:
            xs[HALO:] = x[:TOWN]
        m = dict(shared)
        m["xT"] = np.ascontiguousarray(xs.T)
        m["hmask"] = np.full((128, 2), 1.0 if i > 0 else 0.0, np.float32)
        in_maps.append(m)
    return in_maps


_NC_CACHE = {}


def kernel(**inputs):
    in_maps = _prep(inputs)
    if "nc" not in _NC_CACHE:
        _NC_CACHE["nc"] = build()
    nc = _NC_CACHE["nc"]
    res = run_bass_kernel_spmd(nc, in_maps, core_ids=list(range(NCORES)))
    outs = [np.asarray(r["outT"]).T for r in res.results]
    return np.concatenate(outs, axis=0).reshape(1, NCORES * TOWN, D).astype(np.float32)
```

```python
import contextlib
import numpy as np
import concourse.bass as bass
import concourse.mybir as mybir
from concourse.bass_utils import run_bass_kernel_spmd

F32 = mybir.dt.float32
BF16 = mybir.dt.bfloat16
AF = mybir.ActivationFunctionType
ALU = mybir.AluOpType
AX = mybir.AxisListType

D = 2048
KC = 16
HCH = 44
MEM = 256
NCORES = 8
TOWN = 1024
HALO = 128
NT = TOWN + HALO
SLOT = 8192
NSLOT = 3
GROUPS = [(0, 12), (12, 12), (24, 12), (36, 8)]
BL3 = [(0, 384), (384, 384), (768, 384)]
BX = [(126, 386), (512, 384), (896, 256)]
BY = [(128, 384), (512, 384), (896, 256)]
XB1 = [(128, 512), (640, 512)]
BL2 = BY
CB = BX
NSTAGES_ALL = 9


class Tok:
    __slots__ = ("sem", "val")

    def __init__(self, sem, val):
        self.sem, self.val = sem, val


class Eng:
    def __init__(self, name, sem):
        self.name, self.sem, self.cnt, self.ops, self.seen = name, sem, 0, [], {}

    def wait(self, toks, skip_own=False):
        for t in toks:
            if t is None:
                continue
            if skip_own and t.sem is self.sem:
                continue
            k = id(t.sem)
            if self.seen.get(k, 0) >= t.val:
                continue
            self.seen[k] = t.val
            self.ops.append(("wait", t.sem, t.val))

    def emit(self, fn, signal=False):
        if signal:
            self.cnt += 1
            self.ops.append(("sig", fn))
            return Tok(self.sem, self.cnt)
        self.ops.append(("op", fn))
        return None

    def replay(self, e):
        for op in self.ops:
            if op[0] == "wait":
                e.wait_ge(op[1], op[2])
            elif op[0] == "op":
                op[1](e)
            elif op[0] == "sig":
                op[1](e).then_inc(self.sem, 1)
            elif op[0] == "dma":
                kw = op[4]
                e.dma_start(out=op[1], in_=op[2], **kw).then_inc(op[3], 16)


class Tracker:
    def __init__(self):
        self.w, self.r = {}, {}

    def deps_read(self, keys):
        return [self.w.get(k) for k in keys]

    def deps_write(self, keys):
        out = []
        for k in keys:
            out.append(self.w.get(k))
            out.extend(self.r.get(k, {}).values())
        return out

    def did_read(self, keys, tok):
        for k in keys:
            d = self.r.setdefault(k, {})
            cur = d.get(id(tok.sem))
            if cur is None or cur.val < tok.val:
                d[id(tok.sem)] = tok

    def did_write(self, keys, tok):
        for k in keys:
            self.w[k] = tok
            self.r[k] = {}


def build(nstages=NSTAGES_ALL, dbg=False, dbgopts=None):
    dbgopts = dbgopts or {}
    nc = bass.Bass("TRN2", target_bir_lowering=False)
    es = contextlib.ExitStack()

    def din(name, shape):
        return nc.dram_tensor(name, list(shape), F32, kind="ExternalInput").ap()

    xT_d = din("xT", [D, NT]).rearrange("(c p) t -> p c t", p=128)
    hmask_d = din("hmask", [128, 2])
    memT_d = din("memT", [D, MEM]).rearrange("(c p) t -> p c t", p=128)
    gains_d = din("gains", [128, 11 * 16])
    w13_d = {(l, f): din(f"w13_{l}_{f}", [22, 128, SLOT]) for l in range(2) for f in range(2)}
    w2_d = {(l, f): din(f"w2_{l}_{f}", [16, 128, 12 * 512]) for l in range(2) for f in range(2)}
    gv_d = din("gmlp_v", [4, 128, SLOT])
    gu_d = din("gmlp_u", [4, 128, SLOT])
    go_d = din("gmlp_o", [4, 128, SLOT])
    lng_d = din("ln_g", [1, D])
    lnb_d = din("ln_b", [1, D])
    wsT_d = din("wsT", [128, 8 * 128])
    trilT_d = din("trilT", [128, 128])
    bs_d = din("b_s", [1, 8 * 128])
    cin_d = din("conv_in", [16, 128, 16 * 384])
    cw_d = din("conv_w", [128, 16 * 3])
    co_d = din("conv_o", [4, 128, SLOT])
    wq_d = [din(f"wq_{l}", [4, 128, SLOT]) for l in range(2)]
    wk_d = [din(f"wk_{l}", [4, 128, SLOT]) for l in range(2)]
    wv_d = [din(f"wv_{l}", [4, 128, SLOT]) for l in range(2)]
    wo_d = [din(f"wo_{l}", [4, 128, SLOT]) for l in range(2)]
    if dbg:
        out_d = nc.dram_tensor("dbg", [D, NT], F32, kind="ExternalOutput").ap().rearrange("(c p) t -> p c t", p=128)
    else:
        out_d = nc.dram_tensor("outT", [D, TOWN], F32, kind="ExternalOutput").ap().rearrange("(c p) t -> p c t", p=128)

    def sb(name, shape, dt):
        return es.enter_context(nc.sbuf_tensor(name, list(shape), dt))

    xT = sb("xT_sb", [128, KC, NT], F32)
    P = sb("P_sb", [128, 36864], BF16)
    ring = sb("ring_sb", [128, NSLOT, SLOT], BF16)
    sq = sb("sq_sb", [128, 4, 512], BF16)
    std = sb("std_sb", [128, 2, 512], F32)
    actT = sb("act_sb", [128, 1028], F32)
    ones = sb("ones_sb", [128, 128], BF16)
    e0 = sb("e0_sb", [128, 128], BF16)
    gains = sb("gains_sb", [128, 11, 16], F32)
    eps6 = sb("eps6_sb", [128, 1], F32)
    eps5 = sb("eps5_sb", [128, 1], F32)
    cw = sb("cw_sb", [128, 16, 3], F32)
    hmask = sb("hmask_sb", [128, 2], F32)
    lnst = sb("lnst_sb", [128, 64], F32)

    ps = [es.enter_context(nc.psum_tensor(f"ps{i}", [128, 512], F32)) for i in range(8)]

    def pv(off, dt, shape):
        n = int(np.prod(shape))
        if dt is BF16:
            assert off % 2 == 0
            ap = P[:, off // 2: off // 2 + n]
        else:
            assert off % 4 == 0
            ap = P[:, off // 2: off // 2 + 2 * n].bitcast(F32)
        assert off + n * (2 if dt is BF16 else 4) <= 36864 * 2
        if len(shape) == 2:
            ap = ap.rearrange("p (a b) -> p a b", a=shape[0])
        elif len(shape) == 3:
            ap = ap.rearrange("p (a b c) -> p a b c", a=shape[0], b=shape[1])
        return ap

    def sem(name):
        return es.enter_context(nc.semaphore(name))

    PE = Eng("pe", sem("s_pe"))
    ACT = Eng("act", sem("s_act"))
    DVE = Eng("dve", sem("s_dve"))
    POOL = Eng("pool", sem("s_pool"))
    SP = Eng("sp", sem("s_sp"))
    slot_sems = [sem(f"s_slot{i}") for i in range(NSLOT)]
    slot_cnt = [0] * NSLOT
    ld_sem = sem("s_ld")
    ld_cnt = [0]
    st_sem = sem("s_st")
    st_cnt = [0]
    TR = Tracker()

    def job(eng, fns, reads=(), writes=()):
        reads, writes = list(reads), list(writes)
        eng.wait(TR.deps_read(reads) + TR.deps_write(writes), skip_own=(eng is PE))
        for fn in fns[:-1]:
            eng.emit(fn)
        tok = eng.emit(fns[-1], signal=True)
        TR.did_read(reads, tok)
        TR.did_write(writes, tok)
        return tok

    def dma(q, out_ap, in_ap, semh, cnt, idx, reads=(), writes=(), **kw):
        reads, writes = list(reads), list(writes)
        q.wait(TR.deps_read(reads) + TR.deps_write(writes))
        cnt[idx] += 16
        q.ops.append(("dma", out_ap, in_ap, semh, kw))
        tok = Tok(semh, cnt[idx])
        TR.did_read(reads, tok)
        TR.did_write(writes, tok)
        return tok

    def barrier():
        toks = [Tok(e.sem, e.cnt) for e in (PE, ACT, DVE) if e.cnt > 0]
        for e in (PE, ACT, DVE, SP):
            e.wait(toks)

    def dump(name, ap, dt):
        barrier()
        shp = list(ap.shape)
        dd = nc.dram_tensor(name, shp, dt, kind="ExternalOutput").ap()
        SP.wait([Tok(e.sem, e.cnt) for e in (PE, ACT, DVE) if e.cnt > 0])
        st_cnt[0] += 16
        SP.ops.append(("dma", dd, ap, st_sem, {}))

    slot_next = [0]

    def load_piece(src_ap, nel):
        s = slot_next[0]
        slot_next[0] = (s + 1) % NSLOT
        nb = nel // 2048
        dma(POOL, ring[:, s, :nel].rearrange("p (a b) -> p a b", a=nb),
            src_ap.rearrange("p (a b) -> p a b", a=nb),
            slot_sems[s], slot_cnt, s, writes=[("w", s)])
        return s

    bank_next = [0]

    def next_banks(n):
        out = []
        for _ in range(n):
            out.append(bank_next[0])
            bank_next[0] = (bank_next[0] + 1) % 6
        return out

    stat_next = [0]
    sq_next = [0]
    std_next = [0]

    def xkeys(c, t0, n):
        return [("x", c, ch) for ch in range(t0 // 128, (t0 + n - 1) // 128 + 1)]

    def mm(o, l, r, st, sp):
        return lambda e: e.matmul(o, l, r, start=st, stop=sp)

    def pe_task(mms, reads, banks):
        return job(PE, [mm(*m) for m in mms], reads=reads, writes=[("ps", b) for b in banks])

    def act_fn(out, in_, func, **kw):
        return lambda e: e.activation(out=out, in_=in_, func=func, **kw)

    def tt(out, in0, in1, op):
        return lambda e: e.tensor_tensor(out=out, in0=in0, in1=in1, op=op)

    def stt(out, in0, scalar, in1, op0, op1):
        return lambda e: e.scalar_tensor_tensor(out=out, in0=in0, scalar=scalar, in1=in1, op0=op0, op1=op1)

    def ts(out, in0, s1, s2, op0, op1=None):
        if op1 is None:
            return lambda e: e.tensor_scalar(out=out, in0=in0, scalar1=s1, scalar2=None, op0=op0)
        return lambda e: e.tensor_scalar(out=out, in0=in0, scalar1=s1, scalar2=s2, op0=op0, op1=op1)

    dma(SP, xT[:, :, :], xT_d, ld_sem, ld_cnt, 0, writes=[("x", c, ch) for c in range(KC) for ch in range(9)])
    dma(SP, gains[:, :, :].rearrange("p a b -> p (a b)"), gains_d, ld_sem, ld_cnt, 0, writes=[("gains",)])
    dma(SP, cw[:, :, :].rearrange("p a b -> p (a b)"), cw_d, ld_sem, ld_cnt, 0, writes=[("cw",)])
    tok_ld = dma(SP, hmask[:, :], hmask_d, ld_sem, ld_cnt, 0, writes=[("hmask",)])
    for k in list(TR.w.keys()):
        TR.w[k] = tok_ld
    job(DVE, [lambda e: e.memset(ones[:, :], 1.0)], writes=[("ones",)])
    job(DVE, [lambda e: e.memset(e0[:, :], 0.0)], writes=[("e0",)])
    job(DVE, [lambda e: e.memset(e0[0:1, :], 1.0)], writes=[("e0",)])
    job(DVE, [lambda e: e.memset(eps6[:, :], 1e-6)], writes=[("eps",)])
    job(DVE, [lambda e: e.memset(eps5[:, :], 1e-5)], writes=[("eps",)])

    def norm_block(gi, t0, n, hdst, hkey, f32out=False):
        sbk = 6 + stat_next[0]
        stat_next[0] ^= 1
        for c in range(KC):
            s = sq_next[0]
            sq_next[0] = (s + 1) % 4
            job(ACT, [act_fn(sq[:, s, :n], xT[:, c, t0:t0 + n], AF.Square)],
                reads=xkeys(c, t0, n), writes=[("sq", s)])
            job(PE, [mm(ps[sbk][:, :n], ones[:, :], sq[:, s, :n], c == 0, c == KC - 1)],
                reads=[("sq", s), ("ones",)], writes=[("ps", sbk)])
        sl = std_next[0]
        std_next[0] ^= 1
        job(ACT, [act_fn(std[:, sl, :n], ps[sbk][:, :n], AF.Sqrt, bias=eps6[:, 0:1], scale=1.0 / D)],
            reads=[("ps", sbk), ("eps",)], writes=[("std", sl)])
        job(DVE, [lambda e: e.reciprocal(out=std[:, sl, :n], in_=std[:, sl, :n])],
            reads=[("std", sl)], writes=[("std", sl)])
        fns = [stt(hdst[:, c, :], xT[:, c, t0:t0 + n], gains[:, gi, c:c + 1], std[:, sl, :n], ALU.mult, ALU.mult)
               for c in range(KC)]
        rk = [("std", sl), ("gains",)]
        for c in range(KC):
            rk += xkeys(c, t0, n)
        job(DVE, fns, reads=rk, writes=[hkey])

    act_next = [0]

    def act_slot(n):
        a = act_next[0]
        act_next[0] ^= 1
        return a, actT[:, a * 512: a * 512 + n]

    def resid_evac(d, t0, n, bank, scale):
        if scale == 1.0:
            fn = tt(xT[:, d, t0:t0 + n], ps[bank][:, :n], xT[:, d, t0:t0 + n], ALU.add)
        else:
            fn = stt(xT[:, d, t0:t0 + n], ps[bank][:, :n], scale, xT[:, d, t0:t0 + n], ALU.mult, ALU.add)
        job(DVE, [fn], reads=[("ps", bank)] + xkeys(d, t0, n), writes=xkeys(d, t0, n))

    def proj_out(w_d, blocks, tb, G, gkeys):
        for q in range(4):
            s = load_piece(w_d[q], SLOT)
            W = ring[:, s, :].rearrange("p (k c) -> p k c", k=KC)
            for dd in range(4):
                d = q * 4 + dd
                bk = next_banks(len(blocks))
                mms = []
                for kc in range(KC):
                    for bi, (t0, n) in enumerate(blocks):
                        mms.append((ps[bk[bi]][:, :n], W[:, kc, dd * 128:(dd + 1) * 128],
                                    G[:, kc, t0 - tb:t0 - tb + n], kc == 0, kc == KC - 1))
                pe_task(mms, [("w", s)] + list(gkeys), bk)
                for bi, (t0, n) in enumerate(blocks):
                    resid_evac(d, t0, n, bk[bi], 1.0)

    def ffn(l, f, gi, blocks, tb):
        ntl = sum(n for _, n in blocks)
        barrier()
        H = pv(0, BF16, [KC, ntl])
        HID = pv(KC * ntl * 2, BF16, [12, ntl])
        nb = len(blocks)
        for bi, (t0, n) in enumerate(blocks):
            norm_block(gi, t0, n, H[:, :, t0 - tb:t0 - tb + n], ("h", bi))
        hk = [("h", bi) for bi in range(nb)]
        for g, (j0, ng) in enumerate(GROUPS):
            for pk in range(j0 // 2, (j0 + ng) // 2):
                s = load_piece(w13_d[(l, f)][pk], SLOT)
                W = ring[:, s, :].rearrange("p (k c) -> p k c", k=KC)
                for jj in range(2):
                    jl = 2 * pk + jj - j0
                    gb = next_banks(nb)
                    ub = next_banks(nb)
                    for banks_, col0 in ((gb, jj * 256), (ub, jj * 256 + 128)):
                        mms = []
                        for kc in range(KC):
                            for bi, (t0, n) in enumerate(blocks):
                                mms.append((ps[banks_[bi]][:, :n], W[:, kc, col0:col0 + 128],
                                            H[:, kc, t0 - tb:t0 - tb + n], kc == 0, kc == KC - 1))
                        pe_task(mms, [("w", s)] + hk, banks_)
                    for bi, (t0, n) in enumerate(blocks):
                        a, aT = act_slot(n)
                        job(ACT, [act_fn(aT, ps[gb[bi]][:, :n], AF.Silu)], reads=[("ps", gb[bi])], writes=[("act", a)])
                        job(DVE, [tt(HID[:, jl, t0 - tb:t0 - tb + n], ps[ub[bi]][:, :n], aT, ALU.mult)],
                            reads=[("ps", ub[bi]), ("act", a)], writes=[("hid", jl, bi)])
            for q in range(4):
                s = load_piece(w2_d[(l, f)][g * 4 + q][:, :ng * 512], ng * 512)
                W2 = ring[:, s, :ng * 512].rearrange("p (k c) -> p k c", k=ng)
                for dd in range(4):
                    d = q * 4 + dd
                    bk = next_banks(nb)
                    mms = []
                    for hc in range(ng):
                        for bi, (t0, n) in enumerate(blocks):
                            mms.append((ps[bk[bi]][:, :n], W2[:, hc, dd * 128:(dd + 1) * 128],
                                        HID[:, hc, t0 - tb:t0 - tb + n], hc == 0, hc == ng - 1))
                    pe_task(mms, [("w", s)] + [("hid", hc, bi) for hc in range(ng) for bi in range(nb)], bk)
                    for bi, (t0, n) in enumerate(blocks):
                        resid_evac(d, t0, n, bk[bi], 0.5)

    def gmlp(gi):
        barrier()
        Hb = pv(0, BF16, [KC, 384])
        Vtm = pv(12288, BF16, [3, D])
        Gb = pv(24576, BF16, [KC, 384])
        LNG = pv(36864, F32, [D])
        LNB = pv(45056, F32, [D])
        WsT = pv(53248, BF16, [8, 128])
        BSH = pv(55296, BF16, [8, 384])
        BSL = pv(61440, BF16, [8, 384])
        LNS = pv(67584, F32, [2, 512])
        WsF = pv(0, F32, [8, 128])
        TrF = pv(4096, F32, [128])
        BsF = pv(8192, F32, [8, 128])
        t1 = dma(SP, LNG, lng_d.partition_broadcast(128), ld_sem, ld_cnt, 0, writes=[("lng",)])
        t2 = dma(SP, LNB, lnb_d.partition_broadcast(128), ld_sem, ld_cnt, 0, writes=[("lnb",)])
        t3 = dma(SP, WsF.rearrange("p a b -> p (a b)"), wsT_d, ld_sem, ld_cnt, 0, writes=[("wsf",)])
        t4 = dma(SP, TrF, trilT_d, ld_sem, ld_cnt, 0, writes=[("trf",)])
        t5 = dma(SP, BsF[0:1].rearrange("p a b -> p (a b)"), bs_d, ld_sem, ld_cnt, 0, writes=[("bsf",)])
        for k in (("lng",), ("lnb",), ("wsf",), ("trf",), ("bsf",)):
            TR.w[k] = t5
        job(DVE, [tt(WsT[:, g, :], WsF[:, g, :], TrF, ALU.mult) for g in range(8)],
            reads=[("wsf",), ("trf",)], writes=[("wst",)])
        job(DVE, [lambda e: e.memset(BSH.rearrange("p a b -> p (a b)"), 0.0),
                  lambda e: e.memset(BSL.rearrange("p a b -> p (a b)"), 0.0)], writes=[("bsh",), ("bsl",)])
        job(DVE, [(lambda e, g=g, r=r: e.tensor_copy(out=BSH[0:1, g, r * 128:(r + 1) * 128], in_=BsF[0:1, g, :]))
                  for g in range(8) for r in range(3)], reads=[("bsf",)], writes=[("bsh",)])
        job(DVE, [tt(BSL[0:1, g, r * 128:(r + 1) * 128], BsF[0:1, g, :], BSH[0:1, g, r * 128:(r + 1) * 128], ALU.subtract)
                  for g in range(8) for r in range(3)], reads=[("bsf",), ("bsh",)], writes=[("bsl",)])
        barrier()
        for b, (t0, n) in enumerate(BL3):
            norm_block(gi, t0, n, Hb, ("h", 0))
            for fb in range(4):
                s = load_piece(gv_d[fb], SLOT)
                W = ring[:, s, :].rearrange("p (k c) -> p k c", k=KC)
                for ch in range(3):
                    bk = next_banks(1)
                    mms = [(ps[bk[0]][:, :], Hb[:, kc, ch * 128:(ch + 1) * 128], W[:, kc, :], kc == 0, kc == KC - 1)
                           for kc in range(KC)]
                    pe_task(mms, [("w", s), ("h", 0)], bk)
                    col = ch * 4 + fb
                    job(ACT, [act_fn(Vtm[:, ch, fb * 512:(fb + 1) * 512], ps[bk[0]][:, :], AF.Gelu,
                                     accum_out=lnst[:, col:col + 1])],
                        reads=[("ps", bk[0])], writes=[("v", ch, fb), ("s1", col)])
                    s_ = sq_next[0]
                    sq_next[0] = (s_ + 1) % 4
                    job(ACT, [act_fn(sq[:, s_, :], Vtm[:, ch, fb * 512:(fb + 1) * 512], AF.Square,
                                     accum_out=lnst[:, 12 + col:13 + col])],
                        reads=[("v", ch, fb)], writes=[("sq", s_), ("s2", col)])
            S1 = lnst[:, 0:12].rearrange("p (a b) -> p a b", a=3)
            S2 = lnst[:, 12:24].rearrange("p (a b) -> p a b", a=3)
            m1, m2, msq, var, sd, rstd, nmr = (lnst[:, 24 + 3 * i:27 + 3 * i] for i in range(7))
            allv = [("v", ch, fb) for ch in range(3) for fb in range(4)]
            job(DVE, [lambda e: e.tensor_reduce(out=m1, in_=S1, axis=AX.X, op=ALU.add),
                      lambda e: e.tensor_reduce(out=m2, in_=S2, axis=AX.X, op=ALU.add)],
                reads=[("s1", c_) for c_ in range(12)] + [("s2", c_) for c_ in range(12)], writes=[("m12",)])
            job(DVE, [ts(m1, m1, 1.0 / D, None, ALU.mult), ts(m2, m2, 1.0 / D, None, ALU.mult)],
                reads=[("m12",)], writes=[("m12",)])
            job(DVE, [tt(msq, m1, m1, ALU.mult)], reads=[("m12",)], writes=[("msq",)])
            job(DVE, [tt(var, m2, msq, ALU.subtract)], reads=[("m12",), ("msq",)], writes=[("var",)])
            job(ACT, [act_fn(sd, var, AF.Sqrt, bias=eps5[:, 0:1], scale=1.0)], reads=[("var",), ("eps",)], writes=[("sd",)])
            job(DVE, [lambda e: e.reciprocal(out=rstd, in_=sd)], reads=[("sd",)], writes=[("rstd",)])
            job(DVE, [stt(nmr, m1, -1.0, rstd, ALU.mult, ALU.mult)], reads=[("m12",), ("rstd",)], writes=[("nmr",)])
            for ch in range(3):
                for fb in range(4):
                    sl = (ch * 4 + fb) % 2
                    vs = Vtm[:, ch, fb * 512:(fb + 1) * 512]
                    job(DVE, [ts(LNS[:, sl, :], vs, rstd[:, ch:ch + 1], nmr[:, ch:ch + 1], ALU.mult, ALU.add)],
                        reads=[("v", ch, fb), ("rstd",), ("nmr",)], writes=[("lns", sl)])
                    job(DVE, [tt(LNS[:, sl, :], LNS[:, sl, :], LNG[:, fb * 512:(fb + 1) * 512], ALU.mult)],
                        reads=[("lns", sl), ("lng",)], writes=[("lns", sl)])
                    job(DVE, [tt(vs, LNS[:, sl, :], LNB[:, fb * 512:(fb + 1) * 512], ALU.add)],
                        reads=[("lns", sl), ("lnb",)], writes=[("v", ch, fb)])
            if dbgopts.get("gmlp") == "ln":
                dump("d_vtm", Vtm, BF16)
                dump("d_lnst", lnst[:, :], F32)
                dump("d_hb", Hb, BF16)
                return
            for pk in range(4):
                s = load_piece(gu_d[pk], SLOT)
                W = ring[:, s, :].rearrange("p (k c) -> p k c", k=KC)
                for ff in range(4):
                    fc = pk * 4 + ff
                    g = fc // 2
                    ub = next_banks(1)
                    mms = [(ps[ub[0]][:, :n], W[:, kc, ff * 128:(ff + 1) * 128], Hb[:, kc, :], kc == 0, kc == KC - 1)
                           for kc in range(KC)]
                    pe_task(mms, [("w", s), ("h", 0)], ub)
                    fbk = next_banks(1)
                    mms = [(ps[fbk[0]][:, :n], e0[:, :], BSH[:, g, :], True, False),
                           (ps[fbk[0]][:, :n], e0[:, :], BSL[:, g, :], False, False)]
                    for ch in range(3):
                        mms.append((ps[fbk[0]][:, ch * 128:(ch + 1) * 128], Vtm[:, ch, fc * 128:(fc + 1) * 128],
                                    WsT[:, g, :], False, ch == 2))
                    pe_task(mms, [("e0",), ("bsh",), ("bsl",), ("wst",)] + [("v", ch, fc // 4) for ch in range(3)], fbk)
                    a, aT = act_slot(n)
                    job(ACT, [act_fn(aT, ps[ub[0]][:, :n], AF.Gelu)], reads=[("ps", ub[0])], writes=[("act", a)])
                    job(DVE, [tt(Gb[:, fc, :], ps[fbk[0]][:, :n], aT, ALU.mult)],
                        reads=[("ps", fbk[0]), ("act", a)], writes=[("g", fc)])
            if dbgopts.get("gmlp") == "gate":
                dump("d_vtm", Vtm, BF16)
                dump("d_gb", Gb, BF16)
                dump("d_bsh", BSH, BF16)
                dump("d_bsl", BSL, BF16)
                dump("d_wst", WsT, BF16)
                return
            for q in range(4):
                s = load_piece(go_d[q], SLOT)
                W = ring[:, s, :].rearrange("p (k c) -> p k c", k=KC)
                for dd in range(4):
                    d = q * 4 + dd
                    bk = next_banks(1)
                    mms = [(ps[bk[0]][:, :n], W[:, kc, dd * 128:(dd + 1) * 128], Gb[:, kc, :], kc == 0, kc == KC - 1)
                           for kc in range(KC)]
                    pe_task(mms, [("w", s)] + [("g", fc) for fc in range(KC)], bk)
                    resid_evac(d, t0, n, bk[0], 1.0)

    def xattn(l, gi, gmi, blocks):
        barrier()
        nmax = max(n for _, n in blocks)
        Hb = pv(0, BF16, [KC, 512])
        Qb = pv(16384, BF16, [KC, 512])
        Ob = pv(32768, BF16, [KC, 512])
        KT = pv(49152, BF16, [KC, MEM])
        VT = pv(57344, BF16, [2, D])
        PT = pv(65536, BF16, [2, 2, 512])
        RS = pv(69632, F32, [512])
        MT = pv(0, F32, [KC, MEM])
        MN = pv(16384, BF16, [KC, MEM])
        tm = dma(SP, MT, memT_d, ld_sem, ld_cnt, 0, writes=[("mt",)])
        sbk = 6 + stat_next[0]
        stat_next[0] ^= 1
        for c in range(KC):
            s = sq_next[0]
            sq_next[0] = (s + 1) % 4
            job(ACT, [act_fn(sq[:, s, :MEM], MT[:, c, :], AF.Square)], reads=[("mt",)], writes=[("sq", s)])
            job(PE, [mm(ps[sbk][:, :MEM], ones[:, :], sq[:, s, :MEM], c == 0, c == KC - 1)],
                reads=[("sq", s), ("ones",)], writes=[("ps", sbk)])
        sl = std_next[0]
        std_next[0] ^= 1
        job(ACT, [act_fn(std[:, sl, :MEM], ps[sbk][:, :MEM], AF.Sqrt, bias=eps6[:, 0:1], scale=1.0 / D)],
            reads=[("ps", sbk), ("eps",)], writes=[("std", sl)])
        job(DVE, [lambda e: e.reciprocal(out=std[:, sl, :MEM], in_=std[:, sl, :MEM])], reads=[("std", sl)], writes=[("std", sl)])
        job(DVE, [stt(MN[:, c, :], MT[:, c, :], gains[:, gmi, c:c + 1], std[:, sl, :MEM], ALU.mult, ALU.mult)
                  for c in range(KC)], reads=[("std", sl), ("mt",), ("gains",)], writes=[("mn",)])
        for pk in range(4):
            s = load_piece(wk_d[l][pk], SLOT)
            W = ring[:, s, :].rearrange("p (k c) -> p k c", k=KC)
            for ff in range(4):
                fc = pk * 4 + ff
                bk = next_banks(1)
                mms = [(ps[bk[0]][:, :MEM], W[:, kc, ff * 128:(ff + 1) * 128], MN[:, kc, :], kc == 0, kc == KC - 1)
                       for kc in range(KC)]
                pe_task(mms, [("w", s), ("mn",)], bk)
                job(ACT, [act_fn(KT[:, fc, :], ps[bk[0]][:, :MEM], AF.Copy)], reads=[("ps", bk[0])], writes=[("kt", fc)])
        for fb in range(4):
            s = load_piece(wv_d[l][fb], SLOT)
            W = ring[:, s, :].rearrange("p (k c) -> p k c", k=KC)
            for mc in range(2):
                bk = next_banks(1)
                mms = [(ps[bk[0]][:, :], MN[:, kc, mc * 128:(mc + 1) * 128], W[:, kc, :], kc == 0, kc == KC - 1)
                       for kc in range(KC)]
                pe_task(mms, [("w", s), ("mn",)], bk)
                job(DVE, [lambda e, o=VT[:, mc, fb * 512:(fb + 1) * 512], i=ps[bk[0]][:, :]: e.tensor_copy(out=o, in_=i)],
                    reads=[("ps", bk[0])], writes=[("vt", mc, fb)])
        barrier()
        qscale = 512.0 ** -0.5
        pslot = [0]
        for b, (t0, n) in enumerate(blocks):
            norm_block(gi, t0, n, Hb[:, :, :n], ("h", 0))
            for pk in range(4):
                s = load_piece(wq_d[l][pk], SLOT)
                W = ring[:, s, :].rearrange("p (k c) -> p k c", k=KC)
                for ff in range(4):
                    fc = pk * 4 + ff
                    bk = next_banks(1)
                    mms = [(ps[bk[0]][:, :n], W[:, kc, ff * 128:(ff + 1) * 128], Hb[:, kc, :n], kc == 0, kc == KC - 1)
                           for kc in range(KC)]
                    pe_task(mms, [("w", s), ("h", 0)], bk)
                    job(ACT, [act_fn(Qb[:, fc, :n], ps[bk[0]][:, :n], AF.Copy, scale=qscale)],
                        reads=[("ps", bk[0])], writes=[("q", fc)])
            for hd in range(4):
                sp_ = pslot[0]
                pslot[0] ^= 1
                for mc in range(2):
                    bk = next_banks(1)
                    mms = [(ps[bk[0]][:, :n], KT[:, hd * 4 + dc, mc * 128:(mc + 1) * 128], Qb[:, hd * 4 + dc, :n],
                            dc == 0, dc == 3) for dc in range(4)]
                    pe_task(mms, [("kt", hd * 4 + dc) for dc in range(4)] + [("q", hd * 4 + dc) for dc in range(4)], bk)
                    job(ACT, [act_fn(PT[:, sp_, mc, :n], ps[bk[0]][:, :n], AF.Exp)],
                        reads=[("ps", bk[0])], writes=[("pt", sp_, mc)])
                bk = next_banks(1)
                mms = [(ps[bk[0]][:, :n], ones[:, :], PT[:, sp_, mc, :n], mc == 0, mc == 1) for mc in range(2)]
                pe_task(mms, [("ones",), ("pt", sp_, 0), ("pt", sp_, 1)], bk)
                job(DVE, [lambda e, o=RS[:, :n], i=ps[bk[0]][:, :n]: e.reciprocal(out=o, in_=i)],
                    reads=[("ps", bk[0])], writes=[("rs",)])
                for dc in range(4):
                    fc = hd * 4 + dc
                    bk = next_banks(1)
                    mms = [(ps[bk[0]][:, :n], VT[:, mc, fc * 128:(fc + 1) * 128], PT[:, sp_, mc, :n], mc == 0, mc == 1)
                           for mc in range(2)]
                    pe_task(mms, [("vt", mc, fc // 4) for mc in range(2)] + [("pt", sp_, 0), ("pt", sp_, 1)], bk)
                    job(DVE, [tt(Ob[:, fc, :n], ps[bk[0]][:, :n], RS[:, :n], ALU.mult)],
                        reads=[("ps", bk[0]), ("rs",)], writes=[("o", fc)])
            proj_out(wo_d[l], [(t0, n)], t0, Ob, [("o", fc) for fc in range(KC)])
        return

    def conv(gi):
        barrier()
        tb = 126
        H = pv(0, BF16, [KC, 1028])
        G = pv(32896, BF16, [KC, 1024])
        ACC = pv(65664, F32, [1024])
        Z = actT
        for bi, (t0, n) in enumerate(CB):
            norm_block(gi, t0, n, H[:, :, t0 - tb:t0 - tb + n], ("h", bi))
        hk = [("h", bi) for bi in range(3)]
        for fc in range(KC):
            s = load_piece(cin_d[fc], KC * 384)
            W = ring[:, s, :KC * 384].rearrange("p (k c) -> p k c", k=KC)
            cb_ = next_banks(3)
            vb_ = next_banks(3)
            for banks_, col0 in ((cb_, 128), (vb_, 256)):
                mms = []
                for kc in range(KC):
                    for bi, (t0, n) in enumerate(CB):
                        mms.append((ps[banks_[bi]][:, :n], W[:, kc, col0:col0 + 128], H[:, kc, t0 - tb:t0 - tb + n],
                                    kc == 0, kc == KC - 1))
                pe_task(mms, [("w", s)] + hk, banks_)
            for bi, (t0, n) in enumerate(CB):
                job(ACT, [act_fn(Z[:, t0 - tb:t0 - tb + n], ps[cb_[bi]][:, :n], AF.Copy)],
                    reads=[("ps", cb_[bi])], writes=[("z", bi)])
                job(DVE, [tt(Z[:, t0 - tb:t0 - tb + n], ps[vb_[bi]][:, :n], Z[:, t0 - tb:t0 - tb + n], ALU.mult)],
                    reads=[("ps", vb_[bi]), ("z", bi)], writes=[("z", bi)])
            job(DVE, [ts(Z[:, 0:2], Z[:, 0:2], hmask[:, 0:1], None, ALU.mult)], reads=[("z", 0), ("hmask",)], writes=[("z", 0)])
            zk = [("z", bi) for bi in range(3)]
            job(ACT, [act_fn(ACC, Z[:, 2:1026], AF.Copy, scale=cw[:, fc, 2:3])], reads=zk + [("cw",)], writes=[("acc",)])
            job(DVE, [stt(ACC, Z[:, 1:1025], cw[:, fc, 1:2], ACC, ALU.mult, ALU.add)], reads=zk + [("acc",), ("cw",)], writes=[("acc",)])
            job(DVE, [stt(ACC, Z[:, 0:1024], cw[:, fc, 0:1], ACC, ALU.mult, ALU.add)], reads=zk + [("acc",), ("cw",)], writes=[("acc",)])
            bb_ = next_banks(len(BL2))
            mms = []
            for kc in range(KC):
                for bi, (t0, n) in enumerate(BL2):
                    mms.append((ps[bb_[bi]][:, :n], W[:, kc, 0:128], H[:, kc, t0 - tb:t0 - tb + n], kc == 0, kc == KC - 1))
            pe_task(mms, [("w", s)] + hk, bb_)
            for bi, (t0, n) in enumerate(BL2):
                job(DVE, [tt(G[:, fc, t0 - 128:t0 - 128 + n], ps[bb_[bi]][:, :n], ACC[:, t0 - 128:t0 - 128 + n], ALU.mult)],
                    reads=[("ps", bb_[bi]), ("acc",)], writes=[("g", fc, bi)])
        for q in range(4):
            s = load_piece(co_d[q], SLOT)
            W = ring[:, s, :].rearrange("p (k c) -> p k c", k=KC)
            for dd in range(4):
                d = q * 4 + dd
                bk = next_banks(len(BL2))
                mms = []
                for kc in range(KC):
                    for bi, (t0, n) in enumerate(BL2):
                        mms.append((ps[bk[bi]][:, :n], W[:, kc, dd * 128:(dd + 1) * 128], G[:, kc, t0 - 128:t0 - 128 + n],
                                    kc == 0, kc == KC - 1))
                pe_task(mms, [("w", s)] + [("g", fc, bi) for fc in range(KC) for bi in range(len(BL2))], bk)
                for bi, (t0, n) in enumerate(BL2):
                    resid_evac(d, t0, n, bk[bi], 1.0)

    def final():
        barrier()
        for bi, (t0, n) in enumerate(BL2):
            Yb = pv((t0 - 128) * KC * 4, F32, [KC, n])
            norm_block(10, t0, n, Yb, ("y", bi))
            dma(SP, out_d[:, :, t0 - 128:t0 - 128 + n], Yb, st_sem, st_cnt, 0, reads=[("y", bi)])

    stages = [
        lambda: ffn(0, 0, 0, BL3, 0),
        lambda: gmlp(1),
        lambda: xattn(0, 2, 3, BX),
        lambda: ffn(0, 1, 4, BX, 126),
        lambda: ffn(1, 0, 5, BX, 126),
        lambda: conv(6),
        lambda: xattn(1, 7, 8, XB1),
        lambda: ffn(1, 1, 9, BY, 128),
        lambda: final(),
    ]
    for st in stages[:nstages]:
        st()
    if dbg:
        barrier()
        allx = [("x", c, ch) for c in range(KC) for ch in range(9)]
        dma(SP, out_d, xT[:, :, :], st_sem, st_cnt, 0, reads=allx)
    SP.ops.append(("wait", st_sem, st_cnt[0]))

    with nc.Block() as block:
        @block.tensor
        def _(e):
            PE.replay(e)

        @block.scalar
        def _(e):
            ACT.replay(e)

        @block.vector
        def _(e):
            DVE.replay(e)

        @block.gpsimd
        def _(e):
            POOL.replay(e)

        @block.sync
        def _(e):
            SP.replay(e)
    es.close()
    return nc


def _pieces(W, ncols):
    K, C = W.shape
    npc = C // ncols
    return np.ascontiguousarray(W.reshape(K // 128, 128, npc, ncols).transpose(2, 1, 0, 3)).reshape(npc, 128, (K // 128) * ncols)


def _prep(inputs):
    f = lambda a: np.asarray(a, dtype=np.float32)
    x = f(inputs["x"])[0]
    mem = f(inputs["mem"])[0]
    shared = {}
    shared["memT"] = np.ascontiguousarray(mem.T)
    gl = []
    for l in range(2):
        for nm in ("ffn1_norm", "mix_norm", "xattn_norm", "mem_norm", "ffn2_norm"):
            gl.append(f(inputs[nm])[l])
    gl.append(f(inputs["final_norm"]))
    g = np.stack(gl, 0)
    shared["gains"] = np.ascontiguousarray(g.reshape(11, 16, 128).transpose(2, 0, 1)).reshape(128, 11 * 16)
    for l in range(2):
        for fi, nm in enumerate(("ffn1", "ffn2")):
            w13 = f(inputs[nm + "_w13"])[l]
            gate = w13[:, :5632].reshape(16, 128, 22, 2, 1, 128)
            up = w13[:, 5632:].reshape(16, 128, 22, 2, 1, 128)
            cat = np.concatenate([gate, up], axis=4)
            shared[f"w13_{l}_{fi}"] = np.ascontiguousarray(cat.transpose(2, 1, 0, 3, 4, 5)).reshape(22, 128, SLOT)
            w2 = f(inputs[nm + "_w2"])[l]
            arr = np.zeros((16, 128, 12 * 512), np.float32)
            for gi_, (j0, ng) in enumerate(GROUPS):
                blk = w2[j0 * 128:(j0 + ng) * 128].reshape(ng, 128, 4, 512)
                arr[gi_ * 4:gi_ * 4 + 4, :, :ng * 512] = blk.transpose(2, 1, 0, 3).reshape(4, 128, ng * 512)
            shared[f"w2_{l}_{fi}"] = arr
        shared[f"wq_{l}"] = _pieces(f(inputs["xattn_wq"])[l], 512)
        wkv = f(inputs["xattn_wkv"])[l]
        shared[f"wk_{l}"] = _pieces(wkv[:, :D], 512)
        shared[f"wv_{l}"] = _pieces(wkv[:, D:], 512)
        shared[f"wo_{l}"] = _pieces(f(inputs["xattn_wo"])[l], 512)
    win = f(inputs["gmlp_w_in"])[0]
    shared["gmlp_u"] = _pieces(win[:, :D], 512)
    shared["gmlp_v"] = _pieces(win[:, D:], 512)
    shared["gmlp_o"] = _pieces(f(inputs["gmlp_w_out"])[0], 512)
    shared["ln_g"] = f(inputs["gmlp_ln_g"])[0].reshape(1, D)
    shared["ln_b"] = f(inputs["gmlp_ln_b"])[0].reshape(1, D)
    ws = f(inputs["gmlp_w_s"])[0]
    shared["wsT"] = np.ascontiguousarray(ws.transpose(2, 0, 1)).reshape(128, 8 * 128)
    shared["trilT"] = np.ascontiguousarray(np.tril(np.ones((128, 128), np.float32)).T)
    shared["b_s"] = f(inputs["gmlp_b_s"])[0].reshape(1, 8 * 128)
    cin = f(inputs["conv_w_in"])[0]
    c3 = cin.reshape(16, 128, 3, 16, 128)
    shared["conv_in"] = np.ascontiguousarray(c3.transpose(3, 1, 0, 2, 4)).reshape(16, 128, 16 * 384)
    cwv = f(inputs["conv_w"])[0]
    shared["conv_w"] = np.ascontiguousarray(cwv.reshape(3, 16, 128).transpose(2, 1, 0)).reshape(128, 48)
    shared["conv_o"] = _pieces(f(inputs["conv_w_out"])[0], 512)
    in_maps = []
    for i in range(NCORES):
        xs = np.zeros((NT, D), np.float32)
        if i > 0:
            xs[:] = x[i * TOWN - HALO:(i + 1) * TOWN]
        else:
            xs[HALO:] = x[:TOWN]
        m = dict(shared)
        m["xT"] = np.ascontiguousarray(xs.T)
        m["hmask"] = np.full((128, 2), 1.0 if i > 0 else 0.0, np.float32)
        in_maps.append(m)
    return in_maps


_NC_CACHE = {}


def kernel(**inputs):
    in_maps = _prep(inputs)
    if "nc" not in _NC_CACHE:
        _NC_CACHE["nc"] = build()
    nc = _NC_CACHE["nc"]
    res = run_bass_kernel_spmd(nc, in_maps, core_ids=list(range(NCORES)))
    outs = [np.asarray(r["outT"]).T for r in res.results]
    return np.concatenate(outs, axis=0).reshape(1, NCORES * TOWN, D).astype(np.float32)
```

```python
import contextlib
import numpy as np
import concourse.bass as bass
import concourse.mybir as mybir
from concourse.bass_utils import run_bass_kernel_spmd

F32 = mybir.dt.float32
BF16 = mybir.dt.bfloat16
AF = mybir.ActivationFunctionType
ALU = mybir.AluOpType
AX = mybir.AxisListType

D = 2048
KC = 16
HCH = 44
MEM = 256
NCORES = 8
TOWN = 1024
HALO = 128
NT = TOWN + HALO
SLOT = 8192
NSLOT = 3
GROUPS = [(0, 12), (12, 12), (24, 12), (36, 8)]
BL3 = [(0, 384), (384, 384), (768, 384)]
BX = [(126, 386), (512, 384), (896, 256)]
BY = [(128, 384), (512, 384), (896, 256)]
XB1 = [(128, 512), (640, 512)]
BL2 = BY
CB = BX
NSTAGES_ALL = 9


class Tok:
    __slots__ = ("sem", "val")

    def __init__(self, sem, val):
        self.sem, self.val = sem, val


class Eng:
    def __init__(self, name, sem):
        self.name, self.sem, self.cnt, self.ops, self.seen = name, sem, 0, [], {}

    def wait(self, toks, skip_own=False):
        for t in toks:
            if t is None:
                continue
            if skip_own and t.sem is self.sem:
                continue
            k = id(t.sem)
            if self.seen.get(k, 0) >= t.val:
                continue
            self.seen[k] = t.val
            self.ops.append(("wait", t.sem, t.val))

    def emit(self, fn, signal=False):
        if signal:
            self.cnt += 1
            self.ops.append(("sig", fn))
            return Tok(self.sem, self.cnt)
        self.ops.append(("op", fn))
        return None

    def replay(self, e):
        for op in self.ops:
            if op[0] == "wait":
                e.wait_ge(op[1], op[2])
            elif op[0] == "op":
                op[1](e)
            elif op[0] == "sig":
                op[1](e).then_inc(self.sem, 1)
            elif op[0] == "dma":
                kw = op[4]
                e.dma_start(out=op[1], in_=op[2], **kw).then_inc(op[3], 16)


class Tracker:
    def __init__(self):
        self.w, self.r = {}, {}

    def deps_read(self, keys):
        return [self.w.get(k) for k in keys]

    def deps_write(self, keys):
        out = []
        for k in keys:
            out.append(self.w.get(k))
            out.extend(self.r.get(k, {}).values())
        return out

    def did_read(self, keys, tok):
        for k in keys:
            d = self.r.setdefault(k, {})
            cur = d.get(id(tok.sem))
            if cur is None or cur.val < tok.val:
                d[id(tok.sem)] = tok

    def did_write(self, keys, tok):
        for k in keys:
            self.w[k] = tok
            self.r[k] = {}


def build(nstages=NSTAGES_ALL, dbg=False, dbgopts=None):
    dbgopts = dbgopts or {}
    nc = bass.Bass("TRN2", target_bir_lowering=False)
    es = contextlib.ExitStack()

    def din(name, shape):
        return nc.dram_tensor(name, list(shape), F32, kind="ExternalInput").ap()

    xT_d = din("xT", [D, NT]).rearrange("(c p) t -> p c t", p=128)
    hmask_d = din("hmask", [128, 2])
    memT_d = din("memT", [D, MEM]).rearrange("(c p) t -> p c t", p=128)
    gains_d = din("gains", [128, 11 * 16])
    w13_d = {(l, f): din(f"w13_{l}_{f}", [22, 128, SLOT]) for l in range(2) for f in range(2)}
    w2_d = {(l, f): din(f"w2_{l}_{f}", [16, 128, 12 * 512]) for l in range(2) for f in range(2)}
    gv_d = din("gmlp_v", [4, 128, SLOT])
    gu_d = din("gmlp_u", [4, 128, SLOT])
    go_d = din("gmlp_o", [4, 128, SLOT])
    lng_d = din("ln_g", [1, D])
    lnb_d = din("ln_b", [1, D])
    wsT_d = din("wsT", [128, 8 * 128])
    trilT_d = din("trilT", [128, 128])
    bs_d = din("b_s", [1, 8 * 128])
    cin_d = din("conv_in", [16, 128, 16 * 384])
    cw_d = din("conv_w", [128, 16 * 3])
    co_d = din("conv_o", [4, 128, SLOT])
    wq_d = [din(f"wq_{l}", [4, 128, SLOT]) for l in range(2)]
    wk_d = [din(f"wk_{l}", [4, 128, SLOT]) for l in range(2)]
    wv_d = [din(f"wv_{l}", [4, 128, SLOT]) for l in range(2)]
    wo_d = [din(f"wo_{l}", [4, 128, SLOT]) for l in range(2)]
    if dbg:
        out_d = nc.dram_tensor("dbg", [D, NT], F32, kind="ExternalOutput").ap().rearrange("(c p) t -> p c t", p=128)
    else:
        out_d = nc.dram_tensor("outT", [D, TOWN], F32, kind="ExternalOutput").ap().rearrange("(c p) t -> p c t", p=128)

    def sb(name, shape, dt):
        return es.enter_context(nc.sbuf_tensor(name, list(shape), dt))

    xT = sb("xT_sb", [128, KC, NT], F32)
    P = sb("P_sb", [128, 36864], BF16)
    ring = sb("ring_sb", [128, NSLOT, SLOT], BF16)
    sq = sb("sq_sb", [128, 4, 512], BF16)
    std = sb("std_sb", [128, 2, 512], F32)
    actT = sb("act_sb", [128, 1160], F32)
    ones = sb("ones_sb", [128, 128], BF16)
    e0 = sb("e0_sb", [128, 128], BF16)
    gains = sb("gains_sb", [128, 11, 16], F32)
    eps6 = sb("eps6_sb", [128, 1], F32)
    eps5 = sb("eps5_sb", [128, 1], F32)
    cw = sb("cw_sb", [128, 16, 3], F32)
    hmask = sb("hmask_sb", [128, 2], F32)
    lnst = sb("lnst_sb", [128, 64], F32)

    ps = [es.enter_context(nc.psum_tensor(f"ps{i}", [128, 512], F32)) for i in range(8)]

    def pv(off, dt, shape):
        n = int(np.prod(shape))
        if dt is BF16:
            assert off % 2 == 0
            ap = P[:, off // 2: off // 2 + n]
        else:
            assert off % 4 == 0
            ap = P[:, off // 2: off // 2 + 2 * n].bitcast(F32)
        assert off + n * (2 if dt is BF16 else 4) <= 36864 * 2
        if len(shape) == 2:
            ap = ap.rearrange("p (a b) -> p a b", a=shape[0])
        elif len(shape) == 3:
            ap = ap.rearrange("p (a b c) -> p a b c", a=shape[0], b=shape[1])
        return ap

    def sem(name):
        return es.enter_context(nc.semaphore(name))

    PE = Eng("pe", sem("s_pe"))
    ACT = Eng("act", sem("s_act"))
    DVE = Eng("dve", sem("s_dve"))
    POOL = Eng("pool", sem("s_pool"))
    SP = Eng("sp", sem("s_sp"))
    slot_sems = [sem(f"s_slot{i}") for i in range(NSLOT)]
    slot_cnt = [0] * NSLOT
    ld_sem = sem("s_ld")
    ld_cnt = [0]
    st_sem = sem("s_st")
    st_cnt = [0]
    TR = Tracker()

    def job(eng, fns, reads=(), writes=()):
        reads, writes = list(reads), list(writes)
        eng.wait(TR.deps_read(reads) + TR.deps_write(writes), skip_own=(eng is PE))
        for fn in fns[:-1]:
            eng.emit(fn)
        tok = eng.emit(fns[-1], signal=True)
        TR.did_read(reads, tok)
        TR.did_write(writes, tok)
        return tok

    def dma(q, out_ap, in_ap, semh, cnt, idx, reads=(), writes=(), **kw):
        reads, writes = list(reads), list(writes)
        q.wait(TR.deps_read(reads) + TR.deps_write(writes))
        cnt[idx] += 16
        q.ops.append(("dma", out_ap, in_ap, semh, kw))
        tok = Tok(semh, cnt[idx])
        TR.did_read(reads, tok)
        TR.did_write(writes, tok)
        return tok

    def barrier():
        toks = [Tok(e.sem, e.cnt) for e in (PE, ACT, DVE) if e.cnt > 0]
        for e in (PE, ACT, DVE, SP):
            e.wait(toks)

    def dump(name, ap, dt):
        barrier()
        shp = list(ap.shape)
        dd = nc.dram_tensor(name, shp, dt, kind="ExternalOutput").ap()
        SP.wait([Tok(e.sem, e.cnt) for e in (PE, ACT, DVE) if e.cnt > 0])
        st_cnt[0] += 16
        SP.ops.append(("dma", dd, ap, st_sem, {}))

    slot_next = [0]

    def load_piece(src_ap, nel):
        s = slot_next[0]
        slot_next[0] = (s + 1) % NSLOT
        nb = nel // 2048
        dma(POOL, ring[:, s, :nel].rearrange("p (a b) -> p a b", a=nb),
            src_ap.rearrange("p (a b) -> p a b", a=nb),
            slot_sems[s], slot_cnt, s, writes=[("w", s)])
        return s

    bank_next = [0]

    def next_banks(n):
        out = []
        for _ in range(n):
            out.append(bank_next[0])
            bank_next[0] = (bank_next[0] + 1) % 6
        return out

    stat_next = [0]
    sq_next = [0]
    std_next = [0]

    def xkeys(c, t0, n):
        return [("x", c, ch) for ch in range(t0 // 128, (t0 + n - 1) // 128 + 1)]

    def mm(o, l, r, st, sp):
        return lambda e: e.matmul(o, l, r, start=st, stop=sp)

    def pe_task(mms, reads, banks):
        return job(PE, [mm(*m) for m in mms], reads=reads, writes=[("ps", b) for b in banks])

    def act_fn(out, in_, func, **kw):
        return lambda e: e.activation(out=out, in_=in_, func=func, **kw)

    def tt(out, in0, in1, op):
        return lambda e: e.tensor_tensor(out=out, in0=in0, in1=in1, op=op)

    def stt(out, in0, scalar, in1, op0, op1):
        return lambda e: e.scalar_tensor_tensor(out=out, in0=in0, scalar=scalar, in1=in1, op0=op0, op1=op1)

    def ts(out, in0, s1, s2, op0, op1=None):
        if op1 is None:
            return lambda e: e.tensor_scalar(out=out, in0=in0, scalar1=s1, scalar2=None, op0=op0)
        return lambda e: e.tensor_scalar(out=out, in0=in0, scalar1=s1, scalar2=s2, op0=op0, op1=op1)

    xl_sems = [sem(f"s_xl{i}") for i in range(3)]
    xl_cnt = [0, 0, 0]
    for i, (t0, n) in enumerate(BL3):
        dma(SP, xT[:, :, t0:t0 + n], xT_d[:, :, t0:t0 + n], xl_sems[i], xl_cnt, i,
            writes=[("x", c, ch) for c in range(KC) for ch in range(3 * i, 3 * i + 3)])
    xkeep = dict(TR.w)
    dma(SP, gains[:, :, :].rearrange("p a b -> p (a b)"), gains_d, ld_sem, ld_cnt, 0, writes=[("gains",)])
    dma(SP, cw[:, :, :].rearrange("p a b -> p (a b)"), cw_d, ld_sem, ld_cnt, 0, writes=[("cw",)])
    tok_ld = dma(SP, hmask[:, :], hmask_d, ld_sem, ld_cnt, 0, writes=[("hmask",)])
    for k in list(TR.w.keys()):
        if k not in xkeep:
            TR.w[k] = tok_ld
    job(DVE, [lambda e: e.memset(ones[:, :], 1.0)], writes=[("ones",)])
    job(DVE, [lambda e: e.memset(e0[:, :], 0.0)], writes=[("e0",)])
    job(DVE, [lambda e: e.memset(e0[0:1, :], 1.0)], writes=[("e0",)])
    job(DVE, [lambda e: e.memset(eps6[:, :], 1e-6)], writes=[("eps",)])
    job(DVE, [lambda e: e.memset(eps5[:, :], 1e-5)], writes=[("eps",)])

    def norm_block(gi, t0, n, hdst, hkey, f32out=False):
        sbk = 6 + stat_next[0]
        stat_next[0] ^= 1
        for c in range(KC):
            s = sq_next[0]
            sq_next[0] = (s + 1) % 4
            job(ACT, [act_fn(sq[:, s, :n], xT[:, c, t0:t0 + n], AF.Square)],
                reads=xkeys(c, t0, n), writes=[("sq", s)])
            job(PE, [mm(ps[sbk][:, :n], ones[:, :], sq[:, s, :n], c == 0, c == KC - 1)],
                reads=[("sq", s), ("ones",)], writes=[("ps", sbk)])
        sl = std_next[0]
        std_next[0] ^= 1
        job(ACT, [act_fn(std[:, sl, :n], ps[sbk][:, :n], AF.Sqrt, bias=eps6[:, 0:1], scale=1.0 / D)],
            reads=[("ps", sbk), ("eps",)], writes=[("std", sl)])
        job(DVE, [lambda e: e.reciprocal(out=std[:, sl, :n], in_=std[:, sl, :n])],
            reads=[("std", sl)], writes=[("std", sl)])
        fns = [stt(hdst[:, c, :], xT[:, c, t0:t0 + n], gains[:, gi, c:c + 1], std[:, sl, :n], ALU.mult, ALU.mult)
               for c in range(KC)]
        rk = [("std", sl), ("gains",)]
        for c in range(KC):
            rk += xkeys(c, t0, n)
        job(DVE, fns, reads=rk, writes=[hkey])

    act_next = [0]

    def act_slot(n):
        if n <= 386:
            a = act_next[0] % 3
            act_next[0] = (a + 1) % 3
            return a, actT[:, a * 386: a * 386 + n]
        a = act_next[0] % 2
        act_next[0] = (a + 1) % 2
        return a, actT[:, a * 512: a * 512 + n]

    def resid_evac(d, t0, n, bank, scale):
        if scale == 1.0:
            fn = tt(xT[:, d, t0:t0 + n], ps[bank][:, :n], xT[:, d, t0:t0 + n], ALU.add)
        else:
            fn = stt(xT[:, d, t0:t0 + n], ps[bank][:, :n], scale, xT[:, d, t0:t0 + n], ALU.mult, ALU.add)
        job(DVE, [fn], reads=[("ps", bank)] + xkeys(d, t0, n), writes=xkeys(d, t0, n))

    def proj_out(w_d, blocks, tb, G, gkeys):
        for q in range(4):
            s = load_piece(w_d[q], SLOT)
            W = ring[:, s, :].rearrange("p (k c) -> p k c", k=KC)
            for dd in range(4):
                d = q * 4 + dd
                bk = next_banks(len(blocks))
                mms = []
                for kc in range(KC):
                    for bi, (t0, n) in enumerate(blocks):
                        mms.append((ps[bk[bi]][:, :n], W[:, kc, dd * 128:(dd + 1) * 128],
                                    G[:, kc, t0 - tb:t0 - tb + n], kc == 0, kc == KC - 1))
                pe_task(mms, [("w", s)] + list(gkeys), bk)
                for bi, (t0, n) in enumerate(blocks):
                    resid_evac(d, t0, n, bk[bi], 1.0)

    def ffn(l, f, gi, blocks, tb):
        ntl = sum(n for _, n in blocks)
        barrier()
        H = pv(0, BF16, [KC, ntl])
        HID = pv(KC * ntl * 2, BF16, [12, ntl])
        nb = len(blocks)
        for bi, (t0, n) in enumerate(blocks):
            norm_block(gi, t0, n, H[:, :, t0 - tb:t0 - tb + n], ("h", bi))
        hk = [("h", bi) for bi in range(nb)]
        for g, (j0, ng) in enumerate(GROUPS):
            for pk in range(j0 // 2, (j0 + ng) // 2):
                s = load_piece(w13_d[(l, f)][pk], SLOT)
                W = ring[:, s, :].rearrange("p (k c) -> p k c", k=KC)
                for jj in range(2):
                    jl = 2 * pk + jj - j0
                    gb = next_banks(nb)
                    ub = next_banks(nb)
                    for banks_, col0 in ((gb, jj * 256), (ub, jj * 256 + 128)):
                        for bi, (t0, n) in enumerate(blocks):
                            mms = [(ps[banks_[bi]][:, :n], W[:, kc, col0:col0 + 128],
                                    H[:, kc, t0 - tb:t0 - tb + n], kc == 0, kc == KC - 1) for kc in range(KC)]
                            pe_task(mms, [("w", s), ("h", bi)], [banks_[bi]])
                    for bi, (t0, n) in enumerate(blocks):
                        a, aT = act_slot(n)
                        job(ACT, [act_fn(aT, ps[gb[bi]][:, :n], AF.Silu)], reads=[("ps", gb[bi])], writes=[("act", a)])
                        job(DVE, [tt(HID[:, jl, t0 - tb:t0 - tb + n], ps[ub[bi]][:, :n], aT, ALU.mult)],
                            reads=[("ps", ub[bi]), ("act", a)], writes=[("hid", jl, bi)])
            for q in range(4):
                s = load_piece(w2_d[(l, f)][g * 4 + q][:, :ng * 512], ng * 512)
                W2 = ring[:, s, :ng * 512].rearrange("p (k c) -> p k c", k=ng)
                for dd in range(4):
                    d = q * 4 + dd
                    bk = next_banks(nb)
                    for bi, (t0, n) in enumerate(blocks):
                        mms = [(ps[bk[bi]][:, :n], W2[:, hc, dd * 128:(dd + 1) * 128],
                                HID[:, hc, t0 - tb:t0 - tb + n], hc == 0, hc == ng - 1) for hc in range(ng)]
                        pe_task(mms, [("w", s)] + [("hid", hc, bi) for hc in range(ng)], [bk[bi]])
                        resid_evac(d, t0, n, bk[bi], 0.5)

    def gmlp(gi):
        barrier()
        Hb = pv(0, BF16, [KC, 384])
        Vtm = pv(12288, BF16, [3, D])
        Gb = pv(24576, BF16, [KC, 384])
        LNG = pv(36864, F32, [D])
        LNB = pv(45056, F32, [D])
        WsT = pv(53248, BF16, [8, 128])
        BSH = pv(55296, BF16, [8, 384])
        BSL = pv(61440, BF16, [8, 384])
        LNS = pv(67584, F32, [2, 512])
        WsF = pv(0, F32, [8, 128])
        TrF = pv(4096, F32, [128])
        BsF = pv(8192, F32, [8, 128])
        t1 = dma(SP, LNG, lng_d.partition_broadcast(128), ld_sem, ld_cnt, 0, writes=[("lng",)])
        t2 = dma(SP, LNB, lnb_d.partition_broadcast(128), ld_sem, ld_cnt, 0, writes=[("lnb",)])
        t3 = dma(SP, WsF.rearrange("p a b -> p (a b)"), wsT_d, ld_sem, ld_cnt, 0, writes=[("wsf",)])
        t4 = dma(SP, TrF, trilT_d, ld_sem, ld_cnt, 0, writes=[("trf",)])
        t5 = dma(SP, BsF[0:1].rearrange("p a b -> p (a b)"), bs_d, ld_sem, ld_cnt, 0, writes=[("bsf",)])
        for k in (("lng",), ("lnb",), ("wsf",), ("trf",), ("bsf",)):
            TR.w[k] = t5
        job(DVE, [tt(WsT[:, g, :], WsF[:, g, :], TrF, ALU.mult) for g in range(8)],
            reads=[("wsf",), ("trf",)], writes=[("wst",)])
        job(DVE, [lambda e: e.memset(BSH.rearrange("p a b -> p (a b)"), 0.0),
                  lambda e: e.memset(BSL.rearrange("p a b -> p (a b)"), 0.0)], writes=[("bsh",), ("bsl",)])
        job(DVE, [(lambda e, g=g, r=r: e.tensor_copy(out=BSH[0:1, g, r * 128:(r + 1) * 128], in_=BsF[0:1, g, :]))
                  for g in range(8) for r in range(3)], reads=[("bsf",)], writes=[("bsh",)])
        job(DVE, [tt(BSL[0:1, g, r * 128:(r + 1) * 128], BsF[0:1, g, :], BSH[0:1, g, r * 128:(r + 1) * 128], ALU.subtract)
                  for g in range(8) for r in range(3)], reads=[("bsf",), ("bsh",)], writes=[("bsl",)])
        barrier()
        for b, (t0, n) in enumerate(BL3):
            norm_block(gi, t0, n, Hb, ("h", 0))
            for fb in range(4):
                s = load_piece(gv_d[fb], SLOT)
                W = ring[:, s, :].rearrange("p (k c) -> p k c", k=KC)
                for ch in range(3):
                    bk = next_banks(1)
                    mms = [(ps[bk[0]][:, :], Hb[:, kc, ch * 128:(ch + 1) * 128], W[:, kc, :], kc == 0, kc == KC - 1)
                           for kc in range(KC)]
                    pe_task(mms, [("w", s), ("h", 0)], bk)
                    col = ch * 4 + fb
                    job(ACT, [act_fn(Vtm[:, ch, fb * 512:(fb + 1) * 512], ps[bk[0]][:, :], AF.Gelu,
                                     accum_out=lnst[:, col:col + 1])],
                        reads=[("ps", bk[0])], writes=[("v", ch, fb), ("s1", col)])
                    s_ = sq_next[0]
                    sq_next[0] = (s_ + 1) % 4
                    job(ACT, [act_fn(sq[:, s_, :], Vtm[:, ch, fb * 512:(fb + 1) * 512], AF.Square,
                                     accum_out=lnst[:, 12 + col:13 + col])],
                        reads=[("v", ch, fb)], writes=[("sq", s_), ("s2", col)])
            S1 = lnst[:, 0:12].rearrange("p (a b) -> p a b", a=3)
            S2 = lnst[:, 12:24].rearrange("p (a b) -> p a b", a=3)
            m1, m2, msq, var, sd, rstd, nmr = (lnst[:, 24 + 3 * i:27 + 3 * i] for i in range(7))
            allv = [("v", ch, fb) for ch in range(3) for fb in range(4)]
            job(DVE, [lambda e: e.tensor_reduce(out=m1, in_=S1, axis=AX.X, op=ALU.add),
                      lambda e: e.tensor_reduce(out=m2, in_=S2, axis=AX.X, op=ALU.add)],
                reads=[("s1", c_) for c_ in range(12)] + [("s2", c_) for c_ in range(12)], writes=[("m12",)])
            job(DVE, [ts(m1, m1, 1.0 / D, None, ALU.mult), ts(m2, m2, 1.0 / D, None, ALU.mult)],
                reads=[("m12",)], writes=[("m12",)])
            job(DVE, [tt(msq, m1, m1, ALU.mult)], reads=[("m12",)], writes=[("msq",)])
            job(DVE, [tt(var, m2, msq, ALU.subtract)], reads=[("m12",), ("msq",)], writes=[("var",)])
            job(ACT, [act_fn(sd, var, AF.Sqrt, bias=eps5[:, 0:1], scale=1.0)], reads=[("var",), ("eps",)], writes=[("sd",)])
            job(DVE, [lambda e: e.reciprocal(out=rstd, in_=sd)], reads=[("sd",)], writes=[("rstd",)])
            job(DVE, [stt(nmr, m1, -1.0, rstd, ALU.mult, ALU.mult)], reads=[("m12",), ("rstd",)], writes=[("nmr",)])
            for ch in range(3):
                for fb in range(4):
                    sl = (ch * 4 + fb) % 2
                    vs = Vtm[:, ch, fb * 512:(fb + 1) * 512]
                    job(DVE, [ts(LNS[:, sl, :], vs, rstd[:, ch:ch + 1], nmr[:, ch:ch + 1], ALU.mult, ALU.add)],
                        reads=[("v", ch, fb), ("rstd",), ("nmr",)], writes=[("lns", sl)])
                    job(DVE, [tt(LNS[:, sl, :], LNS[:, sl, :], LNG[:, fb * 512:(fb + 1) * 512], ALU.mult)],
                        reads=[("lns", sl), ("lng",)], writes=[("lns", sl)])
                    job(DVE, [tt(vs, LNS[:, sl, :], LNB[:, fb * 512:(fb + 1) * 512], ALU.add)],
                        reads=[("lns", sl), ("lnb",)], writes=[("v", ch, fb)])
            if dbgopts.get("gmlp") == "ln":
                dump("d_vtm", Vtm, BF16)
                dump("d_lnst", lnst[:, :], F32)
                dump("d_hb", Hb, BF16)
                return
            for pk in range(4):
                s = load_piece(gu_d[pk], SLOT)
                W = ring[:, s, :].rearrange("p (k c) -> p k c", k=KC)
                for ff in range(4):
                    fc = pk * 4 + ff
                    g = fc // 2
                    ub = next_banks(1)
                    mms = [(ps[ub[0]][:, :n], W[:, kc, ff * 128:(ff + 1) * 128], Hb[:, kc, :], kc == 0, kc == KC - 1)
                           for kc in range(KC)]
                    pe_task(mms, [("w", s), ("h", 0)], ub)
                    fbk = next_banks(1)
                    mms = [(ps[fbk[0]][:, :n], e0[:, :], BSH[:, g, :], True, False),
                           (ps[fbk[0]][:, :n], e0[:, :], BSL[:, g, :], False, False)]
                    for ch in range(3):
                        mms.append((ps[fbk[0]][:, ch * 128:(ch + 1) * 128], Vtm[:, ch, fc * 128:(fc + 1) * 128],
                                    WsT[:, g, :], False, ch == 2))
                    pe_task(mms, [("e0",), ("bsh",), ("bsl",), ("wst",)] + [("v", ch, fc // 4) for ch in range(3)], fbk)
                    a, aT = act_slot(n)
                    job(ACT, [act_fn(aT, ps[ub[0]][:, :n], AF.Gelu)], reads=[("ps", ub[0])], writes=[("act", a)])
                    job(DVE, [tt(Gb[:, fc, :], ps[fbk[0]][:, :n], aT, ALU.mult)],
                        reads=[("ps", fbk[0]), ("act", a)], writes=[("g", fc)])
            if dbgopts.get("gmlp") == "gate":
                dump("d_vtm", Vtm, BF16)
                dump("d_gb", Gb, BF16)
                dump("d_bsh", BSH, BF16)
                dump("d_bsl", BSL, BF16)
                dump("d_wst", WsT, BF16)
                return
            for q in range(4):
                s = load_piece(go_d[q], SLOT)
                W = ring[:, s, :].rearrange("p (k c) -> p k c", k=KC)
                for dd in range(4):
                    d = q * 4 + dd
                    bk = next_banks(1)
                    mms = [(ps[bk[0]][:, :n], W[:, kc, dd * 128:(dd + 1) * 128], Gb[:, kc, :], kc == 0, kc == KC - 1)
                           for kc in range(KC)]
                    pe_task(mms, [("w", s)] + [("g", fc) for fc in range(KC)], bk)
                    resid_evac(d, t0, n, bk[0], 1.0)

    def xattn(l, gi, gmi, blocks):
        barrier()
        nmax = max(n for _, n in blocks)
        Hb = pv(0, BF16, [KC, 512])
        Qb = pv(16384, BF16, [KC, 512])
        Ob = pv(32768, BF16, [KC, 512])
        KT = pv(49152, BF16, [KC, MEM])
        VT = pv(57344, BF16, [2, D])
        PT = pv(65536, BF16, [2, 2, 512])
        RS = pv(69632, F32, [512])
        MT = pv(0, F32, [KC, MEM])
        MN = pv(16384, BF16, [KC, MEM])
        tm = dma(SP, MT, memT_d, ld_sem, ld_cnt, 0, writes=[("mt",)])
        sbk = 6 + stat_next[0]
        stat_next[0] ^= 1
        for c in range(KC):
            s = sq_next[0]
            sq_next[0] = (s + 1) % 4
            job(ACT, [act_fn(sq[:, s, :MEM], MT[:, c, :], AF.Square)], reads=[("mt",)], writes=[("sq", s)])
            job(PE, [mm(ps[sbk][:, :MEM], ones[:, :], sq[:, s, :MEM], c == 0, c == KC - 1)],
                reads=[("sq", s), ("ones",)], writes=[("ps", sbk)])
        sl = std_next[0]
        std_next[0] ^= 1
        job(ACT, [act_fn(std[:, sl, :MEM], ps[sbk][:, :MEM], AF.Sqrt, bias=eps6[:, 0:1], scale=1.0 / D)],
            reads=[("ps", sbk), ("eps",)], writes=[("std", sl)])
        job(DVE, [lambda e: e.reciprocal(out=std[:, sl, :MEM], in_=std[:, sl, :MEM])], reads=[("std", sl)], writes=[("std", sl)])
        job(DVE, [stt(MN[:, c, :], MT[:, c, :], gains[:, gmi, c:c + 1], std[:, sl, :MEM], ALU.mult, ALU.mult)
                  for c in range(KC)], reads=[("std", sl), ("mt",), ("gains",)], writes=[("mn",)])
        for pk in range(4):
            s = load_piece(wk_d[l][pk], SLOT)
            W = ring[:, s, :].rearrange("p (k c) -> p k c", k=KC)
            for ff in range(4):
                fc = pk * 4 + ff
                bk = next_banks(1)
                mms = [(ps[bk[0]][:, :MEM], W[:, kc, ff * 128:(ff + 1) * 128], MN[:, kc, :], kc == 0, kc == KC - 1)
                       for kc in range(KC)]
                pe_task(mms, [("w", s), ("mn",)], bk)
                job(ACT, [act_fn(KT[:, fc, :], ps[bk[0]][:, :MEM], AF.Copy)], reads=[("ps", bk[0])], writes=[("kt", fc)])
        for fb in range(4):
            s = load_piece(wv_d[l][fb], SLOT)
            W = ring[:, s, :].rearrange("p (k c) -> p k c", k=KC)
            for mc in range(2):
                bk = next_banks(1)
                mms = [(ps[bk[0]][:, :], MN[:, kc, mc * 128:(mc + 1) * 128], W[:, kc, :], kc == 0, kc == KC - 1)
                       for kc in range(KC)]
                pe_task(mms, [("w", s), ("mn",)], bk)
                job(DVE, [lambda e, o=VT[:, mc, fb * 512:(fb + 1) * 512], i=ps[bk[0]][:, :]: e.tensor_copy(out=o, in_=i)],
                    reads=[("ps", bk[0])], writes=[("vt", mc, fb)])
        barrier()
        qscale = 512.0 ** -0.5
        pslot = [0]
        for b, (t0, n) in enumerate(blocks):
            norm_block(gi, t0, n, Hb[:, :, :n], ("h", 0))
            for pk in range(4):
                s = load_piece(wq_d[l][pk], SLOT)
                W = ring[:, s, :].rearrange("p (k c) -> p k c", k=KC)
                for ff in range(4):
                    fc = pk * 4 + ff
                    bk = next_banks(1)
                    mms = [(ps[bk[0]][:, :n], W[:, kc, ff * 128:(ff + 1) * 128], Hb[:, kc, :n], kc == 0, kc == KC - 1)
                           for kc in range(KC)]
                    pe_task(mms, [("w", s), ("h", 0)], bk)
                    job(ACT, [act_fn(Qb[:, fc, :n], ps[bk[0]][:, :n], AF.Copy, scale=qscale)],
                        reads=[("ps", bk[0])], writes=[("q", fc)])
            for hd in range(4):
                sp_ = pslot[0]
                pslot[0] ^= 1
                for mc in range(2):
                    bk = next_banks(1)
                    mms = [(ps[bk[0]][:, :n], KT[:, hd * 4 + dc, mc * 128:(mc + 1) * 128], Qb[:, hd * 4 + dc, :n],
                            dc == 0, dc == 3) for dc in range(4)]
                    pe_task(mms, [("kt", hd * 4 + dc) for dc in range(4)] + [("q", hd * 4 + dc) for dc in range(4)], bk)
                    job(ACT, [act_fn(PT[:, sp_, mc, :n], ps[bk[0]][:, :n], AF.Exp)],
                        reads=[("ps", bk[0])], writes=[("pt", sp_, mc)])
                bk = next_banks(1)
                mms = [(ps[bk[0]][:, :n], ones[:, :], PT[:, sp_, mc, :n], mc == 0, mc == 1) for mc in range(2)]
                pe_task(mms, [("ones",), ("pt", sp_, 0), ("pt", sp_, 1)], bk)
                job(DVE, [lambda e, o=RS[:, :n], i=ps[bk[0]][:, :n]: e.reciprocal(out=o, in_=i)],
                    reads=[("ps", bk[0])], writes=[("rs",)])
                for dc in range(4):
                    fc = hd * 4 + dc
                    bk = next_banks(1)
                    mms = [(ps[bk[0]][:, :n], VT[:, mc, fc * 128:(fc + 1) * 128], PT[:, sp_, mc, :n], mc == 0, mc == 1)
                           for mc in range(2)]
                    pe_task(mms, [("vt", mc, fc // 4) for mc in range(2)] + [("pt", sp_, 0), ("pt", sp_, 1)], bk)
                    job(DVE, [tt(Ob[:, fc, :n], ps[bk[0]][:, :n], RS[:, :n], ALU.mult)],
                        reads=[("ps", bk[0]), ("rs",)], writes=[("o", fc)])
            proj_out(wo_d[l], [(t0, n)], t0, Ob, [("o", fc) for fc in range(KC)])
        return

    def conv(gi):
        barrier()
        tb = 126
        H = pv(0, BF16, [KC, 1028])
        G = pv(32896, BF16, [KC, 1024])
        ACC = pv(65664, F32, [1024])
        Z = actT
        for bi, (t0, n) in enumerate(CB):
            norm_block(gi, t0, n, H[:, :, t0 - tb:t0 - tb + n], ("h", bi))
        hk = [("h", bi) for bi in range(3)]
        for fc in range(KC):
            s = load_piece(cin_d[fc], KC * 384)
            W = ring[:, s, :KC * 384].rearrange("p (k c) -> p k c", k=KC)
            cb_ = next_banks(3)
            vb_ = next_banks(3)
            for banks_, col0 in ((cb_, 128), (vb_, 256)):
                mms = []
                for kc in range(KC):
                    for bi, (t0, n) in enumerate(CB):
                        mms.append((ps[banks_[bi]][:, :n], W[:, kc, col0:col0 + 128], H[:, kc, t0 - tb:t0 - tb + n],
                                    kc == 0, kc == KC - 1))
                pe_task(mms, [("w", s)] + hk, banks_)
            for bi, (t0, n) in enumerate(CB):
                job(ACT, [act_fn(Z[:, t0 - tb:t0 - tb + n], ps[cb_[bi]][:, :n], AF.Copy)],
                    reads=[("ps", cb_[bi])], writes=[("z", bi)])
                job(DVE, [tt(Z[:, t0 - tb:t0 - tb + n], ps[vb_[bi]][:, :n], Z[:, t0 - tb:t0 - tb + n], ALU.mult)],
                    reads=[("ps", vb_[bi]), ("z", bi)], writes=[("z", bi)])
            job(DVE, [ts(Z[:, 0:2], Z[:, 0:2], hmask[:, 0:1], None, ALU.mult)], reads=[("z", 0), ("hmask",)], writes=[("z", 0)])
            zk = [("z", bi) for bi in range(3)]
            job(ACT, [act_fn(ACC, Z[:, 2:1026], AF.Copy, scale=cw[:, fc, 2:3])], reads=zk + [("cw",)], writes=[("acc",)])
            job(DVE, [stt(ACC, Z[:, 1:1025], cw[:, fc, 1:2], ACC, ALU.mult, ALU.add)], reads=zk + [("acc",), ("cw",)], writes=[("acc",)])
            job(DVE, [stt(ACC, Z[:, 0:1024], cw[:, fc, 0:1], ACC, ALU.mult, ALU.add)], reads=zk + [("acc",), ("cw",)], writes=[("acc",)])
            bb_ = next_banks(len(BL2))
            mms = []
            for kc in range(KC):
                for bi, (t0, n) in enumerate(BL2):
                    mms.append((ps[bb_[bi]][:, :n], W[:, kc, 0:128], H[:, kc, t0 - tb:t0 - tb + n], kc == 0, kc == KC - 1))
            pe_task(mms, [("w", s)] + hk, bb_)
            for bi, (t0, n) in enumerate(BL2):
                job(DVE, [tt(G[:, fc, t0 - 128:t0 - 128 + n], ps[bb_[bi]][:, :n], ACC[:, t0 - 128:t0 - 128 + n], ALU.mult)],
                    reads=[("ps", bb_[bi]), ("acc",)], writes=[("g", fc, bi)])
        for q in range(4):
            s = load_piece(co_d[q], SLOT)
            W = ring[:, s, :].rearrange("p (k c) -> p k c", k=KC)
            for dd in range(4):
                d = q * 4 + dd
                bk = next_banks(len(BL2))
                mms = []
                for kc in range(KC):
                    for bi, (t0, n) in enumerate(BL2):
                        mms.append((ps[bk[bi]][:, :n], W[:, kc, dd * 128:(dd + 1) * 128], G[:, kc, t0 - 128:t0 - 128 + n],
                                    kc == 0, kc == KC - 1))
                pe_task(mms, [("w", s)] + [("g", fc, bi) for fc in range(KC) for bi in range(len(BL2))], bk)
                for bi, (t0, n) in enumerate(BL2):
                    resid_evac(d, t0, n, bk[bi], 1.0)

    def final():
        barrier()
        for bi, (t0, n) in enumerate(BL2):
            Yb = pv((t0 - 128) * KC * 4, F32, [KC, n])
            norm_block(10, t0, n, Yb, ("y", bi))
            dma(SP, out_d[:, :, t0 - 128:t0 - 128 + n], Yb, st_sem, st_cnt, 0, reads=[("y", bi)])

    stages = [
        lambda: ffn(0, 0, 0, BL3, 0),
        lambda: gmlp(1),
        lambda: xattn(0, 2, 3, BX),
        lambda: ffn(0, 1, 4, BX, 126),
        lambda: ffn(1, 0, 5, BX, 126),
        lambda: conv(6),
        lambda: xattn(1, 7, 8, XB1),
        lambda: ffn(1, 1, 9, BY, 128),
        lambda: final(),
    ]
    for st in stages[:nstages]:
        st()
    if dbg:
        barrier()
        allx = [("x", c, ch) for c in range(KC) for ch in range(9)]
        dma(SP, out_d, xT[:, :, :], st_sem, st_cnt, 0, reads=allx)
    SP.ops.append(("wait", st_sem, st_cnt[0]))

    with nc.Block() as block:
        @block.tensor
        def _(e):
            PE.replay(e)

        @block.scalar
        def _(e):
            ACT.replay(e)

        @block.vector
        def _(e):
            DVE.replay(e)

        @block.gpsimd
        def _(e):
            POOL.replay(e)

        @block.sync
        def _(e):
            SP.replay(e)
    es.close()
    return nc


def _pieces(W, ncols):
    K, C = W.shape
    npc = C // ncols
    return np.ascontiguousarray(W.reshape(K // 128, 128, npc, ncols).transpose(2, 1, 0, 3)).reshape(npc, 128, (K // 128) * ncols)


def _prep(inputs):
    f = lambda a: np.asarray(a, dtype=np.float32)
    x = f(inputs["x"])[0]
    mem = f(inputs["mem"])[0]
    shared = {}
    shared["memT"] = np.ascontiguousarray(mem.T)
    gl = []
    for l in range(2):
        for nm in ("ffn1_norm", "mix_norm", "xattn_norm", "mem_norm", "ffn2_norm"):
            gl.append(f(inputs[nm])[l])
    gl.append(f(inputs["final_norm"]))
    g = np.stack(gl, 0)
    shared["gains"] = np.ascontiguousarray(g.reshape(11, 16, 128).transpose(2, 0, 1)).reshape(128, 11 * 16)
    for l in range(2):
        for fi, nm in enumerate(("ffn1", "ffn2")):
            w13 = f(inputs[nm + "_w13"])[l]
            gate = w13[:, :5632].reshape(16, 128, 22, 2, 1, 128)
            up = w13[:, 5632:].reshape(16, 128, 22, 2, 1, 128)
            cat = np.concatenate([gate, up], axis=4)
            shared[f"w13_{l}_{fi}"] = np.ascontiguousarray(cat.transpose(2, 1, 0, 3, 4, 5)).reshape(22, 128, SLOT)
            w2 = f(inputs[nm + "_w2"])[l]
            arr = np.zeros((16, 128, 12 * 512), np.float32)
            for gi_, (j0, ng) in enumerate(GROUPS):
                blk = w2[j0 * 128:(j0 + ng) * 128].reshape(ng, 128, 4, 512)
                arr[gi_ * 4:gi_ * 4 + 4, :, :ng * 512] = blk.transpose(2, 1, 0, 3).reshape(4, 128, ng * 512)
            shared[f"w2_{l}_{fi}"] = arr
        shared[f"wq_{l}"] = _pieces(f(inputs["xattn_wq"])[l], 512)
        wkv = f(inputs["xattn_wkv"])[l]
        shared[f"wk_{l}"] = _pieces(wkv[:, :D], 512)
        shared[f"wv_{l}"] = _pieces(wkv[:, D:], 512)
        shared[f"wo_{l}"] = _pieces(f(inputs["xattn_wo"])[l], 512)
    win = f(inputs["gmlp_w_in"])[0]
    shared["gmlp_u"] = _pieces(win[:, :D], 512)
    shared["gmlp_v"] = _pieces(win[:, D:], 512)
    shared["gmlp_o"] = _pieces(f(inputs["gmlp_w_out"])[0], 512)
    shared["ln_g"] = f(inputs["gmlp_ln_g"])[0].reshape(1, D)
    shared["ln_b"] = f(inputs["gmlp_ln_b"])[0].reshape(1, D)
    ws = f(inputs["gmlp_w_s"])[0]
    shared["wsT"] = np.ascontiguousarray(ws.transpose(2, 0, 1)).reshape(128, 8 * 128)
    shared["trilT"] = np.ascontiguousarray(np.tril(np.ones((128, 128), np.float32)).T)
    shared["b_s"] = f(inputs["gmlp_b_s"])[0].reshape(1, 8 * 128)
    cin = f(inputs["conv_w_in"])[0]
    c3 = cin.reshape(16, 128, 3, 16, 128)
    shared["conv_in"] = np.ascontiguousarray(c3.transpose(3, 1, 0, 2, 4)).reshape(16, 128, 16 * 384)
    cwv = f(inputs["conv_w"])[0]
    shared["conv_w"] = np.ascontiguousarray(cwv.reshape(3, 16, 128).transpose(2, 1, 0)).reshape(128, 48)
    shared["conv_o"] = _pieces(f(inputs["conv_w_out"])[0], 512)
    in_maps = []
    for i in range(NCORES):
        xs = np.zeros((NT, D), np.float32)
        if i > 0:
            xs[:] = x[i * TOWN - HALO:(i + 1) * TOWN]
        else:
            xs[HALO:] = x[:TOWN]
        m = dict(shared)
        m["xT"] = np.ascontiguousarray(xs.T)
        m["hmask"] = np.full((128, 2), 1.0 if i > 0 else 0.0, np.float32)
        in_maps.append(m)
    return in_maps


_NC_CACHE = {}


def kernel(**inputs):
    in_maps = _prep(inputs)
    if "nc" not in _NC_CACHE:
        _NC_CACHE["nc"] = build()
    nc = _NC_CACHE["nc"]
    res = run_bass_kernel_spmd(nc, in_maps, core_ids=list(range(NCORES)))
    outs = [np.asarray(r["outT"]).T for r in res.results]
    return np.concatenate(outs, axis=0).reshape(1, NCORES * TOWN, D).astype(np.float32)
```

```python
import contextlib
import numpy as np
import concourse.bass as bass
import concourse.mybir as mybir
from concourse.bass_utils import run_bass_kernel_spmd

F32 = mybir.dt.float32
BF16 = mybir.dt.bfloat16
AF = mybir.ActivationFunctionType
ALU = mybir.AluOpType
AX = mybir.AxisListType

D = 2048
KC = 16
HCH = 44
MEM = 256
NCORES = 8
TOWN = 1024
HALO = 128
NT = TOWN + HALO
SLOT = 8192
NSLOT = 3
GROUPS = [(0, 12), (12, 12), (24, 12), (36, 8)]
BL3 = [(0, 384), (384, 384), (768, 384)]
BX = [(126, 386), (512, 384), (896, 256)]
BY = [(128, 384), (512, 384), (896, 256)]
XB1 = [(128, 512), (640, 512)]
BL2 = BY
CB = BX
NSTAGES_ALL = 9


class Tok:
    __slots__ = ("sem", "val")

    def __init__(self, sem, val):
        self.sem, self.val = sem, val


class Eng:
    def __init__(self, name, sem):
        self.name, self.sem, self.cnt, self.ops, self.seen = name, sem, 0, [], {}

    def wait(self, toks, skip_own=False):
        for t in toks:
            if t is None:
                continue
            if skip_own and t.sem is self.sem:
                continue
            k = id(t.sem)
            if self.seen.get(k, 0) >= t.val:
                continue
            self.seen[k] = t.val
            self.ops.append(("wait", t.sem, t.val))

    def emit(self, fn, signal=False):
        if signal:
            self.cnt += 1
            self.ops.append(("sig", fn))
            return Tok(self.sem, self.cnt)
        self.ops.append(("op", fn))
        return None

    def replay(self, e):
        for op in self.ops:
            if op[0] == "wait":
                e.wait_ge(op[1], op[2])
            elif op[0] == "op":
                op[1](e)
            elif op[0] == "sig":
                op[1](e).then_inc(self.sem, 1)
            elif op[0] == "dma":
                kw = op[4]
                e.dma_start(out=op[1], in_=op[2], **kw).then_inc(op[3], 16)


class Tracker:
    def __init__(self):
        self.w, self.r = {}, {}

    def deps_read(self, keys):
        return [self.w.get(k) for k in keys]

    def deps_write(self, keys):
        out = []
        for k in keys:
            out.append(self.w.get(k))
            out.extend(self.r.get(k, {}).values())
        return out

    def did_read(self, keys, tok):
        for k in keys:
            d = self.r.setdefault(k, {})
            cur = d.get(id(tok.sem))
            if cur is None or cur.val < tok.val:
                d[id(tok.sem)] = tok

    def did_write(self, keys, tok):
        for k in keys:
            self.w[k] = tok
            self.r[k] = {}


def build(nstages=NSTAGES_ALL, dbg=False, dbgopts=None):
    dbgopts = dbgopts or {}
    nc = bass.Bass("TRN2", target_bir_lowering=False)
    es = contextlib.ExitStack()

    def din(name, shape):
        return nc.dram_tensor(name, list(shape), F32, kind="ExternalInput").ap()

    xT_d = din("xT", [D, NT]).rearrange("(c p) t -> p c t", p=128)
    hmask_d = din("hmask", [128, 2])
    memT_d = din("memT", [D, MEM]).rearrange("(c p) t -> p c t", p=128)
    gains_d = din("gains", [128, 11 * 16])
    w13_d = {(l, f): din(f"w13_{l}_{f}", [22, 128, SLOT]) for l in range(2) for f in range(2)}
    w2_d = {(l, f): din(f"w2_{l}_{f}", [16, 128, 12 * 512]) for l in range(2) for f in range(2)}
    gv_d = din("gmlp_v", [4, 128, SLOT])
    gu_d = din("gmlp_u", [4, 128, SLOT])
    go_d = din("gmlp_o", [4, 128, SLOT])
    lng_d = din("ln_g", [1, D])
    lnb_d = din("ln_b", [1, D])
    wsT_d = din("wsT", [128, 8 * 128])
    trilT_d = din("trilT", [128, 128])
    bs_d = din("b_s", [1, 8 * 128])
    cin_d = din("conv_in", [16, 128, 16 * 384])
    cw_d = din("conv_w", [128, 16 * 3])
    co_d = din("conv_o", [4, 128, SLOT])
    wq_d = [din(f"wq_{l}", [4, 128, SLOT]) for l in range(2)]
    wk_d = [din(f"wk_{l}", [4, 128, SLOT]) for l in range(2)]
    wv_d = [din(f"wv_{l}", [4, 128, SLOT]) for l in range(2)]
    wo_d = [din(f"wo_{l}", [4, 128, SLOT]) for l in range(2)]
    if dbg:
        out_d = nc.dram_tensor("dbg", [D, NT], F32, kind="ExternalOutput").ap().rearrange("(c p) t -> p c t", p=128)
    else:
        out_d = nc.dram_tensor("outT", [D, TOWN], F32, kind="ExternalOutput").ap().rearrange("(c p) t -> p c t", p=128)

    def sb(name, shape, dt):
        return es.enter_context(nc.sbuf_tensor(name, list(shape), dt))

    xT = sb("xT_sb", [128, KC, NT], F32)
    P = sb("P_sb", [128, 36864], BF16)
    ring = sb("ring_sb", [128, NSLOT, SLOT], BF16)
    sq = sb("sq_sb", [128, 4, 512], BF16)
    std = sb("std_sb", [128, 2, 512], F32)
    actT = sb("act_sb", [128, 1160], F32)
    ones = sb("ones_sb", [128, 128], BF16)
    e0 = sb("e0_sb", [128, 128], BF16)
    gains = sb("gains_sb", [128, 11, 16], F32)
    eps6 = sb("eps6_sb", [128, 1], F32)
    eps5 = sb("eps5_sb", [128, 1], F32)
    cw = sb("cw_sb", [128, 16, 3], F32)
    hmask = sb("hmask_sb", [128, 2], F32)
    lnst = sb("lnst_sb", [128, 64], F32)

    ps = [es.enter_context(nc.psum_tensor(f"ps{i}", [128, 512], F32)) for i in range(8)]

    def pv(off, dt, shape):
        n = int(np.prod(shape))
        if dt is BF16:
            assert off % 2 == 0
            ap = P[:, off // 2: off // 2 + n]
        else:
            assert off % 4 == 0
            ap = P[:, off // 2: off // 2 + 2 * n].bitcast(F32)
        assert off + n * (2 if dt is BF16 else 4) <= 36864 * 2
        if len(shape) == 2:
            ap = ap.rearrange("p (a b) -> p a b", a=shape[0])
        elif len(shape) == 3:
            ap = ap.rearrange("p (a b c) -> p a b c", a=shape[0], b=shape[1])
        return ap

    def sem(name):
        return es.enter_context(nc.semaphore(name))

    PE = Eng("pe", sem("s_pe"))
    ACT = Eng("act", sem("s_act"))
    DVE = Eng("dve", sem("s_dve"))
    POOL = Eng("pool", sem("s_pool"))
    SP = Eng("sp", sem("s_sp"))
    slot_sems = [sem(f"s_slot{i}") for i in range(NSLOT)]
    slot_cnt = [0] * NSLOT
    ld_sem = sem("s_ld")
    ld_cnt = [0]
    st_sem = sem("s_st")
    st_cnt = [0]
    TR = Tracker()

    def job(eng, fns, reads=(), writes=()):
        reads, writes = list(reads), list(writes)
        eng.wait(TR.deps_read(reads) + TR.deps_write(writes), skip_own=(eng is PE))
        for fn in fns[:-1]:
            eng.emit(fn)
        tok = eng.emit(fns[-1], signal=True)
        TR.did_read(reads, tok)
        TR.did_write(writes, tok)
        return tok

    def dma(q, out_ap, in_ap, semh, cnt, idx, reads=(), writes=(), **kw):
        reads, writes = list(reads), list(writes)
        q.wait(TR.deps_read(reads) + TR.deps_write(writes))
        cnt[idx] += 16
        q.ops.append(("dma", out_ap, in_ap, semh, kw))
        tok = Tok(semh, cnt[idx])
        TR.did_read(reads, tok)
        TR.did_write(writes, tok)
        return tok

    def barrier():
        toks = [Tok(e.sem, e.cnt) for e in (PE, ACT, DVE) if e.cnt > 0]
        for e in (PE, ACT, DVE, SP):
            e.wait(toks)

    def dump(name, ap, dt):
        barrier()
        shp = list(ap.shape)
        dd = nc.dram_tensor(name, shp, dt, kind="ExternalOutput").ap()
        SP.wait([Tok(e.sem, e.cnt) for e in (PE, ACT, DVE) if e.cnt > 0])
        st_cnt[0] += 16
        SP.ops.append(("dma", dd, ap, st_sem, {}))

    slot_next = [0]

    def load_piece(src_ap, nel):
        s = slot_next[0]
        slot_next[0] = (s + 1) % NSLOT
        nb = nel // 2048
        dma(POOL, ring[:, s, :nel].rearrange("p (a b) -> p a b", a=nb),
            src_ap.rearrange("p (a b) -> p a b", a=nb),
            slot_sems[s], slot_cnt, s, writes=[("w", s)])
        return s

    bank_next = [0]

    def next_banks(n):
        out = []
        for _ in range(n):
            out.append(bank_next[0])
            bank_next[0] = (bank_next[0] + 1) % 6
        return out

    stat_next = [0]
    sq_next = [0]
    std_next = [0]

    def xkeys(c, t0, n):
        return [("x", c, ch) for ch in range(t0 // 128, (t0 + n - 1) // 128 + 1)]

    def mm(o, l, r, st, sp):
        return lambda e: e.matmul(o, l, r, start=st, stop=sp)

    def pe_task(mms, reads, banks):
        return job(PE, [mm(*m) for m in mms], reads=reads, writes=[("ps", b) for b in banks])

    def act_fn(out, in_, func, **kw):
        return lambda e: e.activation(out=out, in_=in_, func=func, **kw)

    def tt(out, in0, in1, op):
        return lambda e: e.tensor_tensor(out=out, in0=in0, in1=in1, op=op)

    def stt(out, in0, scalar, in1, op0, op1):
        return lambda e: e.scalar_tensor_tensor(out=out, in0=in0, scalar=scalar, in1=in1, op0=op0, op1=op1)

    def ts(out, in0, s1, s2, op0, op1=None):
        if op1 is None:
            return lambda e: e.tensor_scalar(out=out, in0=in0, scalar1=s1, scalar2=None, op0=op0)
        return lambda e: e.tensor_scalar(out=out, in0=in0, scalar1=s1, scalar2=s2, op0=op0, op1=op1)

    xl_sems = [sem(f"s_xl{i}") for i in range(3)]
    xl_cnt = [0, 0, 0]
    for i, (t0, n) in enumerate(BL3):
        dma(SP, xT[:, :, t0:t0 + n], xT_d[:, :, t0:t0 + n], xl_sems[i], xl_cnt, i,
            writes=[("x", c, ch) for c in range(KC) for ch in range(3 * i, 3 * i + 3)])
    xkeep = dict(TR.w)
    dma(SP, gains[:, :, :].rearrange("p a b -> p (a b)"), gains_d, ld_sem, ld_cnt, 0, writes=[("gains",)])
    dma(SP, cw[:, :, :].rearrange("p a b -> p (a b)"), cw_d, ld_sem, ld_cnt, 0, writes=[("cw",)])
    tok_ld = dma(SP, hmask[:, :], hmask_d, ld_sem, ld_cnt, 0, writes=[("hmask",)])
    for k in list(TR.w.keys()):
        if k not in xkeep:
            TR.w[k] = tok_ld
    job(DVE, [lambda e: e.memset(ones[:, :], 1.0)], writes=[("ones",)])
    job(DVE, [lambda e: e.memset(e0[:, :], 0.0)], writes=[("e0",)])
    job(DVE, [lambda e: e.memset(e0[0:1, :], 1.0)], writes=[("e0",)])
    job(DVE, [lambda e: e.memset(eps6[:, :], 1e-6)], writes=[("eps",)])
    job(DVE, [lambda e: e.memset(eps5[:, :], 1e-5)], writes=[("eps",)])

    def norm_block(gi, t0, n, hdst, hkey, f32out=False):
        sbk = 6 + stat_next[0]
        stat_next[0] ^= 1
        for c in range(KC):
            s = sq_next[0]
            sq_next[0] = (s + 1) % 4
            job(ACT, [act_fn(sq[:, s, :n], xT[:, c, t0:t0 + n], AF.Square)],
                reads=xkeys(c, t0, n), writes=[("sq", s)])
            job(PE, [mm(ps[sbk][:, :n], ones[:, :], sq[:, s, :n], c == 0, c == KC - 1)],
                reads=[("sq", s), ("ones",)], writes=[("ps", sbk)])
        sl = std_next[0]
        std_next[0] ^= 1
        job(ACT, [act_fn(std[:, sl, :n], ps[sbk][:, :n], AF.Sqrt, bias=eps6[:, 0:1], scale=1.0 / D)],
            reads=[("ps", sbk), ("eps",)], writes=[("std", sl)])
        job(DVE, [lambda e: e.reciprocal(out=std[:, sl, :n], in_=std[:, sl, :n])],
            reads=[("std", sl)], writes=[("std", sl)])
        fns = [stt(hdst[:, c, :], xT[:, c, t0:t0 + n], gains[:, gi, c:c + 1], std[:, sl, :n], ALU.mult, ALU.mult)
               for c in range(KC)]
        rk = [("std", sl), ("gains",)]
        for c in range(KC):
            rk += xkeys(c, t0, n)
        job(DVE, fns, reads=rk, writes=[hkey])

    act_next = [0]

    def act_slot(n):
        if n <= 386:
            a = act_next[0] % 3
            act_next[0] = (a + 1) % 3
            return a, actT[:, a * 386: a * 386 + n]
        a = act_next[0] % 2
        act_next[0] = (a + 1) % 2
        return a, actT[:, a * 512: a * 512 + n]

    def resid_evac(d, t0, n, bank, scale):
        if scale == 1.0:
            fn = tt(xT[:, d, t0:t0 + n], ps[bank][:, :n], xT[:, d, t0:t0 + n], ALU.add)
        else:
            fn = stt(xT[:, d, t0:t0 + n], ps[bank][:, :n], scale, xT[:, d, t0:t0 + n], ALU.mult, ALU.add)
        job(DVE, [fn], reads=[("ps", bank)] + xkeys(d, t0, n), writes=xkeys(d, t0, n))

    def proj_out(w_d, blocks, tb, G, gkeys):
        for q in range(4):
            s = load_piece(w_d[q], SLOT)
            W = ring[:, s, :].rearrange("p (k c) -> p k c", k=KC)
            for dd in range(4):
                d = q * 4 + dd
                bk = next_banks(len(blocks))
                mms = []
                for kc in range(KC):
                    for bi, (t0, n) in enumerate(blocks):
                        mms.append((ps[bk[bi]][:, :n], W[:, kc, dd * 128:(dd + 1) * 128],
                                    G[:, kc, t0 - tb:t0 - tb + n], kc == 0, kc == KC - 1))
                pe_task(mms, [("w", s)] + list(gkeys), bk)
                for bi, (t0, n) in enumerate(blocks):
                    resid_evac(d, t0, n, bk[bi], 1.0)

    def ffn(l, f, gi, blocks, tb):
        ntl = sum(n for _, n in blocks)
        barrier()
        H = pv(0, BF16, [KC, ntl])
        HID = pv(KC * ntl * 2, BF16, [12, ntl])
        nb = len(blocks)
        for bi, (t0, n) in enumerate(blocks):
            norm_block(gi, t0, n, H[:, :, t0 - tb:t0 - tb + n], ("h", bi))
        hk = [("h", bi) for bi in range(nb)]
        for g, (j0, ng) in enumerate(GROUPS):
            for pk in range(j0 // 2, (j0 + ng) // 2):
                s = load_piece(w13_d[(l, f)][pk], SLOT)
                W = ring[:, s, :].rearrange("p (k c) -> p k c", k=KC)
                for jj in range(2):
                    jl = 2 * pk + jj - j0
                    gb = next_banks(nb)
                    ub = next_banks(nb)
                    for banks_, col0 in ((gb, jj * 256), (ub, jj * 256 + 128)):
                        for bi, (t0, n) in enumerate(blocks):
                            mms = [(ps[banks_[bi]][:, :n], W[:, kc, col0:col0 + 128],
                                    H[:, kc, t0 - tb:t0 - tb + n], kc == 0, kc == KC - 1) for kc in range(KC)]
                            pe_task(mms, [("w", s), ("h", bi)], [banks_[bi]])
                    for bi, (t0, n) in enumerate(blocks):
                        a, aT = act_slot(n)
                        job(ACT, [act_fn(aT, ps[gb[bi]][:, :n], AF.Silu)], reads=[("ps", gb[bi])], writes=[("act", a)])
                        job(DVE, [tt(HID[:, jl, t0 - tb:t0 - tb + n], ps[ub[bi]][:, :n], aT, ALU.mult)],
                            reads=[("ps", ub[bi]), ("act", a)], writes=[("hid", jl, bi)])
            for q in range(4):
                s = load_piece(w2_d[(l, f)][g * 4 + q][:, :ng * 512], ng * 512)
                W2 = ring[:, s, :ng * 512].rearrange("p (k c) -> p k c", k=ng)
                for dd in range(4):
                    d = q * 4 + dd
                    bk = next_banks(nb)
                    for bi, (t0, n) in enumerate(blocks):
                        mms = [(ps[bk[bi]][:, :n], W2[:, hc, dd * 128:(dd + 1) * 128],
                                HID[:, hc, t0 - tb:t0 - tb + n], hc == 0, hc == ng - 1) for hc in range(ng)]
                        pe_task(mms, [("w", s)] + [("hid", hc, bi) for hc in range(ng)], [bk[bi]])
                        resid_evac(d, t0, n, bk[bi], 0.5)

    def gmlp(gi):
        barrier()
        Hb = pv(0, BF16, [KC, 384])
        Vtm = pv(12288, BF16, [3, D])
        Gb = pv(24576, BF16, [KC, 384])
        LNG = pv(36864, F32, [D])
        LNB = pv(45056, F32, [D])
        WsT = pv(53248, BF16, [8, 128])
        BSH = pv(55296, BF16, [8, 384])
        BSL = pv(61440, BF16, [8, 384])
        LNS = pv(67584, F32, [2, 512])
        WsF = actT[:, 0:1024].rearrange("p (a b) -> p a b", a=8)
        TrF = actT[:, 1024:1152]
        BsF = pv(67584, F32, [8, 128])
        ak = [("act", a) for a in range(3)]
        lk = [("lns", 0), ("lns", 1)]
        t1 = dma(SP, LNG, lng_d.partition_broadcast(128), ld_sem, ld_cnt, 0, writes=[("lng",)])
        t2 = dma(SP, LNB, lnb_d.partition_broadcast(128), ld_sem, ld_cnt, 0, writes=[("lnb",)])
        t3 = dma(SP, WsF, wsT_d.rearrange("p (a b) -> p a b", a=8), ld_sem, ld_cnt, 0, writes=[("wsf",)] + ak)
        t4 = dma(SP, TrF, trilT_d, ld_sem, ld_cnt, 0, writes=[("trf",)] + ak)
        t5 = dma(SP, BsF[0:1].rearrange("p a b -> p (a b)"), bs_d, ld_sem, ld_cnt, 0, writes=[("bsf",)] + lk)
        for k in [("lng",), ("lnb",), ("wsf",), ("trf",), ("bsf",)] + ak + lk:
            TR.w[k] = t5
        norm_block(gi, BL3[0][0], BL3[0][1], Hb, ("h", 0))
        job(DVE, [tt(WsT[:, g, :], WsF[:, g, :], TrF, ALU.mult) for g in range(8)],
            reads=[("wsf",), ("trf",)] + ak, writes=[("wst",)])
        job(DVE, [lambda e: e.memset(BSH.rearrange("p a b -> p (a b)"), 0.0),
                  lambda e: e.memset(BSL.rearrange("p a b -> p (a b)"), 0.0)], writes=[("bsh",), ("bsl",)])
        job(DVE, [(lambda e, g=g, r=r: e.tensor_copy(out=BSH[0:1, g, r * 128:(r + 1) * 128], in_=BsF[0:1, g, :]))
                  for g in range(8) for r in range(3)], reads=[("bsf",)] + lk, writes=[("bsh",)])
        job(DVE, [tt(BSL[0:1, g, r * 128:(r + 1) * 128], BsF[0:1, g, :], BSH[0:1, g, r * 128:(r + 1) * 128], ALU.subtract)
                  for g in range(8) for r in range(3)], reads=[("bsf",), ("bsh",)] + lk, writes=[("bsl",)])
        UB, FB = [0, 1, 2, 3], [4, 5]
        for b, (t0, n) in enumerate(BL3):
            if b > 0:
                norm_block(gi, t0, n, Hb, ("h", 0))
            for fb in range(4):
                s = load_piece(gv_d[fb], SLOT)
                W = ring[:, s, :].rearrange("p (k c) -> p k c", k=KC)
                for ch in range(3):
                    bk = next_banks(1)
                    mms = [(ps[bk[0]][:, :], Hb[:, kc, ch * 128:(ch + 1) * 128], W[:, kc, :], kc == 0, kc == KC - 1)
                           for kc in range(KC)]
                    pe_task(mms, [("w", s), ("h", 0)], bk)
                    col = ch * 4 + fb
                    job(ACT, [act_fn(Vtm[:, ch, fb * 512:(fb + 1) * 512], ps[bk[0]][:, :], AF.Gelu,
                                     accum_out=lnst[:, col:col + 1])],
                        reads=[("ps", bk[0])], writes=[("v", ch, fb), ("s1", col)])
                    s_ = sq_next[0]
                    sq_next[0] = (s_ + 1) % 4
                    job(ACT, [act_fn(sq[:, s_, :], Vtm[:, ch, fb * 512:(fb + 1) * 512], AF.Square,
                                     accum_out=lnst[:, 12 + col:13 + col])],
                        reads=[("v", ch, fb)], writes=[("sq", s_), ("s2", col)])
            S1 = lnst[:, 0:12].rearrange("p (a b) -> p a b", a=3)
            S2 = lnst[:, 12:24].rearrange("p (a b) -> p a b", a=3)
            m1, m2, msq, var, sd, rstd, nmr = (lnst[:, 24 + 3 * i:27 + 3 * i] for i in range(7))
            allv = [("v", ch, fb) for ch in range(3) for fb in range(4)]
            job(DVE, [lambda e: e.tensor_reduce(out=m1, in_=S1, axis=AX.X, op=ALU.add),
                      lambda e: e.tensor_reduce(out=m2, in_=S2, axis=AX.X, op=ALU.add)],
                reads=[("s1", c_) for c_ in range(12)] + [("s2", c_) for c_ in range(12)], writes=[("m12",)])
            job(DVE, [ts(m1, m1, 1.0 / D, None, ALU.mult), ts(m2, m2, 1.0 / D, None, ALU.mult)],
                reads=[("m12",)], writes=[("m12",)])
            job(DVE, [tt(msq, m1, m1, ALU.mult)], reads=[("m12",)], writes=[("msq",)])
            job(DVE, [tt(var, m2, msq, ALU.subtract)], reads=[("m12",), ("msq",)], writes=[("var",)])
            job(ACT, [act_fn(sd, var, AF.Sqrt, bias=eps5[:, 0:1], scale=1.0)], reads=[("var",), ("eps",)], writes=[("sd",)])
            job(DVE, [lambda e: e.reciprocal(out=rstd, in_=sd)], reads=[("sd",)], writes=[("rstd",)])
            job(DVE, [stt(nmr, m1, -1.0, rstd, ALU.mult, ALU.mult)], reads=[("m12",), ("rstd",)], writes=[("nmr",)])
            for ch in range(3):
                for fb in range(4):
                    sl = (ch * 4 + fb) % 2
                    vs = Vtm[:, ch, fb * 512:(fb + 1) * 512]
                    job(DVE, [ts(LNS[:, sl, :], vs, rstd[:, ch:ch + 1], nmr[:, ch:ch + 1], ALU.mult, ALU.add)],
                        reads=[("v", ch, fb), ("rstd",), ("nmr",)], writes=[("lns", sl)])
                    job(DVE, [tt(LNS[:, sl, :], LNS[:, sl, :], LNG[:, fb * 512:(fb + 1) * 512], ALU.mult)],
                        reads=[("lns", sl), ("lng",)], writes=[("lns", sl)])
                    job(DVE, [tt(vs, LNS[:, sl, :], LNB[:, fb * 512:(fb + 1) * 512], ALU.add)],
                        reads=[("lns", sl), ("lnb",)], writes=[("v", ch, fb)])
            if dbgopts.get("gmlp") == "ln":
                dump("d_vtm", Vtm, BF16)
                dump("d_lnst", lnst[:, :], F32)
                dump("d_hb", Hb, BF16)
                return
            LA = 4
            wslot = {}

            def emit_u(fc):
                pk, ff = fc // 4, fc % 4
                if ff == 0:
                    wslot[pk] = load_piece(gu_d[pk], SLOT)
                s_ = wslot[pk]
                W_ = ring[:, s_, :].rearrange("p (k c) -> p k c", k=KC)
                ubk = UB[fc % 4]
                mms_ = [(ps[ubk][:, :n], W_[:, kc, ff * 128:(ff + 1) * 128], Hb[:, kc, :], kc == 0, kc == KC - 1)
                        for kc in range(KC)]
                pe_task(mms_, [("w", s_), ("h", 0)], [ubk])

            for fc in range(LA):
                emit_u(fc)
            for fc in range(KC):
                g = fc // 2
                ubk = UB[fc % 4]
                fbk = FB[fc % 2]
                mms = [(ps[fbk][:, :n], e0[:, :], BSH[:, g, :], True, False),
                       (ps[fbk][:, :n], e0[:, :], BSL[:, g, :], False, False)]
                for ch in range(3):
                    mms.append((ps[fbk][:, ch * 128:(ch + 1) * 128], Vtm[:, ch, fc * 128:(fc + 1) * 128],
                                WsT[:, g, :], False, ch == 2))
                pe_task(mms, [("e0",), ("bsh",), ("bsl",), ("wst",)] + [("v", ch, fc // 4) for ch in range(3)], [fbk])
                a, aT = act_slot(n)
                job(ACT, [act_fn(aT, ps[ubk][:, :n], AF.Gelu)], reads=[("ps", ubk)], writes=[("act", a)])
                job(DVE, [tt(Gb[:, fc, :], ps[fbk][:, :n], aT, ALU.mult)],
                    reads=[("ps", fbk), ("act", a)], writes=[("g", fc)])
                if fc + LA < KC:
                    emit_u(fc + LA)
            if dbgopts.get("gmlp") == "gate":
                dump("d_vtm", Vtm, BF16)
                dump("d_gb", Gb, BF16)
                dump("d_bsh", BSH, BF16)
                dump("d_bsl", BSL, BF16)
                dump("d_wst", WsT, BF16)
                return
            for q in range(4):
                s = load_piece(go_d[q], SLOT)
                W = ring[:, s, :].rearrange("p (k c) -> p k c", k=KC)
                for dd in range(4):
                    d = q * 4 + dd
                    bk = next_banks(1)
                    mms = [(ps[bk[0]][:, :n], W[:, kc, dd * 128:(dd + 1) * 128], Gb[:, kc, :], kc == 0, kc == KC - 1)
                           for kc in range(KC)]
                    pe_task(mms, [("w", s)] + [("g", fc) for fc in range(KC)], bk)
                    resid_evac(d, t0, n, bk[0], 1.0)

    def xattn(l, gi, gmi, blocks):
        barrier()
        nmax = max(n for _, n in blocks)
        Hb = pv(0, BF16, [KC, 512])
        Qb = pv(16384, BF16, [KC, 512])
        Ob = pv(32768, BF16, [KC, 512])
        KT = pv(49152, BF16, [KC, MEM])
        VT = pv(57344, BF16, [2, D])
        PT = pv(65536, BF16, [2, 2, 512])
        RS = pv(69632, F32, [512])
        MT = pv(0, F32, [KC, MEM])
        MN = pv(16384, BF16, [KC, MEM])
        tm = dma(SP, MT, memT_d, ld_sem, ld_cnt, 0, writes=[("mt",)])
        sbk = 6 + stat_next[0]
        stat_next[0] ^= 1
        for c in range(KC):
            s = sq_next[0]
            sq_next[0] = (s + 1) % 4
            job(ACT, [act_fn(sq[:, s, :MEM], MT[:, c, :], AF.Square)], reads=[("mt",)], writes=[("sq", s)])
            job(PE, [mm(ps[sbk][:, :MEM], ones[:, :], sq[:, s, :MEM], c == 0, c == KC - 1)],
                reads=[("sq", s), ("ones",)], writes=[("ps", sbk)])
        sl = std_next[0]
        std_next[0] ^= 1
        job(ACT, [act_fn(std[:, sl, :MEM], ps[sbk][:, :MEM], AF.Sqrt, bias=eps6[:, 0:1], scale=1.0 / D)],
            reads=[("ps", sbk), ("eps",)], writes=[("std", sl)])
        job(DVE, [lambda e: e.reciprocal(out=std[:, sl, :MEM], in_=std[:, sl, :MEM])], reads=[("std", sl)], writes=[("std", sl)])
        job(DVE, [stt(MN[:, c, :], MT[:, c, :], gains[:, gmi, c:c + 1], std[:, sl, :MEM], ALU.mult, ALU.mult)
                  for c in range(KC)], reads=[("std", sl), ("mt",), ("gains",)], writes=[("mn",)])
        for pk in range(4):
            s = load_piece(wk_d[l][pk], SLOT)
            W = ring[:, s, :].rearrange("p (k c) -> p k c", k=KC)
            for ff in range(4):
                fc = pk * 4 + ff
                bk = next_banks(1)
                mms = [(ps[bk[0]][:, :MEM], W[:, kc, ff * 128:(ff + 1) * 128], MN[:, kc, :], kc == 0, kc == KC - 1)
                       for kc in range(KC)]
                pe_task(mms, [("w", s), ("mn",)], bk)
                job(ACT, [act_fn(KT[:, fc, :], ps[bk[0]][:, :MEM], AF.Copy)], reads=[("ps", bk[0])], writes=[("kt", fc)])
        for fb in range(4):
            s = load_piece(wv_d[l][fb], SLOT)
            W = ring[:, s, :].rearrange("p (k c) -> p k c", k=KC)
            for mc in range(2):
                bk = next_banks(1)
                mms = [(ps[bk[0]][:, :], MN[:, kc, mc * 128:(mc + 1) * 128], W[:, kc, :], kc == 0, kc == KC - 1)
                       for kc in range(KC)]
                pe_task(mms, [("w", s), ("mn",)], bk)
                job(DVE, [lambda e, o=VT[:, mc, fb * 512:(fb + 1) * 512], i=ps[bk[0]][:, :]: e.tensor_copy(out=o, in_=i)],
                    reads=[("ps", bk[0])], writes=[("vt", mc, fb)])
        barrier()
        qscale = 512.0 ** -0.5
        pslot = [0]
        for b, (t0, n) in enumerate(blocks):
            norm_block(gi, t0, n, Hb[:, :, :n], ("h", 0))
            for pk in range(4):
                s = load_piece(wq_d[l][pk], SLOT)
                W = ring[:, s, :].rearrange("p (k c) -> p k c", k=KC)
                for ff in range(4):
                    fc = pk * 4 + ff
                    bk = next_banks(1)
                    mms = [(ps[bk[0]][:, :n], W[:, kc, ff * 128:(ff + 1) * 128], Hb[:, kc, :n], kc == 0, kc == KC - 1)
                           for kc in range(KC)]
                    pe_task(mms, [("w", s), ("h", 0)], bk)
                    job(ACT, [act_fn(Qb[:, fc, :n], ps[bk[0]][:, :n], AF.Copy, scale=qscale)],
                        reads=[("ps", bk[0])], writes=[("q", fc)])
            def s_tasks(hd):
                sp_ = hd % 2
                for mc in range(2):
                    bk = next_banks(1)
                    mms = [(ps[bk[0]][:, :n], KT[:, hd * 4 + dc, mc * 128:(mc + 1) * 128], Qb[:, hd * 4 + dc, :n],
                            dc == 0, dc == 3) for dc in range(4)]
                    pe_task(mms, [("kt", hd * 4 + dc) for dc in range(4)] + [("q", hd * 4 + dc) for dc in range(4)], bk)
                    job(ACT, [act_fn(PT[:, sp_, mc, :n], ps[bk[0]][:, :n], AF.Exp)],
                        reads=[("ps", bk[0])], writes=[("pt", sp_, mc)])

            s_tasks(0)
            for hd in range(4):
                sp_ = hd % 2
                if hd < 3:
                    s_tasks(hd + 1)
                bk = next_banks(1)
                mms = [(ps[bk[0]][:, :n], ones[:, :], PT[:, sp_, mc, :n], mc == 0, mc == 1) for mc in range(2)]
                pe_task(mms, [("ones",), ("pt", sp_, 0), ("pt", sp_, 1)], bk)
                job(DVE, [lambda e, o=RS[:, :n], i=ps[bk[0]][:, :n]: e.reciprocal(out=o, in_=i)],
                    reads=[("ps", bk[0])], writes=[("rs",)])
                for dc in range(4):
                    fc = hd * 4 + dc
                    bk = next_banks(1)
                    mms = [(ps[bk[0]][:, :n], VT[:, mc, fc * 128:(fc + 1) * 128], PT[:, sp_, mc, :n], mc == 0, mc == 1)
                           for mc in range(2)]
                    pe_task(mms, [("vt", mc, fc // 4) for mc in range(2)] + [("pt", sp_, 0), ("pt", sp_, 1)], bk)
                    job(DVE, [tt(Ob[:, fc, :n], ps[bk[0]][:, :n], RS[:, :n], ALU.mult)],
                        reads=[("ps", bk[0]), ("rs",)], writes=[("o", fc)])
            proj_out(wo_d[l], [(t0, n)], t0, Ob, [("o", fc) for fc in range(KC)])
        return

    def conv(gi):
        barrier()
        tb = 126
        H = pv(0, BF16, [KC, 1028])
        G = pv(32896, BF16, [KC, 1024])
        ACC = pv(65664, F32, [1024])
        Z = actT
        for bi, (t0, n) in enumerate(CB):
            norm_block(gi, t0, n, H[:, :, t0 - tb:t0 - tb + n], ("h", bi))
        hk = [("h", bi) for bi in range(3)]
        for fc in range(KC):
            s = load_piece(cin_d[fc], KC * 384)
            W = ring[:, s, :KC * 384].rearrange("p (k c) -> p k c", k=KC)
            cb_ = next_banks(3)
            vb_ = next_banks(3)
            for banks_, col0 in ((cb_, 128), (vb_, 256)):
                mms = []
                for kc in range(KC):
                    for bi, (t0, n) in enumerate(CB):
                        mms.append((ps[banks_[bi]][:, :n], W[:, kc, col0:col0 + 128], H[:, kc, t0 - tb:t0 - tb + n],
                                    kc == 0, kc == KC - 1))
                pe_task(mms, [("w", s)] + hk, banks_)
            for bi, (t0, n) in enumerate(CB):
                job(ACT, [act_fn(Z[:, t0 - tb:t0 - tb + n], ps[cb_[bi]][:, :n], AF.Copy)],
                    reads=[("ps", cb_[bi])], writes=[("z", bi)])
                job(DVE, [tt(Z[:, t0 - tb:t0 - tb + n], ps[vb_[bi]][:, :n], Z[:, t0 - tb:t0 - tb + n], ALU.mult)],
                    reads=[("ps", vb_[bi]), ("z", bi)], writes=[("z", bi)])
            job(DVE, [ts(Z[:, 0:2], Z[:, 0:2], hmask[:, 0:1], None, ALU.mult)], reads=[("z", 0), ("hmask",)], writes=[("z", 0)])
            zk = [("z", bi) for bi in range(3)]
            job(ACT, [act_fn(ACC, Z[:, 2:1026], AF.Copy, scale=cw[:, fc, 2:3])], reads=zk + [("cw",)], writes=[("acc",)])
            job(DVE, [stt(ACC, Z[:, 1:1025], cw[:, fc, 1:2], ACC, ALU.mult, ALU.add)], reads=zk + [("acc",), ("cw",)], writes=[("acc",)])
            job(DVE, [stt(ACC, Z[:, 0:1024], cw[:, fc, 0:1], ACC, ALU.mult, ALU.add)], reads=zk + [("acc",), ("cw",)], writes=[("acc",)])
            bb_ = next_banks(len(BL2))
            mms = []
            for kc in range(KC):
                for bi, (t0, n) in enumerate(BL2):
                    mms.append((ps[bb_[bi]][:, :n], W[:, kc, 0:128], H[:, kc, t0 - tb:t0 - tb + n], kc == 0, kc == KC - 1))
            pe_task(mms, [("w", s)] + hk, bb_)
            for bi, (t0, n) in enumerate(BL2):
                job(DVE, [tt(G[:, fc, t0 - 128:t0 - 128 + n], ps[bb_[bi]][:, :n], ACC[:, t0 - 128:t0 - 128 + n], ALU.mult)],
                    reads=[("ps", bb_[bi]), ("acc",)], writes=[("g", fc, bi)])
        for q in range(4):
            s = load_piece(co_d[q], SLOT)
            W = ring[:, s, :].rearrange("p (k c) -> p k c", k=KC)
            for dd in range(4):
                d = q * 4 + dd
                bk = next_banks(len(BL2))
                mms = []
                for kc in range(KC):
                    for bi, (t0, n) in enumerate(BL2):
                        mms.append((ps[bk[bi]][:, :n], W[:, kc, dd * 128:(dd + 1) * 128], G[:, kc, t0 - 128:t0 - 128 + n],
                                    kc == 0, kc == KC - 1))
                pe_task(mms, [("w", s)] + [("g", fc, bi) for fc in range(KC) for bi in range(len(BL2))], bk)
                for bi, (t0, n) in enumerate(BL2):
                    resid_evac(d, t0, n, bk[bi], 1.0)

    def final():
        barrier()
        for bi, (t0, n) in enumerate(BL2):
            Yb = pv((t0 - 128) * KC * 4, F32, [KC, n])
            norm_block(10, t0, n, Yb, ("y", bi))
            dma(SP, out_d[:, :, t0 - 128:t0 - 128 + n], Yb, st_sem, st_cnt, 0, reads=[("y", bi)])

    stages = [
        lambda: ffn(0, 0, 0, BL3, 0),
        lambda: gmlp(1),
        lambda: xattn(0, 2, 3, BX),
        lambda: ffn(0, 1, 4, BX, 126),
        lambda: ffn(1, 0, 5, BX, 126),
        lambda: conv(6),
        lambda: xattn(1, 7, 8, XB1),
        lambda: ffn(1, 1, 9, BY, 128),
        lambda: final(),
    ]
    for st in stages[:nstages]:
        st()
    if dbg:
        barrier()
        allx = [("x", c, ch) for c in range(KC) for ch in range(9)]
        dma(SP, out_d, xT[:, :, :], st_sem, st_cnt, 0, reads=allx)
    SP.ops.append(("wait", st_sem, st_cnt[0]))

    with nc.Block() as block:
        @block.tensor
        def _(e):
            PE.replay(e)

        @block.scalar
        def _(e):
            ACT.replay(e)

        @block.vector
        def _(e):
            DVE.replay(e)

        @block.gpsimd
        def _(e):
            POOL.replay(e)

        @block.sync
        def _(e):
            SP.replay(e)
    es.close()
    return nc


def _pieces(W, ncols):
    K, C = W.shape
    npc = C // ncols
    return np.ascontiguousarray(W.reshape(K // 128, 128, npc, ncols).transpose(2, 1, 0, 3)).reshape(npc, 128, (K // 128) * ncols)


def _prep(inputs):
    f = lambda a: np.asarray(a, dtype=np.float32)
    x = f(inputs["x"])[0]
    mem = f(inputs["mem"])[0]
    shared = {}
    shared["memT"] = np.ascontiguousarray(mem.T)
    gl = []
    for l in range(2):
        for nm in ("ffn1_norm", "mix_norm", "xattn_norm", "mem_norm", "ffn2_norm"):
            gl.append(f(inputs[nm])[l])
    gl.append(f(inputs["final_norm"]))
    g = np.stack(gl, 0)
    shared["gains"] = np.ascontiguousarray(g.reshape(11, 16, 128).transpose(2, 0, 1)).reshape(128, 11 * 16)
    for l in range(2):
        for fi, nm in enumerate(("ffn1", "ffn2")):
            w13 = f(inputs[nm + "_w13"])[l]
            gate = w13[:, :5632].reshape(16, 128, 22, 2, 1, 128)
            up = w13[:, 5632:].reshape(16, 128, 22, 2, 1, 128)
            cat = np.concatenate([gate, up], axis=4)
            shared[f"w13_{l}_{fi}"] = np.ascontiguousarray(cat.transpose(2, 1, 0, 3, 4, 5)).reshape(22, 128, SLOT)
            w2 = f(inputs[nm + "_w2"])[l]
            arr = np.zeros((16, 128, 12 * 512), np.float32)
            for gi_, (j0, ng) in enumerate(GROUPS):
                blk = w2[j0 * 128:(j0 + ng) * 128].reshape(ng, 128, 4, 512)
                arr[gi_ * 4:gi_ * 4 + 4, :, :ng * 512] = blk.transpose(2, 1, 0, 3).reshape(4, 128, ng * 512)
            shared[f"w2_{l}_{fi}"] = arr
        shared[f"wq_{l}"] = _pieces(f(inputs["xattn_wq"])[l], 512)
        wkv = f(inputs["xattn_wkv"])[l]
        shared[f"wk_{l}"] = _pieces(wkv[:, :D], 512)
        shared[f"wv_{l}"] = _pieces(wkv[:, D:], 512)
        shared[f"wo_{l}"] = _pieces(f(inputs["xattn_wo"])[l], 512)
    win = f(inputs["gmlp_w_in"])[0]
    shared["gmlp_u"] = _pieces(win[:, :D], 512)
    shared["gmlp_v"] = _pieces(win[:, D:], 512)
    shared["gmlp_o"] = _pieces(f(inputs["gmlp_w_out"])[0], 512)
    shared["ln_g"] = f(inputs["gmlp_ln_g"])[0].reshape(1, D)
    shared["ln_b"] = f(inputs["gmlp_ln_b"])[0].reshape(1, D)
    ws = f(inputs["gmlp_w_s"])[0]
    shared["wsT"] = np.ascontiguousarray(ws.transpose(2, 0, 1)).reshape(128, 8 * 128)
    shared["trilT"] = np.ascontiguousarray(np.tril(np.ones((128, 128), np.float32)).T)
    shared["b_s"] = f(inputs["gmlp_b_s"])[0].reshape(1, 8 * 128)
    cin = f(inputs["conv_w_in"])[0]
    c3 = cin.reshape(16, 128, 3, 16, 128)
    shared["conv_in"] = np.ascontiguousarray(c3.transpose(3, 1, 0, 2, 4)).reshape(16, 128, 16 * 384)
    cwv = f(inputs["conv_w"])[0]
    shared["conv_w"] = np.ascontiguousarray(cwv.reshape(3, 16, 128).transpose(2, 1, 0)).reshape(128, 48)
    shared["conv_o"] = _pieces(f(inputs["conv_w_out"])[0], 512)
    in_maps = []
    for i in range(NCORES):
        xs = np.zeros((NT, D), np.float32)
        if i > 0:
            xs[:] = x[i * TOWN - HALO:(i + 1) * TOWN]
        else:
            xs[HALO:] = x[:TOWN]
        m = dict(shared)
        m["xT"] = np.ascontiguousarray(xs.T)
        m["hmask"] = np.full((128, 2), 1.0 if i > 0 else 0.0, np.float32)
        in_maps.append(m)
    return in_maps


_NC_CACHE = {}


def kernel(**inputs):
    in_maps = _prep(inputs)
    if "nc" not in _NC_CACHE:
        _NC_CACHE["nc"] = build()
    nc = _NC_CACHE["nc"]
    res = run_bass_kernel_spmd(nc, in_maps, core_ids=list(range(NCORES)))
    outs = [np.asarray(r["outT"]).T for r in res.results]
    return np.concatenate(outs, axis=0).reshape(1, NCORES * TOWN, D).astype(np.float32)
```

```python
import contextlib
import numpy as np
import concourse.bass as bass
import concourse.mybir as mybir
from concourse.bass_utils import run_bass_kernel_spmd

F32 = mybir.dt.float32
BF16 = mybir.dt.bfloat16
AF = mybir.ActivationFunctionType
ALU = mybir.AluOpType
AX = mybir.AxisListType

D = 2048
KC = 16
HCH = 44
MEM = 256
NCORES = 8
TOWN = 1024
HALO = 128
NT = TOWN + HALO
SLOT = 8192
NSLOT = 3
GROUPS = [(0, 12), (12, 12), (24, 12), (36, 8)]
BL3 = [(0, 384), (384, 384), (768, 384)]
BX = [(126, 386), (512, 384), (896, 256)]
BY = [(128, 384), (512, 384), (896, 256)]
XB1 = [(128, 512), (640, 512)]
BL2 = BY
CB = BX
NSTAGES_ALL = 9


class Tok:
    __slots__ = ("sem", "val")

    def __init__(self, sem, val):
        self.sem, self.val = sem, val


class Eng:
    def __init__(self, name, sem):
        self.name, self.sem, self.cnt, self.ops, self.seen = name, sem, 0, [], {}

    def wait(self, toks, skip_own=False):
        for t in toks:
            if t is None:
                continue
            if skip_own and t.sem is self.sem:
                continue
            k = id(t.sem)
            if self.seen.get(k, 0) >= t.val:
                continue
            self.seen[k] = t.val
            self.ops.append(("wait", t.sem, t.val))

    def emit(self, fn, signal=False):
        if signal:
            self.cnt += 1
            self.ops.append(("sig", fn))
            return Tok(self.sem, self.cnt)
        self.ops.append(("op", fn))
        return None

    def replay(self, e):
        for op in self.ops:
            if op[0] == "wait":
                e.wait_ge(op[1], op[2])
            elif op[0] == "op":
                op[1](e)
            elif op[0] == "sig":
                op[1](e).then_inc(self.sem, 1)
            elif op[0] == "dma":
                kw = op[4]
                e.dma_start(out=op[1], in_=op[2], **kw).then_inc(op[3], 16)


class Tracker:
    def __init__(self):
        self.w, self.r = {}, {}

    def deps_read(self, keys):
        return [self.w.get(k) for k in keys]

    def deps_write(self, keys):
        out = []
        for k in keys:
            out.append(self.w.get(k))
            out.extend(self.r.get(k, {}).values())
        return out

    def did_read(self, keys, tok):
        for k in keys:
            d = self.r.setdefault(k, {})
            cur = d.get(id(tok.sem))
            if cur is None or cur.val < tok.val:
                d[id(tok.sem)] = tok

    def did_write(self, keys, tok):
        for k in keys:
            self.w[k] = tok
            self.r[k] = {}


def build(nstages=NSTAGES_ALL, dbg=False, dbgopts=None):
    dbgopts = dbgopts or {}
    nc = bass.Bass("TRN2", target_bir_lowering=False)
    es = contextlib.ExitStack()

    def din(name, shape):
        return nc.dram_tensor(name, list(shape), F32, kind="ExternalInput").ap()

    xT_d = din("xT", [D, NT]).rearrange("(c p) t -> p c t", p=128)
    hmask_d = din("hmask", [128, 2])
    memT_d = din("memT", [D, MEM]).rearrange("(c p) t -> p c t", p=128)
    gains_d = din("gains", [128, 11 * 16])
    w13_d = {(l, f): din(f"w13_{l}_{f}", [22, 128, SLOT]) for l in range(2) for f in range(2)}
    w2_d = {(l, f): din(f"w2_{l}_{f}", [16, 128, 12 * 512]) for l in range(2) for f in range(2)}
    gv_d = din("gmlp_v", [4, 128, SLOT])
    gu_d = din("gmlp_u", [4, 128, SLOT])
    go_d = din("gmlp_o", [4, 128, SLOT])
    lng_d = din("ln_g", [1, D])
    lnb_d = din("ln_b", [1, D])
    wsT_d = din("wsT", [128, 8 * 128])
    trilT_d = din("trilT", [128, 128])
    bs_d = din("b_s", [1, 8 * 128])
    cin_d = din("conv_in", [16, 128, 16 * 384])
    cw_d = din("conv_w", [128, 16 * 3])
    co_d = din("conv_o", [4, 128, SLOT])
    wq_d = [din(f"wq_{l}", [4, 128, SLOT]) for l in range(2)]
    wk_d = [din(f"wk_{l}", [4, 128, SLOT]) for l in range(2)]
    wv_d = [din(f"wv_{l}", [4, 128, SLOT]) for l in range(2)]
    wo_d = [din(f"wo_{l}", [4, 128, SLOT]) for l in range(2)]
    if dbg:
        out_d = nc.dram_tensor("dbg", [D, NT], F32, kind="ExternalOutput").ap().rearrange("(c p) t -> p c t", p=128)
    else:
        out_d = nc.dram_tensor("outT", [D, TOWN], F32, kind="ExternalOutput").ap().rearrange("(c p) t -> p c t", p=128)

    def sb(name, shape, dt):
        return es.enter_context(nc.sbuf_tensor(name, list(shape), dt))

    xT = sb("xT_sb", [128, KC, NT], F32)
    P = sb("P_sb", [128, 36864], BF16)
    ring = sb("ring_sb", [128, NSLOT, SLOT], BF16)
    sq = sb("sq_sb", [128, 4, 512], BF16)
    std = sb("std_sb", [128, 2, 512], F32)
    actT = sb("act_sb", [128, 1160], F32)
    ones = sb("ones_sb", [128, 128], BF16)
    e0 = sb("e0_sb", [128, 128], BF16)
    gains = sb("gains_sb", [128, 11, 16], F32)
    eps6 = sb("eps6_sb", [128, 1], F32)
    eps5 = sb("eps5_sb", [128, 1], F32)
    cw = sb("cw_sb", [128, 16, 3], F32)
    hmask = sb("hmask_sb", [128, 2], F32)
    lnst = sb("lnst_sb", [128, 64], F32)

    ps = [es.enter_context(nc.psum_tensor(f"ps{i}", [128, 512], F32)) for i in range(8)]

    def pv(off, dt, shape):
        n = int(np.prod(shape))
        if dt is BF16:
            assert off % 2 == 0
            ap = P[:, off // 2: off // 2 + n]
        else:
            assert off % 4 == 0
            ap = P[:, off // 2: off // 2 + 2 * n].bitcast(F32)
        assert off + n * (2 if dt is BF16 else 4) <= 36864 * 2
        if len(shape) == 2:
            ap = ap.rearrange("p (a b) -> p a b", a=shape[0])
        elif len(shape) == 3:
            ap = ap.rearrange("p (a b c) -> p a b c", a=shape[0], b=shape[1])
        return ap

    def sem(name):
        return es.enter_context(nc.semaphore(name))

    PE = Eng("pe", sem("s_pe"))
    ACT = Eng("act", sem("s_act"))
    DVE = Eng("dve", sem("s_dve"))
    POOL = Eng("pool", sem("s_pool"))
    SP = Eng("sp", sem("s_sp"))
    slot_sems = [sem(f"s_slot{i}") for i in range(NSLOT)]
    slot_cnt = [0] * NSLOT
    ld_sem = sem("s_ld")
    ld_cnt = [0]
    st_sem = sem("s_st")
    st_cnt = [0]
    TR = Tracker()

    def job(eng, fns, reads=(), writes=()):
        reads, writes = list(reads), list(writes)
        eng.wait(TR.deps_read(reads) + TR.deps_write(writes), skip_own=(eng is PE))
        for fn in fns[:-1]:
            eng.emit(fn)
        tok = eng.emit(fns[-1], signal=True)
        TR.did_read(reads, tok)
        TR.did_write(writes, tok)
        return tok

    def dma(q, out_ap, in_ap, semh, cnt, idx, reads=(), writes=(), **kw):
        reads, writes = list(reads), list(writes)
        q.wait(TR.deps_read(reads) + TR.deps_write(writes))
        cnt[idx] += 16
        q.ops.append(("dma", out_ap, in_ap, semh, kw))
        tok = Tok(semh, cnt[idx])
        TR.did_read(reads, tok)
        TR.did_write(writes, tok)
        return tok

    def barrier():
        toks = [Tok(e.sem, e.cnt) for e in (PE, ACT, DVE) if e.cnt > 0]
        for e in (PE, ACT, DVE, SP):
            e.wait(toks)

    def dump(name, ap, dt):
        barrier()
        shp = list(ap.shape)
        dd = nc.dram_tensor(name, shp, dt, kind="ExternalOutput").ap()
        SP.wait([Tok(e.sem, e.cnt) for e in (PE, ACT, DVE) if e.cnt > 0])
        st_cnt[0] += 16
        SP.ops.append(("dma", dd, ap, st_sem, {}))

    slot_next = [0]

    def load_piece(src_ap, nel):
        s = slot_next[0]
        slot_next[0] = (s + 1) % NSLOT
        nb = nel // 2048
        dma(POOL, ring[:, s, :nel].rearrange("p (a b) -> p a b", a=nb),
            src_ap.rearrange("p (a b) -> p a b", a=nb),
            slot_sems[s], slot_cnt, s, writes=[("w", s)])
        return s

    bank_next = [0]

    def next_banks(n):
        out = []
        for _ in range(n):
            out.append(bank_next[0])
            bank_next[0] = (bank_next[0] + 1) % 6
        return out

    stat_next = [0]
    sq_next = [0]
    std_next = [0]

    def xkeys(c, t0, n):
        return [("x", c, ch) for ch in range(t0 // 128, (t0 + n - 1) // 128 + 1)]

    def mm(o, l, r, st, sp):
        return lambda e: e.matmul(o, l, r, start=st, stop=sp)

    def pe_task(mms, reads, banks):
        return job(PE, [mm(*m) for m in mms], reads=reads, writes=[("ps", b) for b in banks])

    def act_fn(out, in_, func, **kw):
        return lambda e: e.activation(out=out, in_=in_, func=func, **kw)

    def tt(out, in0, in1, op):
        return lambda e: e.tensor_tensor(out=out, in0=in0, in1=in1, op=op)

    def stt(out, in0, scalar, in1, op0, op1):
        return lambda e: e.scalar_tensor_tensor(out=out, in0=in0, scalar=scalar, in1=in1, op0=op0, op1=op1)

    def ts(out, in0, s1, s2, op0, op1=None):
        if op1 is None:
            return lambda e: e.tensor_scalar(out=out, in0=in0, scalar1=s1, scalar2=None, op0=op0)
        return lambda e: e.tensor_scalar(out=out, in0=in0, scalar1=s1, scalar2=s2, op0=op0, op1=op1)

    xl_sems = [sem(f"s_xl{i}") for i in range(3)]
    xl_cnt = [0, 0, 0]
    for i, (t0, n) in enumerate(BL3):
        dma(SP, xT[:, :, t0:t0 + n], xT_d[:, :, t0:t0 + n], xl_sems[i], xl_cnt, i,
            writes=[("x", c, ch) for c in range(KC) for ch in range(3 * i, 3 * i + 3)])
    xkeep = dict(TR.w)
    dma(SP, gains[:, :, :].rearrange("p a b -> p (a b)"), gains_d, ld_sem, ld_cnt, 0, writes=[("gains",)])
    dma(SP, cw[:, :, :].rearrange("p a b -> p (a b)"), cw_d, ld_sem, ld_cnt, 0, writes=[("cw",)])
    tok_ld = dma(SP, hmask[:, :], hmask_d, ld_sem, ld_cnt, 0, writes=[("hmask",)])
    for k in list(TR.w.keys()):
        if k not in xkeep:
            TR.w[k] = tok_ld
    job(DVE, [lambda e: e.memset(ones[:, :], 1.0)], writes=[("ones",)])
    job(DVE, [lambda e: e.memset(e0[:, :], 0.0)], writes=[("e0",)])
    job(DVE, [lambda e: e.memset(e0[0:1, :], 1.0)], writes=[("e0",)])
    job(DVE, [lambda e: e.memset(eps6[:, :], 1e-6)], writes=[("eps",)])
    job(DVE, [lambda e: e.memset(eps5[:, :], 1e-5)], writes=[("eps",)])

    def norm_begin():
        sbk = 6 + stat_next[0]
        stat_next[0] ^= 1
        return sbk

    def norm_sq_act(c, t0, n):
        s = sq_next[0]
        sq_next[0] = (s + 1) % 4
        job(ACT, [act_fn(sq[:, s, :n], xT[:, c, t0:t0 + n], AF.Square)],
            reads=xkeys(c, t0, n), writes=[("sq", s)])
        return s

    def norm_sq_pe(sbk, c, s, n):
        job(PE, [mm(ps[sbk][:, :n], ones[:, :], sq[:, s, :n], c == 0, c == KC - 1)],
            reads=[("sq", s), ("ones",)], writes=[("ps", sbk)])

    def norm_sq_pair(sbk, c, t0, n):
        norm_sq_pe(sbk, c, norm_sq_act(c, t0, n), n)

    def norm_end(sbk, n):
        sl = std_next[0]
        std_next[0] ^= 1
        job(ACT, [act_fn(std[:, sl, :n], ps[sbk][:, :n], AF.Sqrt, bias=eps6[:, 0:1], scale=1.0 / D)],
            reads=[("ps", sbk), ("eps",)], writes=[("std", sl)])
        job(DVE, [lambda e: e.reciprocal(out=std[:, sl, :n], in_=std[:, sl, :n])],
            reads=[("std", sl)], writes=[("std", sl)])
        return sl

    def norm_apply(gi, t0, n, sl, hdst, hkey):
        fns = [stt(hdst[:, c, :], xT[:, c, t0:t0 + n], gains[:, gi, c:c + 1], std[:, sl, :n], ALU.mult, ALU.mult)
               for c in range(KC)]
        rk = [("std", sl), ("gains",)]
        for c in range(KC):
            rk += xkeys(c, t0, n)
        job(DVE, fns, reads=rk, writes=[hkey])

    def norm_block(gi, t0, n, hdst, hkey, f32out=False):
        sbk = norm_begin()
        for c in range(KC):
            norm_sq_pair(sbk, c, t0, n)
        sl = norm_end(sbk, n)
        norm_apply(gi, t0, n, sl, hdst, hkey)

    act_next = [0]

    def act_slot(n):
        if n <= 386:
            a = act_next[0] % 3
            act_next[0] = (a + 1) % 3
            return a, actT[:, a * 386: a * 386 + n]
        a = act_next[0] % 2
        act_next[0] = (a + 1) % 2
        return a, actT[:, a * 512: a * 512 + n]

    def resid_evac(d, t0, n, bank, scale):
        if scale == 1.0:
            fn = tt(xT[:, d, t0:t0 + n], ps[bank][:, :n], xT[:, d, t0:t0 + n], ALU.add)
        else:
            fn = stt(xT[:, d, t0:t0 + n], ps[bank][:, :n], scale, xT[:, d, t0:t0 + n], ALU.mult, ALU.add)
        job(DVE, [fn], reads=[("ps", bank)] + xkeys(d, t0, n), writes=xkeys(d, t0, n))

    def proj_out(w_d, blocks, tb, G, gkeys):
        for q in range(4):
            s = load_piece(w_d[q], SLOT)
            W = ring[:, s, :].rearrange("p (k c) -> p k c", k=KC)
            for dd in range(4):
                d = q * 4 + dd
                bk = next_banks(len(blocks))
                mms = []
                for kc in range(KC):
                    for bi, (t0, n) in enumerate(blocks):
                        mms.append((ps[bk[bi]][:, :n], W[:, kc, dd * 128:(dd + 1) * 128],
                                    G[:, kc, t0 - tb:t0 - tb + n], kc == 0, kc == KC - 1))
                pe_task(mms, [("w", s)] + list(gkeys), bk)
                for bi, (t0, n) in enumerate(blocks):
                    resid_evac(d, t0, n, bk[bi], 1.0)

    def ffn(l, f, gi, blocks, tb):
        ntl = sum(n for _, n in blocks)
        barrier()
        H = pv(0, BF16, [KC, ntl])
        HID = pv(KC * ntl * 2, BF16, [12, ntl])
        nb = len(blocks)
        for bi, (t0, n) in enumerate(blocks):
            norm_block(gi, t0, n, H[:, :, t0 - tb:t0 - tb + n], ("h", bi))
        hk = [("h", bi) for bi in range(nb)]
        for g, (j0, ng) in enumerate(GROUPS):
            for pk in range(j0 // 2, (j0 + ng) // 2):
                s = load_piece(w13_d[(l, f)][pk], SLOT)
                W = ring[:, s, :].rearrange("p (k c) -> p k c", k=KC)
                for jj in range(2):
                    jl = 2 * pk + jj - j0
                    gb = next_banks(nb)
                    ub = next_banks(nb)
                    for banks_, col0 in ((gb, jj * 256), (ub, jj * 256 + 128)):
                        for bi, (t0, n) in enumerate(blocks):
                            mms = [(ps[banks_[bi]][:, :n], W[:, kc, col0:col0 + 128],
                                    H[:, kc, t0 - tb:t0 - tb + n], kc == 0, kc == KC - 1) for kc in range(KC)]
                            pe_task(mms, [("w", s), ("h", bi)], [banks_[bi]])
                    for bi, (t0, n) in enumerate(blocks):
                        a, aT = act_slot(n)
                        job(ACT, [act_fn(aT, ps[gb[bi]][:, :n], AF.Silu)], reads=[("ps", gb[bi])], writes=[("act", a)])
                        job(DVE, [tt(HID[:, jl, t0 - tb:t0 - tb + n], ps[ub[bi]][:, :n], aT, ALU.mult)],
                            reads=[("ps", ub[bi]), ("act", a)], writes=[("hid", jl, bi)])
            for q in range(4):
                s = load_piece(w2_d[(l, f)][g * 4 + q][:, :ng * 512], ng * 512)
                W2 = ring[:, s, :ng * 512].rearrange("p (k c) -> p k c", k=ng)
                for dd in range(4):
                    d = q * 4 + dd
                    bk = next_banks(nb)
                    for bi, (t0, n) in enumerate(blocks):
                        mms = [(ps[bk[bi]][:, :n], W2[:, hc, dd * 128:(dd + 1) * 128],
                                HID[:, hc, t0 - tb:t0 - tb + n], hc == 0, hc == ng - 1) for hc in range(ng)]
                        pe_task(mms, [("w", s)] + [("hid", hc, bi) for hc in range(ng)], [bk[bi]])
                        resid_evac(d, t0, n, bk[bi], 0.5)

    def gmlp(gi):
        barrier()
        Hb = pv(0, BF16, [KC, 384])
        Vtm = pv(12288, BF16, [3, D])
        Gb = pv(24576, BF16, [KC, 384])
        LNG = pv(36864, F32, [D])
        LNB = pv(45056, F32, [D])
        WsT = pv(53248, BF16, [8, 128])
        BSH = pv(55296, BF16, [8, 384])
        BSL = pv(61440, BF16, [8, 384])
        LNS = pv(67584, F32, [2, 512])
        WsF = actT[:, 0:1024].rearrange("p (a b) -> p a b", a=8)
        TrF = actT[:, 1024:1152]
        BsF = pv(67584, F32, [8, 128])
        ak = [("act", a) for a in range(3)]
        lk = [("lns", 0), ("lns", 1)]
        t1 = dma(SP, LNG, lng_d.partition_broadcast(128), ld_sem, ld_cnt, 0, writes=[("lng",)])
        t2 = dma(SP, LNB, lnb_d.partition_broadcast(128), ld_sem, ld_cnt, 0, writes=[("lnb",)])
        t3 = dma(SP, WsF, wsT_d.rearrange("p (a b) -> p a b", a=8), ld_sem, ld_cnt, 0, writes=[("wsf",)] + ak)
        t4 = dma(SP, TrF, trilT_d, ld_sem, ld_cnt, 0, writes=[("trf",)] + ak)
        t5 = dma(SP, BsF[0:1].rearrange("p a b -> p (a b)"), bs_d, ld_sem, ld_cnt, 0, writes=[("bsf",)] + lk)
        for k in [("lng",), ("lnb",), ("wsf",), ("trf",), ("bsf",)] + ak + lk:
            TR.w[k] = t5
        norm_block(gi, BL3[0][0], BL3[0][1], Hb, ("h", 0))
        job(DVE, [tt(WsT[:, g, :], WsF[:, g, :], TrF, ALU.mult) for g in range(8)],
            reads=[("wsf",), ("trf",)] + ak, writes=[("wst",)])
        job(DVE, [lambda e: e.memset(BSH.rearrange("p a b -> p (a b)"), 0.0),
                  lambda e: e.memset(BSL.rearrange("p a b -> p (a b)"), 0.0)], writes=[("bsh",), ("bsl",)])
        job(DVE, [(lambda e, g=g, r=r: e.tensor_copy(out=BSH[0:1, g, r * 128:(r + 1) * 128], in_=BsF[0:1, g, :]))
                  for g in range(8) for r in range(3)], reads=[("bsf",)] + lk, writes=[("bsh",)])
        job(DVE, [tt(BSL[0:1, g, r * 128:(r + 1) * 128], BsF[0:1, g, :], BSH[0:1, g, r * 128:(r + 1) * 128], ALU.subtract)
                  for g in range(8) for r in range(3)], reads=[("bsf",), ("bsh",)] + lk, writes=[("bsl",)])
        UB, FB = [0, 1, 2, 3], [4, 5]
        for b, (t0, n) in enumerate(BL3):
            nxt = BL3[b + 1] if b + 1 < len(BL3) else None
            sbk_n = norm_begin() if nxt else None
            for fb in range(4):
                s = load_piece(gv_d[fb], SLOT)
                W = ring[:, s, :].rearrange("p (k c) -> p k c", k=KC)
                for ch in range(3):
                    bk = next_banks(1)
                    mms = [(ps[bk[0]][:, :], Hb[:, kc, ch * 128:(ch + 1) * 128], W[:, kc, :], kc == 0, kc == KC - 1)
                           for kc in range(KC)]
                    pe_task(mms, [("w", s), ("h", 0)], bk)
                    col = ch * 4 + fb
                    job(ACT, [act_fn(Vtm[:, ch, fb * 512:(fb + 1) * 512], ps[bk[0]][:, :], AF.Gelu,
                                     accum_out=lnst[:, col:col + 1])],
                        reads=[("ps", bk[0])], writes=[("v", ch, fb), ("s1", col)])
                    s_ = sq_next[0]
                    sq_next[0] = (s_ + 1) % 4
                    job(ACT, [act_fn(sq[:, s_, :], Vtm[:, ch, fb * 512:(fb + 1) * 512], AF.Square,
                                     accum_out=lnst[:, 12 + col:13 + col])],
                        reads=[("v", ch, fb)], writes=[("sq", s_), ("s2", col)])
            S1 = lnst[:, 0:12].rearrange("p (a b) -> p a b", a=3)
            S2 = lnst[:, 12:24].rearrange("p (a b) -> p a b", a=3)
            m1, m2, msq, var, sd, rstd, nmr = (lnst[:, 24 + 3 * i:27 + 3 * i] for i in range(7))
            allv = [("v", ch, fb) for ch in range(3) for fb in range(4)]
            job(DVE, [lambda e: e.tensor_reduce(out=m1, in_=S1, axis=AX.X, op=ALU.add),
                      lambda e: e.tensor_reduce(out=m2, in_=S2, axis=AX.X, op=ALU.add)],
                reads=[("s1", c_) for c_ in range(12)] + [("s2", c_) for c_ in range(12)], writes=[("m12",)])
            job(DVE, [ts(m1, m1, 1.0 / D, None, ALU.mult), ts(m2, m2, 1.0 / D, None, ALU.mult)],
                reads=[("m12",)], writes=[("m12",)])
            job(DVE, [tt(msq, m1, m1, ALU.mult)], reads=[("m12",)], writes=[("msq",)])
            job(DVE, [tt(var, m2, msq, ALU.subtract)], reads=[("m12",), ("msq",)], writes=[("var",)])
            job(ACT, [act_fn(sd, var, AF.Sqrt, bias=eps5[:, 0:1], scale=1.0)], reads=[("var",), ("eps",)], writes=[("sd",)])
            job(DVE, [lambda e: e.reciprocal(out=rstd, in_=sd)], reads=[("sd",)], writes=[("rstd",)])
            job(DVE, [stt(nmr, m1, -1.0, rstd, ALU.mult, ALU.mult)], reads=[("m12",), ("rstd",)], writes=[("nmr",)])
            for ch in range(3):
                for fb in range(4):
                    sl = (ch * 4 + fb) % 2
                    vs = Vtm[:, ch, fb * 512:(fb + 1) * 512]
                    job(ACT, [act_fn(LNS[:, sl, :], vs, AF.Identity, scale=rstd[:, ch:ch + 1], bias=nmr[:, ch:ch + 1])],
                        reads=[("v", ch, fb), ("rstd",), ("nmr",)], writes=[("lns", sl)])
                    job(DVE, [tt(LNS[:, sl, :], LNS[:, sl, :], LNG[:, fb * 512:(fb + 1) * 512], ALU.mult)],
                        reads=[("lns", sl), ("lng",)], writes=[("lns", sl)])
                    job(DVE, [tt(vs, LNS[:, sl, :], LNB[:, fb * 512:(fb + 1) * 512], ALU.add)],
                        reads=[("lns", sl), ("lnb",)], writes=[("v", ch, fb)])
            if dbgopts.get("gmlp") == "ln":
                dump("d_vtm", Vtm, BF16)
                dump("d_lnst", lnst[:, :], F32)
                dump("d_hb", Hb, BF16)
                return
            LA = 4
            wslot = {}

            def emit_u(fc):
                pk, ff = fc // 4, fc % 4
                if ff == 0:
                    wslot[pk] = load_piece(gu_d[pk], SLOT)
                s_ = wslot[pk]
                W_ = ring[:, s_, :].rearrange("p (k c) -> p k c", k=KC)
                ubk = UB[fc % 4]
                mms_ = [(ps[ubk][:, :n], W_[:, kc, ff * 128:(ff + 1) * 128], Hb[:, kc, :], kc == 0, kc == KC - 1)
                        for kc in range(KC)]
                pe_task(mms_, [("w", s_), ("h", 0)], [ubk])

            for fc in range(LA):
                emit_u(fc)
            for fc in range(KC):
                g = fc // 2
                ubk = UB[fc % 4]
                fbk = FB[fc % 2]
                mms = [(ps[fbk][:, :n], e0[:, :], BSH[:, g, :], True, False),
                       (ps[fbk][:, :n], e0[:, :], BSL[:, g, :], False, False)]
                for ch in range(3):
                    mms.append((ps[fbk][:, ch * 128:(ch + 1) * 128], Vtm[:, ch, fc * 128:(fc + 1) * 128],
                                WsT[:, g, :], False, ch == 2))
                pe_task(mms, [("e0",), ("bsh",), ("bsl",), ("wst",)] + [("v", ch, fc // 4) for ch in range(3)], [fbk])
                a, aT = act_slot(n)
                job(ACT, [act_fn(aT, ps[ubk][:, :n], AF.Gelu)], reads=[("ps", ubk)], writes=[("act", a)])
                job(DVE, [tt(Gb[:, fc, :], ps[fbk][:, :n], aT, ALU.mult)],
                    reads=[("ps", fbk), ("act", a)], writes=[("g", fc)])
                if fc + LA < KC:
                    emit_u(fc + LA)
                if nxt:
                    norm_sq_pair(sbk_n, fc, nxt[0], nxt[1])
            if nxt:
                sl_n = norm_end(sbk_n, nxt[1])
                norm_apply(gi, nxt[0], nxt[1], sl_n, Hb, ("h", 0))
            if dbgopts.get("gmlp") == "gate":
                dump("d_vtm", Vtm, BF16)
                dump("d_gb", Gb, BF16)
                dump("d_bsh", BSH, BF16)
                dump("d_bsl", BSL, BF16)
                dump("d_wst", WsT, BF16)
                return
            for q in range(4):
                s = load_piece(go_d[q], SLOT)
                W = ring[:, s, :].rearrange("p (k c) -> p k c", k=KC)
                for dd in range(4):
                    d = q * 4 + dd
                    bk = next_banks(1)
                    mms = [(ps[bk[0]][:, :n], W[:, kc, dd * 128:(dd + 1) * 128], Gb[:, kc, :], kc == 0, kc == KC - 1)
                           for kc in range(KC)]
                    pe_task(mms, [("w", s)] + [("g", fc) for fc in range(KC)], bk)
                    resid_evac(d, t0, n, bk[0], 1.0)

    def xattn(l, gi, gmi, blocks):
        barrier()
        nmax = max(n for _, n in blocks)
        Hb = pv(0, BF16, [KC, 512])
        Qb = pv(16384, BF16, [KC, 512])
        Ob = pv(32768, BF16, [KC, 512])
        KT = pv(49152, BF16, [KC, MEM])
        VT = pv(57344, BF16, [2, D])
        PT = pv(65536, BF16, [2, 2, 512])
        RS = pv(69632, F32, [512])
        MT = pv(0, F32, [KC, MEM])
        MN = pv(16384, BF16, [KC, MEM])
        tm = dma(SP, MT, memT_d, ld_sem, ld_cnt, 0, writes=[("mt",)])
        sbk = 6 + stat_next[0]
        stat_next[0] ^= 1
        for c in range(KC):
            s = sq_next[0]
            sq_next[0] = (s + 1) % 4
            job(ACT, [act_fn(sq[:, s, :MEM], MT[:, c, :], AF.Square)], reads=[("mt",)], writes=[("sq", s)])
            job(PE, [mm(ps[sbk][:, :MEM], ones[:, :], sq[:, s, :MEM], c == 0, c == KC - 1)],
                reads=[("sq", s), ("ones",)], writes=[("ps", sbk)])
        sl = std_next[0]
        std_next[0] ^= 1
        job(ACT, [act_fn(std[:, sl, :MEM], ps[sbk][:, :MEM], AF.Sqrt, bias=eps6[:, 0:1], scale=1.0 / D)],
            reads=[("ps", sbk), ("eps",)], writes=[("std", sl)])
        job(DVE, [lambda e: e.reciprocal(out=std[:, sl, :MEM], in_=std[:, sl, :MEM])], reads=[("std", sl)], writes=[("std", sl)])
        job(DVE, [stt(MN[:, c, :], MT[:, c, :], gains[:, gmi, c:c + 1], std[:, sl, :MEM], ALU.mult, ALU.mult)
                  for c in range(KC)], reads=[("std", sl), ("mt",), ("gains",)], writes=[("mn",)])
        for pk in range(4):
            s = load_piece(wk_d[l][pk], SLOT)
            W = ring[:, s, :].rearrange("p (k c) -> p k c", k=KC)
            for ff in range(4):
                fc = pk * 4 + ff
                bk = next_banks(1)
                mms = [(ps[bk[0]][:, :MEM], W[:, kc, ff * 128:(ff + 1) * 128], MN[:, kc, :], kc == 0, kc == KC - 1)
                       for kc in range(KC)]
                pe_task(mms, [("w", s), ("mn",)], bk)
                job(ACT, [act_fn(KT[:, fc, :], ps[bk[0]][:, :MEM], AF.Copy)], reads=[("ps", bk[0])], writes=[("kt", fc)])
        for fb in range(4):
            s = load_piece(wv_d[l][fb], SLOT)
            W = ring[:, s, :].rearrange("p (k c) -> p k c", k=KC)
            for mc in range(2):
                bk = next_banks(1)
                mms = [(ps[bk[0]][:, :], MN[:, kc, mc * 128:(mc + 1) * 128], W[:, kc, :], kc == 0, kc == KC - 1)
                       for kc in range(KC)]
                pe_task(mms, [("w", s), ("mn",)], bk)
                job(DVE, [lambda e, o=VT[:, mc, fb * 512:(fb + 1) * 512], i=ps[bk[0]][:, :]: e.tensor_copy(out=o, in_=i)],
                    reads=[("ps", bk[0])], writes=[("vt", mc, fb)])
        barrier()
        qscale = 512.0 ** -0.5
        pslot = [0]
        for b, (t0, n) in enumerate(blocks):
            if b == 0:
                norm_block(gi, t0, n, Hb[:, :, :n], ("h", 0))
            nxt = blocks[b + 1] if b + 1 < len(blocks) else None
            sbk_n = norm_begin() if nxt else None
            for pk in range(4):
                s = load_piece(wq_d[l][pk], SLOT)
                W = ring[:, s, :].rearrange("p (k c) -> p k c", k=KC)
                for ff in range(4):
                    fc = pk * 4 + ff
                    bk = next_banks(1)
                    mms = [(ps[bk[0]][:, :n], W[:, kc, ff * 128:(ff + 1) * 128], Hb[:, kc, :n], kc == 0, kc == KC - 1)
                           for kc in range(KC)]
                    pe_task(mms, [("w", s), ("h", 0)], bk)
                    job(ACT, [act_fn(Qb[:, fc, :n], ps[bk[0]][:, :n], AF.Copy, scale=qscale)],
                        reads=[("ps", bk[0])], writes=[("q", fc)])
                    if nxt:
                        if fc >= 1:
                            norm_sq_pe(sbk_n, fc - 1, sq_prev, nxt[1])
                        sq_prev = norm_sq_act(fc, nxt[0], nxt[1])
            if nxt:
                norm_sq_pe(sbk_n, KC - 1, sq_prev, nxt[1])
                sl_n = norm_end(sbk_n, nxt[1])
                norm_apply(gi, nxt[0], nxt[1], sl_n, Hb[:, :, :nxt[1]], ("h", 0))
            def s_tasks(hd):
                sp_ = hd % 2
                for mc in range(2):
                    bk = next_banks(1)
                    mms = [(ps[bk[0]][:, :n], KT[:, hd * 4 + dc, mc * 128:(mc + 1) * 128], Qb[:, hd * 4 + dc, :n],
                            dc == 0, dc == 3) for dc in range(4)]
                    pe_task(mms, [("kt", hd * 4 + dc) for dc in range(4)] + [("q", hd * 4 + dc) for dc in range(4)], bk)
                    job(ACT, [act_fn(PT[:, sp_, mc, :n], ps[bk[0]][:, :n], AF.Exp)],
                        reads=[("ps", bk[0])], writes=[("pt", sp_, mc)])

            s_tasks(0)
            for hd in range(4):
                sp_ = hd % 2
                if hd < 3:
                    s_tasks(hd + 1)
                bk = next_banks(1)
                mms = [(ps[bk[0]][:, :n], ones[:, :], PT[:, sp_, mc, :n], mc == 0, mc == 1) for mc in range(2)]
                pe_task(mms, [("ones",), ("pt", sp_, 0), ("pt", sp_, 1)], bk)
                job(DVE, [lambda e, o=RS[:, :n], i=ps[bk[0]][:, :n]: e.reciprocal(out=o, in_=i)],
                    reads=[("ps", bk[0])], writes=[("rs",)])
                for dc in range(4):
                    fc = hd * 4 + dc
                    bk = next_banks(1)
                    mms = [(ps[bk[0]][:, :n], VT[:, mc, fc * 128:(fc + 1) * 128], PT[:, sp_, mc, :n], mc == 0, mc == 1)
                           for mc in range(2)]
                    pe_task(mms, [("vt", mc, fc // 4) for mc in range(2)] + [("pt", sp_, 0), ("pt", sp_, 1)], bk)
                    job(DVE, [tt(Ob[:, fc, :n], ps[bk[0]][:, :n], RS[:, :n], ALU.mult)],
                        reads=[("ps", bk[0]), ("rs",)], writes=[("o", fc)])
            proj_out(wo_d[l], [(t0, n)], t0, Ob, [("o", fc) for fc in range(KC)])
        return

    def conv(gi):
        barrier()
        tb = 126
        H = pv(0, BF16, [KC, 1028])
        G = pv(32896, BF16, [KC, 1024])
        ACC = pv(65664, F32, [1024])
        Z = actT
        for bi, (t0, n) in enumerate(CB):
            norm_block(gi, t0, n, H[:, :, t0 - tb:t0 - tb + n], ("h", bi))
        hk = [("h", bi) for bi in range(3)]
        for fc in range(KC):
            s = load_piece(cin_d[fc], KC * 384)
            W = ring[:, s, :KC * 384].rearrange("p (k c) -> p k c", k=KC)
            cb_ = next_banks(3)
            vb_ = next_banks(3)
            for banks_, col0 in ((cb_, 128), (vb_, 256)):
                mms = []
                for kc in range(KC):
                    for bi, (t0, n) in enumerate(CB):
                        mms.append((ps[banks_[bi]][:, :n], W[:, kc, col0:col0 + 128], H[:, kc, t0 - tb:t0 - tb + n],
                                    kc == 0, kc == KC - 1))
                pe_task(mms, [("w", s)] + hk, banks_)
            for bi, (t0, n) in enumerate(CB):
                job(ACT, [act_fn(Z[:, t0 - tb:t0 - tb + n], ps[cb_[bi]][:, :n], AF.Copy)],
                    reads=[("ps", cb_[bi])], writes=[("z", bi)])
                job(DVE, [tt(Z[:, t0 - tb:t0 - tb + n], ps[vb_[bi]][:, :n], Z[:, t0 - tb:t0 - tb + n], ALU.mult)],
                    reads=[("ps", vb_[bi]), ("z", bi)], writes=[("z", bi)])
            job(DVE, [ts(Z[:, 0:2], Z[:, 0:2], hmask[:, 0:1], None, ALU.mult)], reads=[("z", 0), ("hmask",)], writes=[("z", 0)])
            zk = [("z", bi) for bi in range(3)]
            job(ACT, [act_fn(ACC, Z[:, 2:1026], AF.Copy, scale=cw[:, fc, 2:3])], reads=zk + [("cw",)], writes=[("acc",)])
            job(DVE, [stt(ACC, Z[:, 1:1025], cw[:, fc, 1:2], ACC, ALU.mult, ALU.add)], reads=zk + [("acc",), ("cw",)], writes=[("acc",)])
            job(DVE, [stt(ACC, Z[:, 0:1024], cw[:, fc, 0:1], ACC, ALU.mult, ALU.add)], reads=zk + [("acc",), ("cw",)], writes=[("acc",)])
            bb_ = next_banks(len(BL2))
            mms = []
            for kc in range(KC):
                for bi, (t0, n) in enumerate(BL2):
                    mms.append((ps[bb_[bi]][:, :n], W[:, kc, 0:128], H[:, kc, t0 - tb:t0 - tb + n], kc == 0, kc == KC - 1))
            pe_task(mms, [("w", s)] + hk, bb_)
            for bi, (t0, n) in enumerate(BL2):
                job(DVE, [tt(G[:, fc, t0 - 128:t0 - 128 + n], ps[bb_[bi]][:, :n], ACC[:, t0 - 128:t0 - 128 + n], ALU.mult)],
                    reads=[("ps", bb_[bi]), ("acc",)], writes=[("g", fc, bi)])
        for q in range(4):
            s = load_piece(co_d[q], SLOT)
            W = ring[:, s, :].rearrange("p (k c) -> p k c", k=KC)
            for dd in range(4):
                d = q * 4 + dd
                bk = next_banks(len(BL2))
                mms = []
                for kc in range(KC):
                    for bi, (t0, n) in enumerate(BL2):
                        mms.append((ps[bk[bi]][:, :n], W[:, kc, dd * 128:(dd + 1) * 128], G[:, kc, t0 - 128:t0 - 128 + n],
                                    kc == 0, kc == KC - 1))
                pe_task(mms, [("w", s)] + [("g", fc, bi) for fc in range(KC) for bi in range(len(BL2))], bk)
                for bi, (t0, n) in enumerate(BL2):
                    resid_evac(d, t0, n, bk[bi], 1.0)

    def final():
        barrier()
        for bi, (t0, n) in enumerate(BL2):
            Yb = pv((t0 - 128) * KC * 4, F32, [KC, n])
            norm_block(10, t0, n, Yb, ("y", bi))
            dma(SP, out_d[:, :, t0 - 128:t0 - 128 + n], Yb, st_sem, st_cnt, 0, reads=[("y", bi)])

    stages = [
        lambda: ffn(0, 0, 0, BL3, 0),
        lambda: gmlp(1),
        lambda: xattn(0, 2, 3, BX),
        lambda: ffn(0, 1, 4, BX, 126),
        lambda: ffn(1, 0, 5, BX, 126),
        lambda: conv(6),
        lambda: xattn(1, 7, 8, XB1),
        lambda: ffn(1, 1, 9, BY, 128),
        lambda: final(),
    ]
    for st in stages[:nstages]:
        st()
    if dbg:
        barrier()
        allx = [("x", c, ch) for c in range(KC) for ch in range(9)]
        dma(SP, out_d, xT[:, :, :], st_sem, st_cnt, 0, reads=allx)
    SP.ops.append(("wait", st_sem, st_cnt[0]))

    with nc.Block() as block:
        @block.tensor
        def _(e):
            PE.replay(e)

        @block.scalar
        def _(e):
            ACT.replay(e)

        @block.vector
        def _(e):
            DVE.replay(e)

        @block.gpsimd
        def _(e):
            POOL.replay(e)

        @block.sync
        def _(e):
            SP.replay(e)
    es.close()
    return nc


def _pieces(W, ncols):
    K, C = W.shape
    npc = C // ncols
    return np.ascontiguousarray(W.reshape(K // 128, 128, npc, ncols).transpose(2, 1, 0, 3)).reshape(npc, 128, (K // 128) * ncols)


def _prep(inputs):
    f = lambda a: np.asarray(a, dtype=np.float32)
    x = f(inputs["x"])[0]
    mem = f(inputs["mem"])[0]
    shared = {}
    shared["memT"] = np.ascontiguousarray(mem.T)
    gl = []
    for l in range(2):
        for nm in ("ffn1_norm", "mix_norm", "xattn_norm", "mem_norm", "ffn2_norm"):
            gl.append(f(inputs[nm])[l])
    gl.append(f(inputs["final_norm"]))
    g = np.stack(gl, 0)
    shared["gains"] = np.ascontiguousarray(g.reshape(11, 16, 128).transpose(2, 0, 1)).reshape(128, 11 * 16)
    for l in range(2):
        for fi, nm in enumerate(("ffn1", "ffn2")):
            w13 = f(inputs[nm + "_w13"])[l]
            gate = w13[:, :5632].reshape(16, 128, 22, 2, 1, 128)
            up = w13[:, 5632:].reshape(16, 128, 22, 2, 1, 128)
            cat = np.concatenate([gate, up], axis=4)
            shared[f"w13_{l}_{fi}"] = np.ascontiguousarray(cat.transpose(2, 1, 0, 3, 4, 5)).reshape(22, 128, SLOT)
            w2 = f(inputs[nm + "_w2"])[l]
            arr = np.zeros((16, 128, 12 * 512), np.float32)
            for gi_, (j0, ng) in enumerate(GROUPS):
                blk = w2[j0 * 128:(j0 + ng) * 128].reshape(ng, 128, 4, 512)
                arr[gi_ * 4:gi_ * 4 + 4, :, :ng * 512] = blk.transpose(2, 1, 0, 3).reshape(4, 128, ng * 512)
            shared[f"w2_{l}_{fi}"] = arr
        shared[f"wq_{l}"] = _pieces(f(inputs["xattn_wq"])[l], 512)
        wkv = f(inputs["xattn_wkv"])[l]
        shared[f"wk_{l}"] = _pieces(wkv[:, :D], 512)
        shared[f"wv_{l}"] = _pieces(wkv[:, D:], 512)
        shared[f"wo_{l}"] = _pieces(f(inputs["xattn_wo"])[l], 512)
    win = f(inputs["gmlp_w_in"])[0]
    shared["gmlp_u"] = _pieces(win[:, :D], 512)
    shared["gmlp_v"] = _pieces(win[:, D:], 512)
    shared["gmlp_o"] = _pieces(f(inputs["gmlp_w_out"])[0], 512)
    shared["ln_g"] = f(inputs["gmlp_ln_g"])[0].reshape(1, D)
    shared["ln_b"] = f(inputs["gmlp_ln_b"])[0].reshape(1, D)
    ws = f(inputs["gmlp_w_s"])[0]
    shared["wsT"] = np.ascontiguousarray(ws.transpose(2, 0, 1)).reshape(128, 8 * 128)
    shared["trilT"] = np.ascontiguousarray(np.tril(np.ones((128, 128), np.float32)).T)
    shared["b_s"] = f(inputs["gmlp_b_s"])[0].reshape(1, 8 * 128)
    cin = f(inputs["conv_w_in"])[0]
    c3 = cin.reshape(16, 128, 3, 16, 128)
    shared["conv_in"] = np.ascontiguousarray(c3.transpose(3, 1, 0, 2, 4)).reshape(16, 128, 16 * 384)
    cwv = f(inputs["conv_w"])[0]
    shared["conv_w"] = np.ascontiguousarray(cwv.reshape(3, 16, 128).transpose(2, 1, 0)).reshape(128, 48)
    shared["conv_o"] = _pieces(f(inputs["conv_w_out"])[0], 512)
    in_maps = []
    for i in range(NCORES):
        xs = np.zeros((NT, D), np.float32)
        if i > 0:
            xs[:] = x[i * TOWN - HALO:(i + 1) * TOWN]
        else:
            xs[HALO:] = x[:TOWN]
        m = dict(shared)
        m["xT"] = np.ascontiguousarray(xs.T)
        m["hmask"] = np.full((128, 2), 1.0 if i > 0 else 0.0, np.float32)
        in_maps.append(m)
    return in_maps


_NC_CACHE = {}


def kernel(**inputs):
    in_maps = _prep(inputs)
    if "nc" not in _NC_CACHE:
        _NC_CACHE["nc"] = build()
    nc = _NC_CACHE["nc"]
    res = run_bass_kernel_spmd(nc, in_maps, core_ids=list(range(NCORES)))
    outs = [np.asarray(r["outT"]).T for r in res.results]
    return np.concatenate(outs, axis=0).reshape(1, NCORES * TOWN, D).astype(np.float32)
```

```python
import contextlib
import numpy as np
import concourse.bass as bass
import concourse.mybir as mybir
from concourse.bass_utils import run_bass_kernel_spmd

F32 = mybir.dt.float32
BF16 = mybir.dt.bfloat16
AF = mybir.ActivationFunctionType
ALU = mybir.AluOpType
AX = mybir.AxisListType

D = 2048
KC = 16
HCH = 44
MEM = 256
NCORES = 8
TOWN = 1024
HALO = 128
NT = TOWN + HALO
SLOT = 8192
NSLOT = 3
GROUPS = [(0, 12), (12, 12), (24, 12), (36, 8)]
BL3 = [(0, 384), (384, 384), (768, 384)]
BX = [(126, 386), (512, 384), (896, 256)]
BY = [(128, 384), (512, 384), (896, 256)]
XB1 = [(128, 512), (640, 512)]
BL2 = BY
CB = BX
NSTAGES_ALL = 9


class Tok:
    __slots__ = ("sem", "val")

    def __init__(self, sem, val):
        self.sem, self.val = sem, val


class Eng:
    def __init__(self, name, sem):
        self.name, self.sem, self.cnt, self.ops, self.seen = name, sem, 0, [], {}

    def wait(self, toks, skip_own=False):
        for t in toks:
            if t is None:
                continue
            if skip_own and t.sem is self.sem:
                continue
            k = id(t.sem)
            if self.seen.get(k, 0) >= t.val:
                continue
            self.seen[k] = t.val
            self.ops.append(("wait", t.sem, t.val))

    def emit(self, fn, signal=False):
        if signal:
            self.cnt += 1
            self.ops.append(("sig", fn))
            return Tok(self.sem, self.cnt)
        self.ops.append(("op", fn))
        return None

    def replay(self, e):
        for op in self.ops:
            if op[0] == "wait":
                e.wait_ge(op[1], op[2])
            elif op[0] == "op":
                op[1](e)
            elif op[0] == "sig":
                op[1](e).then_inc(self.sem, 1)
            elif op[0] == "dma":
                kw = op[4]
                e.dma_start(out=op[1], in_=op[2], **kw).then_inc(op[3], 16)


class Tracker:
    def __init__(self):
        self.w, self.r = {}, {}

    def deps_read(self, keys):
        return [self.w.get(k) for k in keys]

    def deps_write(self, keys):
        out = []
        for k in keys:
            out.append(self.w.get(k))
            out.extend(self.r.get(k, {}).values())
        return out

    def did_read(self, keys, tok):
        for k in keys:
            d = self.r.setdefault(k, {})
            cur = d.get(id(tok.sem))
            if cur is None or cur.val < tok.val:
                d[id(tok.sem)] = tok

    def did_write(self, keys, tok):
        for k in keys:
            self.w[k] = tok
            self.r[k] = {}


def build(nstages=NSTAGES_ALL, dbg=False, dbgopts=None):
    dbgopts = dbgopts or {}
    nc = bass.Bass("TRN2", target_bir_lowering=False)
    es = contextlib.ExitStack()

    def din(name, shape):
        return nc.dram_tensor(name, list(shape), F32, kind="ExternalInput").ap()

    xT_d = din("xT", [D, NT]).rearrange("(c p) t -> p c t", p=128)
    hmask_d = din("hmask", [128, 2])
    memT_d = din("memT", [D, MEM]).rearrange("(c p) t -> p c t", p=128)
    gains_d = din("gains", [128, 11 * 16])
    w13_d = {(l, f): din(f"w13_{l}_{f}", [22, 128, SLOT]) for l in range(2) for f in range(2)}
    w2_d = {(l, f): din(f"w2_{l}_{f}", [16, 128, 12 * 512]) for l in range(2) for f in range(2)}
    gv_d = din("gmlp_v", [4, 128, SLOT])
    gu_d = din("gmlp_u", [4, 128, SLOT])
    go_d = din("gmlp_o", [4, 128, SLOT])
    lng_d = din("ln_g", [1, D])
    lnb_d = din("ln_b", [1, D])
    wsT_d = din("wsT", [128, 8 * 128])
    trilT_d = din("trilT", [128, 128])
    bs_d = din("b_s", [1, 8 * 128])
    cin_d = din("conv_in", [16, 128, 16 * 384])
    cw_d = din("conv_w", [128, 16 * 3])
    co_d = din("conv_o", [4, 128, SLOT])
    wq_d = [din(f"wq_{l}", [4, 128, SLOT]) for l in range(2)]
    wk_d = [din(f"wk_{l}", [4, 128, SLOT]) for l in range(2)]
    wv_d = [din(f"wv_{l}", [4, 128, SLOT]) for l in range(2)]
    wo_d = [din(f"wo_{l}", [4, 128, SLOT]) for l in range(2)]
    if dbg:
        out_d = nc.dram_tensor("dbg", [D, NT], F32, kind="ExternalOutput").ap().rearrange("(c p) t -> p c t", p=128)
    else:
        out_d = nc.dram_tensor("outT", [D, TOWN], F32, kind="ExternalOutput").ap().rearrange("(c p) t -> p c t", p=128)

    def sb(name, shape, dt):
        return es.enter_context(nc.sbuf_tensor(name, list(shape), dt))

    xT = sb("xT_sb", [128, KC, NT], F32)
    P = sb("P_sb", [128, 36864], BF16)
    ring = sb("ring_sb", [128, NSLOT, SLOT], BF16)
    sq = sb("sq_sb", [128, 4, 512], BF16)
    std = sb("std_sb", [128, 2, 512], F32)
    actT = sb("act_sb", [128, 1160], F32)
    ones = sb("ones_sb", [128, 128], BF16)
    e0 = sb("e0_sb", [128, 128], BF16)
    gains = sb("gains_sb", [128, 11, 16], F32)
    eps6 = sb("eps6_sb", [128, 1], F32)
    eps5 = sb("eps5_sb", [128, 1], F32)
    cw = sb("cw_sb", [128, 16, 3], F32)
    hmask = sb("hmask_sb", [128, 2], F32)
    lnst = sb("lnst_sb", [128, 64], F32)

    ps = [es.enter_context(nc.psum_tensor(f"ps{i}", [128, 512], F32)) for i in range(8)]

    def pv(off, dt, shape):
        n = int(np.prod(shape))
        if dt is BF16:
            assert off % 2 == 0
            ap = P[:, off // 2: off // 2 + n]
        else:
            assert off % 4 == 0
            ap = P[:, off // 2: off // 2 + 2 * n].bitcast(F32)
        assert off + n * (2 if dt is BF16 else 4) <= 36864 * 2
        if len(shape) == 2:
            ap = ap.rearrange("p (a b) -> p a b", a=shape[0])
        elif len(shape) == 3:
            ap = ap.rearrange("p (a b c) -> p a b c", a=shape[0], b=shape[1])
        return ap

    def sem(name):
        return es.enter_context(nc.semaphore(name))

    PE = Eng("pe", sem("s_pe"))
    ACT = Eng("act", sem("s_act"))
    DVE = Eng("dve", sem("s_dve"))
    POOL = Eng("pool", sem("s_pool"))
    SP = Eng("sp", sem("s_sp"))
    slot_sems = [sem(f"s_slot{i}") for i in range(NSLOT)]
    slot_cnt = [0] * NSLOT
    ld_sem = sem("s_ld")
    ld_cnt = [0]
    st_sem = sem("s_st")
    st_cnt = [0]
    TR = Tracker()

    def job(eng, fns, reads=(), writes=()):
        reads, writes = list(reads), list(writes)
        eng.wait(TR.deps_read(reads) + TR.deps_write(writes), skip_own=(eng is PE))
        for fn in fns[:-1]:
            eng.emit(fn)
        tok = eng.emit(fns[-1], signal=True)
        TR.did_read(reads, tok)
        TR.did_write(writes, tok)
        return tok

    def dma(q, out_ap, in_ap, semh, cnt, idx, reads=(), writes=(), **kw):
        reads, writes = list(reads), list(writes)
        q.wait(TR.deps_read(reads) + TR.deps_write(writes))
        cnt[idx] += 16
        q.ops.append(("dma", out_ap, in_ap, semh, kw))
        tok = Tok(semh, cnt[idx])
        TR.did_read(reads, tok)
        TR.did_write(writes, tok)
        return tok

    def barrier():
        toks = [Tok(e.sem, e.cnt) for e in (PE, ACT, DVE) if e.cnt > 0]
        for e in (PE, ACT, DVE, SP):
            e.wait(toks)

    def dump(name, ap, dt):
        barrier()
        shp = list(ap.shape)
        dd = nc.dram_tensor(name, shp, dt, kind="ExternalOutput").ap()
        SP.wait([Tok(e.sem, e.cnt) for e in (PE, ACT, DVE) if e.cnt > 0])
        st_cnt[0] += 16
        SP.ops.append(("dma", dd, ap, st_sem, {}))

    slot_next = [0]

    def load_piece(src_ap, nel):
        s = slot_next[0]
        slot_next[0] = (s + 1) % NSLOT
        nb = nel // 2048
        dma(POOL, ring[:, s, :nel].rearrange("p (a b) -> p a b", a=nb),
            src_ap.rearrange("p (a b) -> p a b", a=nb),
            slot_sems[s], slot_cnt, s, writes=[("w", s)])
        return s

    bank_next = [0]

    def next_banks(n):
        out = []
        for _ in range(n):
            out.append(bank_next[0])
            bank_next[0] = (bank_next[0] + 1) % 6
        return out

    stat_next = [0]
    sq_next = [0]
    std_next = [0]

    def xkeys(c, t0, n):
        return [("x", c, ch) for ch in range(t0 // 128, (t0 + n - 1) // 128 + 1)]

    def mm(o, l, r, st, sp):
        return lambda e: e.matmul(o, l, r, start=st, stop=sp)

    def pe_task(mms, reads, banks):
        return job(PE, [mm(*m) for m in mms], reads=reads, writes=[("ps", b) for b in banks])

    def act_fn(out, in_, func, **kw):
        return lambda e: e.activation(out=out, in_=in_, func=func, **kw)

    def tt(out, in0, in1, op):
        return lambda e: e.tensor_tensor(out=out, in0=in0, in1=in1, op=op)

    def stt(out, in0, scalar, in1, op0, op1):
        return lambda e: e.scalar_tensor_tensor(out=out, in0=in0, scalar=scalar, in1=in1, op0=op0, op1=op1)

    def ts(out, in0, s1, s2, op0, op1=None):
        if op1 is None:
            return lambda e: e.tensor_scalar(out=out, in0=in0, scalar1=s1, scalar2=None, op0=op0)
        return lambda e: e.tensor_scalar(out=out, in0=in0, scalar1=s1, scalar2=s2, op0=op0, op1=op1)

    xl_sems = [sem(f"s_xl{i}") for i in range(3)]
    xl_cnt = [0, 0, 0]
    for i, (t0, n) in enumerate(BL3):
        dma(SP, xT[:, :, t0:t0 + n], xT_d[:, :, t0:t0 + n], xl_sems[i], xl_cnt, i,
            writes=[("x", c, ch) for c in range(KC) for ch in range(3 * i, 3 * i + 3)])
    xkeep = dict(TR.w)
    dma(SP, gains[:, :, :].rearrange("p a b -> p (a b)"), gains_d, ld_sem, ld_cnt, 0, writes=[("gains",)])
    dma(SP, cw[:, :, :].rearrange("p a b -> p (a b)"), cw_d, ld_sem, ld_cnt, 0, writes=[("cw",)])
    tok_ld = dma(SP, hmask[:, :], hmask_d, ld_sem, ld_cnt, 0, writes=[("hmask",)])
    for k in list(TR.w.keys()):
        if k not in xkeep:
            TR.w[k] = tok_ld
    job(DVE, [lambda e: e.memset(ones[:, :], 1.0)], writes=[("ones",)])
    job(DVE, [lambda e: e.memset(e0[:, :], 0.0)], writes=[("e0",)])
    job(DVE, [lambda e: e.memset(e0[0:1, :], 1.0)], writes=[("e0",)])
    job(DVE, [lambda e: e.memset(eps6[:, :], 1e-6)], writes=[("eps",)])
    job(DVE, [lambda e: e.memset(eps5[:, :], 1e-5)], writes=[("eps",)])

    def norm_begin():
        sbk = 6 + stat_next[0]
        stat_next[0] ^= 1
        return sbk

    def norm_sq_act(c, t0, n):
        s = sq_next[0]
        sq_next[0] = (s + 1) % 4
        job(ACT, [act_fn(sq[:, s, :n], xT[:, c, t0:t0 + n], AF.Square)],
            reads=xkeys(c, t0, n), writes=[("sq", s)])
        return s

    def norm_sq_pe(sbk, c, s, n):
        job(PE, [mm(ps[sbk][:, :n], ones[:, :], sq[:, s, :n], c == 0, c == KC - 1)],
            reads=[("sq", s), ("ones",)], writes=[("ps", sbk)])

    def norm_sq_pair(sbk, c, t0, n):
        norm_sq_pe(sbk, c, norm_sq_act(c, t0, n), n)

    def norm_end(sbk, n):
        sl = std_next[0]
        std_next[0] ^= 1
        job(ACT, [act_fn(std[:, sl, :n], ps[sbk][:, :n], AF.Sqrt, bias=eps6[:, 0:1], scale=1.0 / D)],
            reads=[("ps", sbk), ("eps",)], writes=[("std", sl)])
        job(DVE, [lambda e: e.reciprocal(out=std[:, sl, :n], in_=std[:, sl, :n])],
            reads=[("std", sl)], writes=[("std", sl)])
        return sl

    def norm_apply(gi, t0, n, sl, hdst, hkey):
        fns = [stt(hdst[:, c, :], xT[:, c, t0:t0 + n], gains[:, gi, c:c + 1], std[:, sl, :n], ALU.mult, ALU.mult)
               for c in range(KC)]
        rk = [("std", sl), ("gains",)]
        for c in range(KC):
            rk += xkeys(c, t0, n)
        job(DVE, fns, reads=rk, writes=(hkey if isinstance(hkey, list) else [hkey]))

    def norm_block(gi, t0, n, hdst, hkey, f32out=False):
        sbk = norm_begin()
        for c in range(KC):
            norm_sq_pair(sbk, c, t0, n)
        sl = norm_end(sbk, n)
        norm_apply(gi, t0, n, sl, hdst, hkey)

    act_next = [0]

    def act_slot(n):
        if n <= 386:
            a = act_next[0] % 3
            act_next[0] = (a + 1) % 3
            return a, actT[:, a * 386: a * 386 + n]
        a = act_next[0] % 2
        act_next[0] = (a + 1) % 2
        return a, actT[:, a * 512: a * 512 + n]

    def resid_evac(d, t0, n, bank, scale):
        if scale == 1.0:
            fn = tt(xT[:, d, t0:t0 + n], ps[bank][:, :n], xT[:, d, t0:t0 + n], ALU.add)
        else:
            fn = stt(xT[:, d, t0:t0 + n], ps[bank][:, :n], scale, xT[:, d, t0:t0 + n], ALU.mult, ALU.add)
        job(DVE, [fn], reads=[("ps", bank)] + xkeys(d, t0, n), writes=xkeys(d, t0, n))

    def proj_out(w_d, blocks, tb, G, gkeys):
        for q in range(4):
            s = load_piece(w_d[q], SLOT)
            W = ring[:, s, :].rearrange("p (k c) -> p k c", k=KC)
            for dd in range(4):
                d = q * 4 + dd
                bk = next_banks(len(blocks))
                mms = []
                for kc in range(KC):
                    for bi, (t0, n) in enumerate(blocks):
                        mms.append((ps[bk[bi]][:, :n], W[:, kc, dd * 128:(dd + 1) * 128],
                                    G[:, kc, t0 - tb:t0 - tb + n], kc == 0, kc == KC - 1))
                pe_task(mms, [("w", s)] + list(gkeys), bk)
                for bi, (t0, n) in enumerate(blocks):
                    resid_evac(d, t0, n, bk[bi], 1.0)

    def ffn(l, f, gi, blocks, tb):
        ntl = sum(n for _, n in blocks)
        barrier()
        H = pv(0, BF16, [KC, ntl])
        HID = pv(KC * ntl * 2, BF16, [12, ntl])
        nb = len(blocks)
        for bi, (t0, n) in enumerate(blocks):
            norm_block(gi, t0, n, H[:, :, t0 - tb:t0 - tb + n], ("h", bi))
        hk = [("h", bi) for bi in range(nb)]
        for g, (j0, ng) in enumerate(GROUPS):
            for pk in range(j0 // 2, (j0 + ng) // 2):
                s = load_piece(w13_d[(l, f)][pk], SLOT)
                W = ring[:, s, :].rearrange("p (k c) -> p k c", k=KC)
                for jj in range(2):
                    jl = 2 * pk + jj - j0
                    gb = next_banks(nb)
                    ub = next_banks(nb)
                    for banks_, col0 in ((gb, jj * 256), (ub, jj * 256 + 128)):
                        for bi, (t0, n) in enumerate(blocks):
                            mms = [(ps[banks_[bi]][:, :n], W[:, kc, col0:col0 + 128],
                                    H[:, kc, t0 - tb:t0 - tb + n], kc == 0, kc == KC - 1) for kc in range(KC)]
                            pe_task(mms, [("w", s), ("h", bi)], [banks_[bi]])
                    for bi, (t0, n) in enumerate(blocks):
                        a, aT = act_slot(n)
                        job(ACT, [act_fn(aT, ps[gb[bi]][:, :n], AF.Silu)], reads=[("ps", gb[bi])], writes=[("act", a)])
                        job(DVE, [tt(HID[:, jl, t0 - tb:t0 - tb + n], ps[ub[bi]][:, :n], aT, ALU.mult)],
                            reads=[("ps", ub[bi]), ("act", a)], writes=[("hid", jl, bi)])
            for q in range(4):
                s = load_piece(w2_d[(l, f)][g * 4 + q][:, :ng * 512], ng * 512)
                W2 = ring[:, s, :ng * 512].rearrange("p (k c) -> p k c", k=ng)
                for dd in range(4):
                    d = q * 4 + dd
                    bk = next_banks(nb)
                    for bi, (t0, n) in enumerate(blocks):
                        mms = [(ps[bk[bi]][:, :n], W2[:, hc, dd * 128:(dd + 1) * 128],
                                HID[:, hc, t0 - tb:t0 - tb + n], hc == 0, hc == ng - 1) for hc in range(ng)]
                        pe_task(mms, [("w", s)] + [("hid", hc, bi) for hc in range(ng)], [bk[bi]])
                        resid_evac(d, t0, n, bk[bi], 0.5)

    def gmlp(gi):
        barrier()
        Hb = pv(0, BF16, [KC, 384])
        Vtm = pv(12288, BF16, [3, D])
        Gb = pv(24576, BF16, [KC, 384])
        LNG = pv(36864, F32, [D])
        LNB = pv(45056, F32, [D])
        WsT = pv(53248, BF16, [8, 128])
        BSH = pv(55296, BF16, [8, 384])
        BSL = pv(61440, BF16, [8, 384])
        LNS = pv(67584, F32, [2, 512])
        WsF = actT[:, 0:1024].rearrange("p (a b) -> p a b", a=8)
        TrF = actT[:, 1024:1152]
        BsF = pv(67584, F32, [8, 128])
        ak = [("act", a) for a in range(3)]
        lk = [("lns", 0), ("lns", 1)]
        t1 = dma(SP, LNG, lng_d.partition_broadcast(128), ld_sem, ld_cnt, 0, writes=[("lng",)])
        t2 = dma(SP, LNB, lnb_d.partition_broadcast(128), ld_sem, ld_cnt, 0, writes=[("lnb",)])
        t3 = dma(SP, WsF, wsT_d.rearrange("p (a b) -> p a b", a=8), ld_sem, ld_cnt, 0, writes=[("wsf",)] + ak)
        t4 = dma(SP, TrF, trilT_d, ld_sem, ld_cnt, 0, writes=[("trf",)] + ak)
        t5 = dma(SP, BsF[0:1].rearrange("p a b -> p (a b)"), bs_d, ld_sem, ld_cnt, 0, writes=[("bsf",)] + lk)
        for k in [("lng",), ("lnb",), ("wsf",), ("trf",), ("bsf",)] + ak + lk:
            TR.w[k] = t5
        norm_block(gi, BL3[0][0], BL3[0][1], Hb, ("h", 0))
        job(DVE, [tt(WsT[:, g, :], WsF[:, g, :], TrF, ALU.mult) for g in range(8)],
            reads=[("wsf",), ("trf",)] + ak, writes=[("wst",)])
        job(DVE, [lambda e: e.memset(BSH.rearrange("p a b -> p (a b)"), 0.0),
                  lambda e: e.memset(BSL.rearrange("p a b -> p (a b)"), 0.0)], writes=[("bsh",), ("bsl",)])
        job(DVE, [(lambda e, g=g, r=r: e.tensor_copy(out=BSH[0:1, g, r * 128:(r + 1) * 128], in_=BsF[0:1, g, :]))
                  for g in range(8) for r in range(3)], reads=[("bsf",)] + lk, writes=[("bsh",)])
        job(DVE, [tt(BSL[0:1, g, r * 128:(r + 1) * 128], BsF[0:1, g, :], BSH[0:1, g, r * 128:(r + 1) * 128], ALU.subtract)
                  for g in range(8) for r in range(3)], reads=[("bsf",), ("bsh",)] + lk, writes=[("bsl",)])
        UB, FB = [0, 1, 2, 3], [4, 5]
        for b, (t0, n) in enumerate(BL3):
            nxt = BL3[b + 1] if b + 1 < len(BL3) else None
            sbk_n = norm_begin() if nxt else None
            for fb in range(4):
                s = load_piece(gv_d[fb], SLOT)
                W = ring[:, s, :].rearrange("p (k c) -> p k c", k=KC)
                for ch in range(3):
                    bk = next_banks(1)
                    mms = [(ps[bk[0]][:, :], Hb[:, kc, ch * 128:(ch + 1) * 128], W[:, kc, :], kc == 0, kc == KC - 1)
                           for kc in range(KC)]
                    pe_task(mms, [("w", s), ("h", 0)], bk)
                    col = ch * 4 + fb
                    job(ACT, [act_fn(Vtm[:, ch, fb * 512:(fb + 1) * 512], ps[bk[0]][:, :], AF.Gelu,
                                     accum_out=lnst[:, col:col + 1])],
                        reads=[("ps", bk[0])], writes=[("v", ch, fb), ("s1", col)])
                    s_ = sq_next[0]
                    sq_next[0] = (s_ + 1) % 4
                    job(ACT, [act_fn(sq[:, s_, :], Vtm[:, ch, fb * 512:(fb + 1) * 512], AF.Square,
                                     accum_out=lnst[:, 12 + col:13 + col])],
                        reads=[("v", ch, fb)], writes=[("sq", s_), ("s2", col)])
            S1 = lnst[:, 0:12].rearrange("p (a b) -> p a b", a=3)
            S2 = lnst[:, 12:24].rearrange("p (a b) -> p a b", a=3)
            m1, m2, msq, var, sd, rstd, nmr = (lnst[:, 24 + 3 * i:27 + 3 * i] for i in range(7))
            allv = [("v", ch, fb) for ch in range(3) for fb in range(4)]
            job(DVE, [lambda e: e.tensor_reduce(out=m1, in_=S1, axis=AX.X, op=ALU.add),
                      lambda e: e.tensor_reduce(out=m2, in_=S2, axis=AX.X, op=ALU.add)],
                reads=[("s1", c_) for c_ in range(12)] + [("s2", c_) for c_ in range(12)], writes=[("m12",)])
            job(DVE, [ts(m1, m1, 1.0 / D, None, ALU.mult), ts(m2, m2, 1.0 / D, None, ALU.mult)],
                reads=[("m12",)], writes=[("m12",)])
            job(DVE, [tt(msq, m1, m1, ALU.mult)], reads=[("m12",)], writes=[("msq",)])
            job(DVE, [tt(var, m2, msq, ALU.subtract)], reads=[("m12",), ("msq",)], writes=[("var",)])
            job(ACT, [act_fn(sd, var, AF.Sqrt, bias=eps5[:, 0:1], scale=1.0)], reads=[("var",), ("eps",)], writes=[("sd",)])
            job(DVE, [lambda e: e.reciprocal(out=rstd, in_=sd)], reads=[("sd",)], writes=[("rstd",)])
            job(DVE, [stt(nmr, m1, -1.0, rstd, ALU.mult, ALU.mult)], reads=[("m12",), ("rstd",)], writes=[("nmr",)])
            for ch in range(3):
                for fb in range(4):
                    sl = (ch * 4 + fb) % 2
                    vs = Vtm[:, ch, fb * 512:(fb + 1) * 512]
                    job(ACT, [act_fn(LNS[:, sl, :], vs, AF.Identity, scale=rstd[:, ch:ch + 1], bias=nmr[:, ch:ch + 1])],
                        reads=[("v", ch, fb), ("rstd",), ("nmr",)], writes=[("lns", sl)])
                    job(DVE, [tt(LNS[:, sl, :], LNS[:, sl, :], LNG[:, fb * 512:(fb + 1) * 512], ALU.mult)],
                        reads=[("lns", sl), ("lng",)], writes=[("lns", sl)])
                    job(DVE, [tt(vs, LNS[:, sl, :], LNB[:, fb * 512:(fb + 1) * 512], ALU.add)],
                        reads=[("lns", sl), ("lnb",)], writes=[("v", ch, fb)])
            if dbgopts.get("gmlp") == "ln":
                dump("d_vtm", Vtm, BF16)
                dump("d_lnst", lnst[:, :], F32)
                dump("d_hb", Hb, BF16)
                return
            LA = 4
            wslot = {}

            def emit_u(fc):
                pk, ff = fc // 4, fc % 4
                if ff == 0:
                    wslot[pk] = load_piece(gu_d[pk], SLOT)
                s_ = wslot[pk]
                W_ = ring[:, s_, :].rearrange("p (k c) -> p k c", k=KC)
                ubk = UB[fc % 4]
                mms_ = [(ps[ubk][:, :n], W_[:, kc, ff * 128:(ff + 1) * 128], Hb[:, kc, :], kc == 0, kc == KC - 1)
                        for kc in range(KC)]
                pe_task(mms_, [("w", s_), ("h", 0)], [ubk])

            for fc in range(LA):
                emit_u(fc)
            for fc in range(KC):
                g = fc // 2
                ubk = UB[fc % 4]
                fbk = FB[fc % 2]
                mms = [(ps[fbk][:, :n], e0[:, :], BSH[:, g, :], True, False),
                       (ps[fbk][:, :n], e0[:, :], BSL[:, g, :], False, False)]
                for ch in range(3):
                    mms.append((ps[fbk][:, ch * 128:(ch + 1) * 128], Vtm[:, ch, fc * 128:(fc + 1) * 128],
                                WsT[:, g, :], False, ch == 2))
                pe_task(mms, [("e0",), ("bsh",), ("bsl",), ("wst",)] + [("v", ch, fc // 4) for ch in range(3)], [fbk])
                a, aT = act_slot(n)
                job(ACT, [act_fn(aT, ps[ubk][:, :n], AF.Gelu)], reads=[("ps", ubk)], writes=[("act", a)])
                job(DVE, [tt(Gb[:, fc, :], ps[fbk][:, :n], aT, ALU.mult)],
                    reads=[("ps", fbk), ("act", a)], writes=[("g", fc)])
                if fc + LA < KC:
                    emit_u(fc + LA)
                if nxt:
                    norm_sq_pair(sbk_n, fc, nxt[0], nxt[1])
            if nxt:
                sl_n = norm_end(sbk_n, nxt[1])
                norm_apply(gi, nxt[0], nxt[1], sl_n, Hb, ("h", 0))
            if dbgopts.get("gmlp") == "gate":
                dump("d_vtm", Vtm, BF16)
                dump("d_gb", Gb, BF16)
                dump("d_bsh", BSH, BF16)
                dump("d_bsl", BSL, BF16)
                dump("d_wst", WsT, BF16)
                return
            for q in range(4):
                s = load_piece(go_d[q], SLOT)
                W = ring[:, s, :].rearrange("p (k c) -> p k c", k=KC)
                for dd in range(4):
                    d = q * 4 + dd
                    bk = next_banks(1)
                    mms = [(ps[bk[0]][:, :n], W[:, kc, dd * 128:(dd + 1) * 128], Gb[:, kc, :], kc == 0, kc == KC - 1)
                           for kc in range(KC)]
                    pe_task(mms, [("w", s)] + [("g", fc) for fc in range(KC)], bk)
                    resid_evac(d, t0, n, bk[0], 1.0)

    def xattn(l, gi, gmi, sblocks):
        barrier()
        WB = 514
        Hb = pv(0, BF16, [KC, WB])
        Qb = pv(16448, BF16, [KC, WB])
        Ob = pv(32896, BF16, [KC, WB])
        KT = pv(49344, BF16, [KC, MEM])
        VT = pv(57536, BF16, [2, D])
        PT = pv(65728, BF16, [2, 2, WB])
        RS = pv(69840, F32, [WB])
        MT = pv(0, F32, [KC, MEM])
        MN = pv(16448, BF16, [KC, MEM])
        tm = dma(SP, MT, memT_d, ld_sem, ld_cnt, 0, writes=[("mt",)])
        sbk = 6 + stat_next[0]
        stat_next[0] ^= 1
        for c in range(KC):
            s = sq_next[0]
            sq_next[0] = (s + 1) % 4
            job(ACT, [act_fn(sq[:, s, :MEM], MT[:, c, :], AF.Square)], reads=[("mt",)], writes=[("sq", s)])
            job(PE, [mm(ps[sbk][:, :MEM], ones[:, :], sq[:, s, :MEM], c == 0, c == KC - 1)],
                reads=[("sq", s), ("ones",)], writes=[("ps", sbk)])
        sl = std_next[0]
        std_next[0] ^= 1
        job(ACT, [act_fn(std[:, sl, :MEM], ps[sbk][:, :MEM], AF.Sqrt, bias=eps6[:, 0:1], scale=1.0 / D)],
            reads=[("ps", sbk), ("eps",)], writes=[("std", sl)])
        job(DVE, [lambda e: e.reciprocal(out=std[:, sl, :MEM], in_=std[:, sl, :MEM])], reads=[("std", sl)], writes=[("std", sl)])
        job(DVE, [stt(MN[:, c, :], MT[:, c, :], gains[:, gmi, c:c + 1], std[:, sl, :MEM], ALU.mult, ALU.mult)
                  for c in range(KC)], reads=[("std", sl), ("mt",), ("gains",)], writes=[("mn",)])
        for pk in range(4):
            s = load_piece(wk_d[l][pk], SLOT)
            W = ring[:, s, :].rearrange("p (k c) -> p k c", k=KC)
            for ff in range(4):
                fc = pk * 4 + ff
                bk = next_banks(1)
                mms = [(ps[bk[0]][:, :MEM], W[:, kc, ff * 128:(ff + 1) * 128], MN[:, kc, :], kc == 0, kc == KC - 1)
                       for kc in range(KC)]
                pe_task(mms, [("w", s), ("mn",)], bk)
                job(ACT, [act_fn(KT[:, fc, :], ps[bk[0]][:, :MEM], AF.Copy)], reads=[("ps", bk[0])], writes=[("kt", fc)])
        for fb in range(4):
            s = load_piece(wv_d[l][fb], SLOT)
            W = ring[:, s, :].rearrange("p (k c) -> p k c", k=KC)
            for mc in range(2):
                bk = next_banks(1)
                mms = [(ps[bk[0]][:, :], MN[:, kc, mc * 128:(mc + 1) * 128], W[:, kc, :], kc == 0, kc == KC - 1)
                       for kc in range(KC)]
                pe_task(mms, [("w", s), ("mn",)], bk)
                job(DVE, [lambda e, o=VT[:, mc, fb * 512:(fb + 1) * 512], i=ps[bk[0]][:, :]: e.tensor_copy(out=o, in_=i)],
                    reads=[("ps", bk[0])], writes=[("vt", mc, fb)])
        barrier()
        qscale = 512.0 ** -0.5
        for b, subs_ in enumerate(sblocks):
            subs, off = [], 0
            for (t0, n) in subs_:
                subs.append((t0, n, off))
                off += n
            nsub = len(subs)
            if b == 0:
                for si, (t0, n, o) in enumerate(subs):
                    norm_block(gi, t0, n, Hb[:, :, o:o + n], ("h", si))
            nxt = None
            if b + 1 < len(sblocks):
                assert len(sblocks[b + 1]) == 1
                nxt = sblocks[b + 1][0]
            sbk_n = norm_begin() if nxt else None
            for pk in range(4):
                s = load_piece(wq_d[l][pk], SLOT)
                W = ring[:, s, :].rearrange("p (k c) -> p k c", k=KC)
                for ff in range(4):
                    fc = pk * 4 + ff
                    for si, (t0, n, o) in enumerate(subs):
                        bk = next_banks(1)
                        mms = [(ps[bk[0]][:, :n], W[:, kc, ff * 128:(ff + 1) * 128], Hb[:, kc, o:o + n], kc == 0, kc == KC - 1)
                               for kc in range(KC)]
                        pe_task(mms, [("w", s), ("h", si)], bk)
                        job(ACT, [act_fn(Qb[:, fc, o:o + n], ps[bk[0]][:, :n], AF.Copy, scale=qscale)],
                            reads=[("ps", bk[0])], writes=[("q", fc, si)])
                    if nxt:
                        if fc >= 1:
                            norm_sq_pe(sbk_n, fc - 1, sq_prev, nxt[1])
                        sq_prev = norm_sq_act(fc, nxt[0], nxt[1])
            if nxt:
                norm_sq_pe(sbk_n, KC - 1, sq_prev, nxt[1])
                sl_n = norm_end(sbk_n, nxt[1])
                norm_apply(gi, nxt[0], nxt[1], sl_n, Hb[:, :, :nxt[1]], [("h", si) for si in range(max(nsub, 1))])

            def s_tasks(hd):
                sp_ = hd % 2
                for mc in range(2):
                    for si, (t0, n, o) in enumerate(subs):
                        bk = next_banks(1)
                        mms = [(ps[bk[0]][:, :n], KT[:, hd * 4 + dc, mc * 128:(mc + 1) * 128], Qb[:, hd * 4 + dc, o:o + n],
                                dc == 0, dc == 3) for dc in range(4)]
                        pe_task(mms, [("kt", hd * 4 + dc) for dc in range(4)] + [("q", hd * 4 + dc, si) for dc in range(4)], bk)
                        job(ACT, [act_fn(PT[:, sp_, mc, o:o + n], ps[bk[0]][:, :n], AF.Exp)],
                            reads=[("ps", bk[0])], writes=[("pt", sp_, mc, si)])

            s_tasks(0)
            for hd in range(4):
                sp_ = hd % 2
                if hd < 3:
                    s_tasks(hd + 1)
                for si, (t0, n, o) in enumerate(subs):
                    bk = next_banks(1)
                    mms = [(ps[bk[0]][:, :n], ones[:, :], PT[:, sp_, mc, o:o + n], mc == 0, mc == 1) for mc in range(2)]
                    pe_task(mms, [("ones",), ("pt", sp_, 0, si), ("pt", sp_, 1, si)], bk)
                    job(DVE, [lambda e, o_=RS[:, o:o + n], i=ps[bk[0]][:, :n]: e.reciprocal(out=o_, in_=i)],
                        reads=[("ps", bk[0])], writes=[("rs", si)])
                for dc in range(4):
                    fc = hd * 4 + dc
                    for si, (t0, n, o) in enumerate(subs):
                        bk = next_banks(1)
                        mms = [(ps[bk[0]][:, :n], VT[:, mc, fc * 128:(fc + 1) * 128], PT[:, sp_, mc, o:o + n], mc == 0, mc == 1)
                               for mc in range(2)]
                        pe_task(mms, [("vt", mc, fc // 4) for mc in range(2)] + [("pt", sp_, 0, si), ("pt", sp_, 1, si)], bk)
                        job(DVE, [tt(Ob[:, fc, o:o + n], ps[bk[0]][:, :n], RS[:, o:o + n], ALU.mult)],
                            reads=[("ps", bk[0]), ("rs", si)], writes=[("o", fc, si)])
            proj_out(wo_d[l], [(t0, n) for (t0, n, o) in subs], subs[0][0], Ob,
                     [("o", fc, si) for fc in range(KC) for si in range(nsub)])
        return

    def conv(gi):
        barrier()
        tb = 126
        H = pv(0, BF16, [KC, 1028])
        G = pv(32896, BF16, [KC, 1024])
        ACC = pv(65664, F32, [1024])
        Z = actT
        for bi, (t0, n) in enumerate(CB):
            norm_block(gi, t0, n, H[:, :, t0 - tb:t0 - tb + n], ("h", bi))
        hk = [("h", bi) for bi in range(3)]
        for fc in range(KC):
            s = load_piece(cin_d[fc], KC * 384)
            W = ring[:, s, :KC * 384].rearrange("p (k c) -> p k c", k=KC)
            cb_ = next_banks(3)
            vb_ = next_banks(3)
            for banks_, col0 in ((cb_, 128), (vb_, 256)):
                mms = []
                for kc in range(KC):
                    for bi, (t0, n) in enumerate(CB):
                        mms.append((ps[banks_[bi]][:, :n], W[:, kc, col0:col0 + 128], H[:, kc, t0 - tb:t0 - tb + n],
                                    kc == 0, kc == KC - 1))
                pe_task(mms, [("w", s)] + hk, banks_)
            for bi, (t0, n) in enumerate(CB):
                job(ACT, [act_fn(Z[:, t0 - tb:t0 - tb + n], ps[cb_[bi]][:, :n], AF.Copy)],
                    reads=[("ps", cb_[bi])], writes=[("z", bi)])
                job(DVE, [tt(Z[:, t0 - tb:t0 - tb + n], ps[vb_[bi]][:, :n], Z[:, t0 - tb:t0 - tb + n], ALU.mult)],
                    reads=[("ps", vb_[bi]), ("z", bi)], writes=[("z", bi)])
            job(DVE, [ts(Z[:, 0:2], Z[:, 0:2], hmask[:, 0:1], None, ALU.mult)], reads=[("z", 0), ("hmask",)], writes=[("z", 0)])
            zk = [("z", bi) for bi in range(3)]
            job(ACT, [act_fn(ACC, Z[:, 2:1026], AF.Copy, scale=cw[:, fc, 2:3])], reads=zk + [("cw",)], writes=[("acc",)])
            job(DVE, [stt(ACC, Z[:, 1:1025], cw[:, fc, 1:2], ACC, ALU.mult, ALU.add)], reads=zk + [("acc",), ("cw",)], writes=[("acc",)])
            job(DVE, [stt(ACC, Z[:, 0:1024], cw[:, fc, 0:1], ACC, ALU.mult, ALU.add)], reads=zk + [("acc",), ("cw",)], writes=[("acc",)])
            bb_ = next_banks(len(BL2))
            mms = []
            for kc in range(KC):
                for bi, (t0, n) in enumerate(BL2):
                    mms.append((ps[bb_[bi]][:, :n], W[:, kc, 0:128], H[:, kc, t0 - tb:t0 - tb + n], kc == 0, kc == KC - 1))
            pe_task(mms, [("w", s)] + hk, bb_)
            for bi, (t0, n) in enumerate(BL2):
                job(DVE, [tt(G[:, fc, t0 - 128:t0 - 128 + n], ps[bb_[bi]][:, :n], ACC[:, t0 - 128:t0 - 128 + n], ALU.mult)],
                    reads=[("ps", bb_[bi]), ("acc",)], writes=[("g", fc, bi)])
        for q in range(4):
            s = load_piece(co_d[q], SLOT)
            W = ring[:, s, :].rearrange("p (k c) -> p k c", k=KC)
            for dd in range(4):
                d = q * 4 + dd
                bk = next_banks(len(BL2))
                mms = []
                for kc in range(KC):
                    for bi, (t0, n) in enumerate(BL2):
                        mms.append((ps[bk[bi]][:, :n], W[:, kc, dd * 128:(dd + 1) * 128], G[:, kc, t0 - 128:t0 - 128 + n],
                                    kc == 0, kc == KC - 1))
                pe_task(mms, [("w", s)] + [("g", fc, bi) for fc in range(KC) for bi in range(len(BL2))], bk)
                for bi, (t0, n) in enumerate(BL2):
                    resid_evac(d, t0, n, bk[bi], 1.0)

    def final():
        barrier()
        for bi, (t0, n) in enumerate(BL2):
            Yb = pv((t0 - 128) * KC * 4, F32, [KC, n])
            norm_block(10, t0, n, Yb, ("y", bi))
            dma(SP, out_d[:, :, t0 - 128:t0 - 128 + n], Yb, st_sem, st_cnt, 0, reads=[("y", bi)])

    stages = [
        lambda: ffn(0, 0, 0, BL3, 0),
        lambda: gmlp(1),
        lambda: xattn(0, 2, 3, [[(126, 2), (128, 512)], [(640, 512)]]),
        lambda: ffn(0, 1, 4, BX, 126),
        lambda: ffn(1, 0, 5, BX, 126),
        lambda: conv(6),
        lambda: xattn(1, 7, 8, [[(128, 512)], [(640, 512)]]),
        lambda: ffn(1, 1, 9, BY, 128),
        lambda: final(),
    ]
    for st in stages[:nstages]:
        st()
    if dbg:
        barrier()
        allx = [("x", c, ch) for c in range(KC) for ch in range(9)]
        dma(SP, out_d, xT[:, :, :], st_sem, st_cnt, 0, reads=allx)
    SP.ops.append(("wait", st_sem, st_cnt[0]))

    with nc.Block() as block:
        @block.tensor
        def _(e):
            PE.replay(e)

        @block.scalar
        def _(e):
            ACT.replay(e)

        @block.vector
        def _(e):
            DVE.replay(e)

        @block.gpsimd
        def _(e):
            POOL.replay(e)

        @block.sync
        def _(e):
            SP.replay(e)
    es.close()
    return nc


def _pieces(W, ncols):
    K, C = W.shape
    npc = C // ncols
    return np.ascontiguousarray(W.reshape(K // 128, 128, npc, ncols).transpose(2, 1, 0, 3)).reshape(npc, 128, (K // 128) * ncols)


def _prep(inputs):
    f = lambda a: np.asarray(a, dtype=np.float32)
    x = f(inputs["x"])[0]
    mem = f(inputs["mem"])[0]
    shared = {}
    shared["memT"] = np.ascontiguousarray(mem.T)
    gl = []
    for l in range(2):
        for nm in ("ffn1_norm", "mix_norm", "xattn_norm", "mem_norm", "ffn2_norm"):
            gl.append(f(inputs[nm])[l])
    gl.append(f(inputs["final_norm"]))
    g = np.stack(gl, 0)
    shared["gains"] = np.ascontiguousarray(g.reshape(11, 16, 128).transpose(2, 0, 1)).reshape(128, 11 * 16)
    for l in range(2):
        for fi, nm in enumerate(("ffn1", "ffn2")):
            w13 = f(inputs[nm + "_w13"])[l]
            gate = w13[:, :5632].reshape(16, 128, 22, 2, 1, 128)
            up = w13[:, 5632:].reshape(16, 128, 22, 2, 1, 128)
            cat = np.concatenate([gate, up], axis=4)
            shared[f"w13_{l}_{fi}"] = np.ascontiguousarray(cat.transpose(2, 1, 0, 3, 4, 5)).reshape(22, 128, SLOT)
            w2 = f(inputs[nm + "_w2"])[l]
            arr = np.zeros((16, 128, 12 * 512), np.float32)
            for gi_, (j0, ng) in enumerate(GROUPS):
                blk = w2[j0 * 128:(j0 + ng) * 128].reshape(ng, 128, 4, 512)
                arr[gi_ * 4:gi_ * 4 + 4, :, :ng * 512] = blk.transpose(2, 1, 0, 3).reshape(4, 128, ng * 512)
            shared[f"w2_{l}_{fi}"] = arr
        shared[f"wq_{l}"] = _pieces(f(inputs["xattn_wq"])[l], 512)
        wkv = f(inputs["xattn_wkv"])[l]
        shared[f"wk_{l}"] = _pieces(wkv[:, :D], 512)
        shared[f"wv_{l}"] = _pieces(wkv[:, D:], 512)
        shared[f"wo_{l}"] = _pieces(f(inputs["xattn_wo"])[l], 512)
    win = f(inputs["gmlp_w_in"])[0]
    shared["gmlp_u"] = _pieces(win[:, :D], 512)
    shared["gmlp_v"] = _pieces(win[:, D:], 512)
    shared["gmlp_o"] = _pieces(f(inputs["gmlp_w_out"])[0], 512)
    shared["ln_g"] = f(inputs["gmlp_ln_g"])[0].reshape(1, D)
    shared["ln_b"] = f(inputs["gmlp_ln_b"])[0].reshape(1, D)
    ws = f(inputs["gmlp_w_s"])[0]
    shared["wsT"] = np.ascontiguousarray(ws.transpose(2, 0, 1)).reshape(128, 8 * 128)
    shared["trilT"] = np.ascontiguousarray(np.tril(np.ones((128, 128), np.float32)).T)
    shared["b_s"] = f(inputs["gmlp_b_s"])[0].reshape(1, 8 * 128)
    cin = f(inputs["conv_w_in"])[0]
    c3 = cin.reshape(16, 128, 3, 16, 128)
    shared["conv_in"] = np.ascontiguousarray(c3.transpose(3, 1, 0, 2, 4)).reshape(16, 128, 16 * 384)
    cwv = f(inputs["conv_w"])[0]
    shared["conv_w"] = np.ascontiguousarray(cwv.reshape(3, 16, 128).transpose(2, 1, 0)).reshape(128, 48)
    shared["conv_o"] = _pieces(f(inputs["conv_w_out"])[0], 512)
    in_maps = []
    for i in range(NCORES):
        xs = np.zeros((NT, D), np.float32)
        if i > 0:
            xs[:] = x[i * TOWN - HALO:(i + 1) * TOWN]
        else:
            xs[HALO:] = x[:TOWN]
        m = dict(shared)
        m["xT"] = np.ascontiguousarray(xs.T)
        m["hmask"] = np.full((128, 2), 1.0 if i > 0 else 0.0, np.float32)
        in_maps.append(m)
    return in_maps


_NC_CACHE = {}


def kernel(**inputs):
    in_maps = _prep(inputs)
    if "nc" not in _NC_CACHE:
        _NC_CACHE["nc"] = build()
    nc = _NC_CACHE["nc"]
    res = run_bass_kernel_spmd(nc, in_maps, core_ids=list(range(NCORES)))
    outs = [np.asarray(r["outT"]).T for r in res.results]
    return np.concatenate(outs, axis=0).reshape(1, NCORES * TOWN, D).astype(np.float32)
```
